# Optimizing a Trainium2 kernel written in Bass

```python
import math
import jax, jax.numpy as jnp
from jax import lax
import numpy as np

D_MODEL = 2048
BATCH = 2
SEQ = 8192
DEPTH = 1

HG_HEADS = 8
HG_DK = 128
HG_DV = 128
HG_KEY = HG_HEADS * HG_DK
HG_VAL = HG_HEADS * HG_DV
HG_CHUNK = 64
SSD_HEADS = 16
SSD_HEADDIM = 64
SSD_WIDTH = SSD_HEADS * SSD_HEADDIM
SSD_GROUPS = 2
SSD_HPG = SSD_HEADS // SSD_GROUPS
SSD_STATE = 128
SSD_CONV = 4
SSD_CHUNK = 128
SSD_CONV_DIM = SSD_WIDTH + 2 * SSD_GROUPS * SSD_STATE
D_MIX = HG_VAL + SSD_WIDTH
D_FF = ((8 * D_MODEL + 3 * 256 - 1) // (3 * 256)) * 256
NORM_EPS = 1e-6
IN_SPLITS = (HG_KEY, HG_KEY, HG_VAL, HG_VAL, SSD_WIDTH, SSD_CONV_DIM, SSD_HEADS)
N_IN = HG_KEY * 2 + HG_VAL * 2 + SSD_WIDTH + SSD_CONV_DIM + SSD_HEADS

kernel_name = "hgrn2_mamba2_parallel_hybrid_block"


def _split_points():
    pts, acc = [], 0
    for s in IN_SPLITS[:-1]:
        acc += s
        pts.append(acc)
    return pts


def rms_norm(x, w):
    xf = x.astype(jnp.float32)
    y = xf * lax.rsqrt(jnp.mean(xf * xf, axis=-1, keepdims=True) + NORM_EPS)
    return (y * w.astype(jnp.float32)).astype(x.dtype)


def group_rms_norm(x, w, n_groups):
    xf = x.astype(jnp.float32)
    shp = xf.shape
    xg = xf.reshape(shp[:-1] + (n_groups, shp[-1] // n_groups))
    xg = xg * lax.rsqrt(jnp.mean(xg * xg, axis=-1, keepdims=True) + NORM_EPS)
    return xg.reshape(shp) * w.astype(jnp.float32)


def hgrn2_mixer(q_raw, f_raw, i_in, g_in, lb, norm_w):
    bsz, seqlen, _ = q_raw.shape
    nc = seqlen // HG_CHUNK
    q = jax.nn.silu(q_raw.astype(jnp.float32)) * (HG_DK ** -0.5)
    forget = lb + (1.0 - lb) * jax.nn.sigmoid(f_raw.astype(jnp.float32))
    log_f = jnp.log(forget)
    k = 1.0 - forget
    v = i_in.astype(jnp.float32)

    def to_chunks(t, d):
        return t.reshape(bsz, nc, HG_CHUNK, HG_HEADS, d).transpose(1, 0, 3, 2, 4)

    causal = jnp.tril(jnp.ones((HG_CHUNK, HG_CHUNK), dtype=bool))

    def step(S, inp):
        qc, kc, vc, gc = inp
        b = jnp.cumsum(gc, axis=2)
        rel = jnp.where(causal[:, :, None], b[:, :, :, None, :] - b[:, :, None, :, :], -jnp.inf)
        scores = jnp.sum(qc[:, :, :, None, :] * kc[:, :, None, :, :] * jnp.exp(rel), axis=-1)
        o = (jnp.einsum('bhts,bhsv->bhtv', scores, vc)
             + jnp.einsum('bhtk,bhkv->bhtv', qc * jnp.exp(b), S))
        b_last = b[:, :, -1, :]
        S_new = (S * jnp.exp(b_last)[..., None]
                 + jnp.einsum('bhsk,bhsv->bhkv', kc * jnp.exp(b_last[:, :, None, :] - b), vc))
        return S_new, o

    S0 = jnp.zeros((bsz, HG_HEADS, HG_DK, HG_DV), jnp.float32)
    _, o = lax.scan(step, S0, (to_chunks(q, HG_DK), to_chunks(k, HG_DK),
                               to_chunks(v, HG_DV), to_chunks(log_f, HG_DK)))
    o = o.transpose(1, 0, 3, 2, 4).reshape(bsz, seqlen, HG_VAL)
    o = group_rms_norm(o, norm_w, HG_HEADS) * jax.nn.silu(g_in.astype(jnp.float32))
    return o.astype(q_raw.dtype)


def causal_depthwise_conv(u, w, b):
    ch = u.shape[-1]
    out = lax.conv_general_dilated(u, w[:, None, :].astype(u.dtype), window_strides=(1,),
                                   padding=[(SSD_CONV - 1, 0)],
                                   dimension_numbers=('NWC', 'WIO', 'NWC'),
                                   feature_group_count=ch)
    return out + b.astype(u.dtype)


def ssd_chunked(x, a, Bm, Cm):
    bsz, seqlen = x.shape[:2]
    nc = seqlen // SSD_CHUNK
    x = x.reshape(bsz, nc, SSD_CHUNK, SSD_GROUPS, SSD_HPG, SSD_HEADDIM)
    a = a.reshape(bsz, nc, SSD_CHUNK, SSD_GROUPS, SSD_HPG)
    Bm = Bm.reshape(bsz, nc, SSD_CHUNK, SSD_GROUPS, SSD_STATE)
    Cm = Cm.reshape(bsz, nc, SSD_CHUNK, SSD_GROUPS, SSD_STATE)
    a_cs = jnp.cumsum(a, axis=2)
    tril = jnp.tril(jnp.ones((SSD_CHUNK, SSD_CHUNK), dtype=bool))
    seg = jnp.where(tril[:, :, None, None], a_cs[:, :, :, None] - a_cs[:, :, None, :], -jnp.inf)
    decay_in = jnp.exp(seg)
    cb = jnp.einsum('bctgn,bcsgn->bctsg', Cm, Bm)
    y_diag = jnp.einsum('bctsgh,bcsghp->bctghp', cb[..., None] * decay_in, x)
    decay_states = jnp.exp(a_cs[:, :, -1:] - a_cs)
    states = jnp.einsum('bcsgn,bcsghp->bcghpn', Bm, x * decay_states[..., None])
    states = jnp.concatenate([jnp.zeros_like(states[:, :1]), states], axis=1)
    chunk_tot = jnp.pad(a_cs[:, :, -1], ((0, 0), (1, 0), (0, 0), (0, 0)))
    cs2 = jnp.cumsum(chunk_tot, axis=1)
    trilc = jnp.tril(jnp.ones((nc + 1, nc + 1), dtype=bool))
    seg_c = jnp.where(trilc[:, :, None, None], cs2[:, :, None] - cs2[:, None, :], -jnp.inf)
    new_states = jnp.einsum('bzcgh,bcghpn->bzghpn', jnp.exp(seg_c), states)
    prev = new_states[:, :-1]
    y_off = jnp.einsum('bctgn,bcghpn->bctghp', Cm, prev) * jnp.exp(a_cs)[..., None]
    return (y_diag + y_off).reshape(bsz, seqlen, SSD_GROUPS, SSD_HPG, SSD_HEADDIM)


def ssd_mixer(z, xbc, dt_raw, conv_w, conv_b, dt_bias, a_log, d_skip, norm_w):
    bsz, seqlen, _ = z.shape
    xbc = jax.nn.silu(causal_depthwise_conv(xbc, conv_w, conv_b)).astype(jnp.float32)
    xs = xbc[..., :SSD_WIDTH].reshape(bsz, seqlen, SSD_GROUPS, SSD_HPG, SSD_HEADDIM)
    Bm = xbc[..., SSD_WIDTH:SSD_WIDTH + SSD_GROUPS * SSD_STATE].reshape(bsz, seqlen, SSD_GROUPS, SSD_STATE)
    Cm = xbc[..., SSD_WIDTH + SSD_GROUPS * SSD_STATE:].reshape(bsz, seqlen, SSD_GROUPS, SSD_STATE)
    dt = jax.nn.softplus(dt_raw.astype(jnp.float32) + dt_bias.astype(jnp.float32))
    dt = dt.reshape(bsz, seqlen, SSD_GROUPS, SSD_HPG)
    A = -jnp.exp(a_log.astype(jnp.float32)).reshape(SSD_GROUPS, SSD_HPG)
    y = ssd_chunked(xs * dt[..., None], dt * A, Bm, Cm)
    y = y + d_skip.astype(jnp.float32).reshape(SSD_GROUPS, SSD_HPG)[:, :, None] * xs
    y = y.reshape(bsz, seqlen, SSD_WIDTH)
    y = group_rms_norm(y * jax.nn.silu(z.astype(jnp.float32)), norm_w, SSD_GROUPS)
    return y.astype(z.dtype)


def setup_inputs(seed: int = 0) -> dict:
    key = jax.random.key(seed)
    ks = jax.random.split(key, 20)
    f32 = jnp.float32

    def nrm(k, shape, scale):
        return jax.random.normal(k, shape, f32) * scale

    x = nrm(ks[0], (BATCH, SEQ, D_MODEL), 1.0)
    pre_mix_norm_w = 1.0 + nrm(ks[1], (DEPTH, D_MODEL), 0.02)
    w_in = nrm(ks[2], (DEPTH, D_MODEL, N_IN), D_MODEL ** -0.5)
    lb_logits = nrm(ks[3], (DEPTH + 1, HG_KEY), 0.5)
    conv_w = nrm(ks[4], (DEPTH, SSD_CONV, SSD_CONV_DIM), SSD_CONV ** -0.5)
    conv_b = nrm(ks[5], (DEPTH, SSD_CONV_DIM), 0.02)
    dt0 = jnp.exp(jax.random.uniform(ks[6], (DEPTH, SSD_HEADS), f32, math.log(1e-3), math.log(1e-1)))
    dt_bias = dt0 + jnp.log(-jnp.expm1(-dt0))
    a_log = jnp.log(jax.random.uniform(ks[7], (DEPTH, SSD_HEADS), f32, 1.0, 16.0))
    d_skip = 1.0 + nrm(ks[8], (DEPTH, SSD_HEADS), 0.1)
    hgrn_norm_w = 1.0 + nrm(ks[9], (DEPTH, HG_VAL), 0.02)
    ssd_norm_w = 1.0 + nrm(ks[10], (DEPTH, SSD_WIDTH), 0.02)
    w_out = nrm(ks[11], (DEPTH, D_MIX, D_MODEL), D_MIX ** -0.5)
    post_mix_norm_w = 1.0 + nrm(ks[12], (DEPTH, D_MODEL), 0.02)
    pre_ffn_norm_w = 1.0 + nrm(ks[13], (DEPTH, D_MODEL), 0.02)
    w_gate = nrm(ks[14], (DEPTH, D_MODEL, D_FF), D_MODEL ** -0.5)
    w_up = nrm(ks[15], (DEPTH, D_MODEL, D_FF), D_MODEL ** -0.5)
    w_down = nrm(ks[16], (DEPTH, D_FF, D_MODEL), D_FF ** -0.5)
    post_ffn_norm_w = 1.0 + nrm(ks[17], (DEPTH, D_MODEL), 0.02)
    return {"x": x, "pre_mix_norm_w": pre_mix_norm_w, "w_in": w_in, "lb_logits": lb_logits,
            "conv_w": conv_w, "conv_b": conv_b, "dt_bias": dt_bias, "a_log": a_log,
            "d_skip": d_skip, "hgrn_norm_w": hgrn_norm_w, "ssd_norm_w": ssd_norm_w,
            "w_out": w_out, "post_mix_norm_w": post_mix_norm_w, "pre_ffn_norm_w": pre_ffn_norm_w,
            "w_gate": w_gate, "w_up": w_up, "w_down": w_down, "post_ffn_norm_w": post_ffn_norm_w}


def reference(x, pre_mix_norm_w, w_in, lb_logits, conv_w, conv_b, dt_bias, a_log, d_skip,
              hgrn_norm_w, ssd_norm_w, w_out, post_mix_norm_w, pre_ffn_norm_w,
              w_gate, w_up, w_down, post_ffn_norm_w):
    lb_all = jnp.cumsum(jax.nn.softmax(lb_logits.astype(jnp.float32), axis=0), axis=0)
    split_pts = _split_points()
    for l in range(DEPTH):
        h = rms_norm(x, pre_mix_norm_w[l])
        proj = jnp.einsum('bld,dn->bln', h, w_in[l])
        q_raw, f_raw, i_in, g_in, z, xbc, dt_raw = jnp.split(proj, split_pts, axis=-1)
        o_a = hgrn2_mixer(q_raw, f_raw, i_in, g_in, lb_all[l], hgrn_norm_w[l])
        o_b = ssd_mixer(z, xbc, dt_raw, conv_w[l], conv_b[l], dt_bias[l], a_log[l],
                        d_skip[l], ssd_norm_w[l])
        mix = jnp.einsum('blm,md->bld', jnp.concatenate([o_a, o_b], axis=-1), w_out[l])
        x = x + rms_norm(mix, post_mix_norm_w[l])
        h = rms_norm(x, pre_ffn_norm_w[l])
        hid = jax.nn.silu(jnp.einsum('bld,df->blf', h, w_gate[l])) * jnp.einsum('bld,df->blf', h, w_up[l])
        ff = jnp.einsum('blf,fd->bld', hid, w_down[l])
        x = x + rms_norm(ff, post_ffn_norm_w[l])
    return x
```

```python
import heapq
import numpy as np
from contextlib import ExitStack
import concourse.bass as bass
import concourse.mybir as mybir
from concourse.bass_utils import run_bass_kernel_spmd

F32 = mybir.dt.float32
BF16 = mybir.dt.bfloat16
ALU = mybir.AluOpType
AF = mybir.ActivationFunctionType

NCORES = 8
D = 2048
NT = 2048
NIN = 6672
DFF = 5632
EPS = 1e-6
HK = 8
XW = 2080
SAME_ENGINE_SYNC = True

C_PREMIX = 0
C_PREFFN = 16
C_L0 = 32
C_L1 = 40
C_HNW = 48
C_CONVW = 56
C_CONVB = 104
C_SNW = 116
C_DSK = 124
C_DTB = 132
C_ALOG = 133
C_M = 134
C_OM = 142
C_V = 150
PPW = 153
NW = 3
HW = 3
K_ID = 0
K_NEG = 128
K_TRE = 256
K_TRO = 320
K_ONE = 384
CW = 512


class Ev:
    __slots__ = ("key", "sem", "val")

    def __init__(self, key, sem, val):
        self.key, self.sem, self.val = key, sem, val


class Buf:
    __slots__ = ("name", "w", "r")

    def __init__(self, name):
        self.name, self.w, self.r = name, None, {}


class Unit:
    __slots__ = ("idx", "eng", "fns", "preds", "opreds", "dma", "dinc", "cost", "succ", "npred", "start", "finish", "ev")

    def __init__(self, idx, eng):
        self.idx, self.eng = idx, eng
        self.fns, self.preds = [], set()
        self.opreds = set()
        self.dma, self.dinc, self.cost = None, 16, 0.0
        self.succ, self.npred = [], 0
        self.start = self.finish = 0.0
        self.ev = None


class Prog:
    ENG = ("pe", "act", "dve", "pool", "sp")
    LAT_X, LAT_S = 700.0, 300.0

    def __init__(self, nc, es):
        self.nc = nc
        self.esem = {e: es.enter_context(nc.semaphore("sem_" + e)) for e in self.ENG}
        self.cnt = {e: 0 for e in self.ENG}
        self.waited = {e: {} for e in self.ENG}
        self.dsem = {}
        self.bufs = {}
        self.es = es
        self.units = []
        self.open_grp = {e: None for e in self.ENG}
        self.bank_last = {}
        self.all_dma_ev = {}
        self.nidx = 0

    def B(self, name):
        b = self.bufs.get(name)
        if b is None:
            b = self.bufs[name] = Buf(name)
        return b

    def _bl(self, x):
        return [self.B(n) if isinstance(n, str) else n for n in x]

    def op(self, eng, fn, r=(), w=(), inc=True, dma=None, dinc=16, cost=200.0):
        r, w = self._bl(r), self._bl(w)
        u = self.open_grp[eng] if dma is None else None
        if u is None:
            u = Unit(self.nidx, eng)
            self.nidx += 1
            self.units.append(u)
        u.fns.append(fn)
        u.cost += cost
        if dma is not None:
            u.dma, u.dinc = dma, dinc
        else:
            self.open_grp[eng] = None if inc else u

        def add(p):
            if p is not None and p is not u:
                u.preds.add(p)

        banks = []
        for b in r + w:
            if b.name.startswith("ps") and b.name[2:3].isdigit() and b.name[2] not in banks:
                banks.append(b.name[2])
        for b in r:
            add(b.w)
        for b in w:
            add(b.w)
            for p in b.r.values():
                add(p)
        for bn in banks:
            for e2, p in self.bank_last.setdefault(bn, {}).items():
                if e2 != eng:
                    add(p)
                elif p is not u:
                    u.opreds.add(p)
            self.bank_last[bn][eng] = u
        for b in r:
            b.r[u.idx] = u
        for b in w:
            b.w = u
            b.r = {}
        return u

    def flush(self):
        nc = self.nc
        units = self.units
        self.units = []
        self.open_grp = {e: None for e in self.ENG}
        inphase = set(id(u) for u in units)
        for u in units:
            u.preds = [p for p in u.preds if id(p) in inphase]
            u.opreds = [p for p in u.opreds if id(p) in inphase and p not in u.preds]
            u.npred = len(u.preds) + len(u.opreds)
            u.succ = []
        for u in units:
            for p in u.preds:
                p.succ.append(u)
            for p in u.opreds:
                p.succ.append(u)
        bl = {}
        for u in reversed(units):
            best = 0.0
            for v in u.succ:
                lat = self.LAT_S if (v.eng == u.eng and u.dma is None and v.dma is None) else self.LAT_X
                t = lat + bl[id(v)]
                if t > best:
                    best = t
            bl[id(u)] = best + u.cost
        eng_t = {e: 0.0 for e in self.ENG}
        avail = {e: [] for e in self.ENG}
        ready = {}
        for u in units:
            if u.npred == 0:
                ready[id(u)] = 0.0
                avail[u.eng].append(u)
        order = {e: [] for e in self.ENG}
        nleft = len(units)
        while nleft:
            best = None
            for e in self.ENG:
                lst = avail[e]
                if not lst:
                    continue
                t_e = eng_t[e]
                cand = None
                for u in lst:
                    s = ready[id(u)]
                    if s < t_e:
                        s = t_e
                    key = (s, -bl[id(u)], u.idx)
                    if cand is None or key < cand[0]:
                        cand = (key, u)
                if best is None or cand[0] < best[0]:
                    best = (cand[0], cand[1], e)
            key, u, e = best
            s = key[0]
            avail[e].remove(u)
            u.start = s
            if u.dma is not None:
                eng_t[e] = s + 60.0
                u.finish = s + u.cost
            else:
                u.finish = s + u.cost
                eng_t[e] = u.finish
            order[e].append(u)
            nleft -= 1
            for v in u.succ:
                lat = self.LAT_S if (v.eng == u.eng and u.dma is None and v.dma is None) else self.LAT_X
                t = u.finish + lat
                if ready.get(id(v), 0.0) < t:
                    ready[id(v)] = t
                v.npred -= 1
                if v.npred == 0:
                    avail[v.eng].append(v)
        for e in self.ENG:
            for u in order[e]:
                if u.dma is not None:
                    d = self.dsem.get(u.dma)
                    if d is None:
                        d = self.dsem[u.dma] = [self.es.enter_context(nc.semaphore("dq_" + u.dma)), 0]
                    d[1] += u.dinc
                    u.ev = Ev(("d", u.dma), d[0], d[1])
                    self.all_dma_ev[u.dma] = u.ev
                else:
                    self.cnt[e] += 1
                    u.ev = Ev(e, self.esem[e], self.cnt[e])
        streams = {}
        for e in self.ENG:
            lst = []
            for u in order[e]:
                need = {}
                for p in u.preds:
                    if p.dma is None and u.dma is None and p.eng == e and (e == "pe" or not SAME_ENGINE_SYNC):
                        continue
                    ev = p.ev
                    cur = need.get(ev.key)
                    if cur is None or cur[1] < ev.val:
                        need[ev.key] = (ev.sem, ev.val)
                waits = []
                for kk_, (s_, v_) in need.items():
                    if self.waited[e].get(kk_, 0) < v_:
                        self.waited[e][kk_] = v_
                        waits.append((s_, v_))
                lst.append((waits, u))
            streams[e] = lst
        tail = []
        for name, ev in self.all_dma_ev.items():
            if self.waited["sp"].get(ev.key, 0) < ev.val:
                self.waited["sp"][ev.key] = ev.val
                tail.append((ev.sem, ev.val))

        def run(eng, lst, extra=()):
            for waits, u in lst:
                for s_, v_ in waits:
                    eng.wait_ge(s_, v_)
                ins = None
                for fn in u.fns:
                    ins = fn(eng)
                if u.dma is not None and u.dinc == 1:
                    ins.then_inc(u.ev.sem)
                else:
                    ins.then_inc(u.ev.sem, u.dinc if u.dma is not None else 1)
            for s_, v_ in extra:
                eng.wait_ge(s_, v_)

        with nc.Block() as block:
            @block.tensor
            def _(e):
                run(e, streams["pe"])

            @block.scalar
            def _(e):
                run(e, streams["act"])

            @block.vector
            def _(e):
                run(e, streams["dve"])

            @block.gpsimd
            def _(e):
                run(e, streams["pool"])

            @block.sync
            def _(e):
                run(e, streams["sp"], tail)

    @staticmethod
    def _n(ap):
        n = 1
        for d in ap.shape[1:]:
            n *= int(d)
        return n

    def mm(self, out, lhsT, rhs, start=True, stop=True, r=(), w=(), inc=None):
        n = self._n(rhs)
        c = 35.0 + n / 2.0 * (4.0 if rhs.dtype == F32 else 1.0)
        return self.op("pe", lambda e: e.matmul(out, lhsT, rhs, start=start, stop=stop),
                       r, w, stop if inc is None else inc, cost=c)

    def tr(self, out, in_, ident, r=(), w=(), inc=True):
        return self.op("pe", lambda e: e.transpose(out, in_, ident), r, w, inc, cost=110.0)

    def act(self, out, in_, func, r=(), w=(), bias=None, scale=None, accum_out=None, eng="act"):
        kw = {}
        if bias is not None:
            kw["bias"] = bias
        if scale is not None:
            kw["scale"] = scale
        if accum_out is not None:
            kw["accum_out"] = accum_out
        c = 220.0 + 0.85 * self._n(in_) + (100.0 if accum_out is not None else 0.0)
        return self.op(eng, lambda e: e.activation(out, in_, func, **kw), r, w, cost=c)

    def _dc(self, out, n_in=1, eng="dve"):
        n = self._n(out)
        if eng == "pool":
            return 300.0 + 2.0 * n
        return 100.0 + (1.05 if n_in == 1 else 2.1) * n * (0.5 if out.dtype == BF16 and n_in == 1 else 1.0)

    def tt(self, out, in0, in1, op, r=(), w=(), eng="dve"):
        return self.op(eng, lambda e: e.tensor_tensor(out, in0, in1, op), r, w, cost=self._dc(out, 2, eng))

    def ts(self, out, in0, s1, s2, op0, op1=None, r=(), w=(), eng="dve"):
        c = self._dc(out, 1, eng)
        if op1 is None:
            return self.op(eng, lambda e: e.tensor_scalar(out, in0, s1, None, op0), r, w, cost=c)
        return self.op(eng, lambda e: e.tensor_scalar(out, in0, s1, s2, op0, op1), r, w, cost=c)

    def stt(self, out, in0, scalar, in1, op0, op1, r=(), w=()):
        return self.op("dve", lambda e: e.scalar_tensor_tensor(out, in0, scalar, in1, op0, op1), r, w,
                       cost=self._dc(out, 2))

    def scan(self, out, d0, d1, init, op0, op1, r=(), w=()):
        return self.op("dve", lambda e: e.tensor_tensor_scan(out, d0, d1, init, op0, op1), r, w,
                       cost=100.0 + 2.1 * self._n(out))

    def cp(self, out, in_, r=(), w=(), eng="dve"):
        if eng == "act":
            return self.op("act", lambda e: e.activation(out, in_, AF.Copy), r, w, cost=220.0 + 0.85 * self._n(out))
        return self.op(eng, lambda e: e.tensor_copy(out, in_), r, w, cost=self._dc(out, 1, eng))

    def recip(self, out, in_, r=(), w=()):
        return self.op("dve", lambda e: e.reciprocal(out, in_), r, w, cost=self._dc(out, 1))

    def memset(self, ap, val, w=(), eng="dve"):
        return self.op(eng, lambda e: e.memset(ap, val), (), w, cost=self._dc(ap, 1, eng))

    def dma(self, q, out, in_, sem, r=(), w=()):
        nb = 1
        for d in out.shape:
            nb *= int(d)
        c = 2200.0 + nb * (4 if in_.dtype == F32 else 2) / 250.0
        return self.op(q, lambda e: e.dma_start(out=out, in_=in_), r, w, dma=sem, cost=c)


def bc_last(ap2d, n):
    return ap2d.unsqueeze(2).broadcast_to([ap2d.shape[0], ap2d.shape[1], n])


def build(dbg=None, nocc=False, stop=None, psec="abc"):
    nc = bass.Bass("TRN2", target_bir_lowering=False)
    xin = nc.dram_tensor("xin", [NT + 128, D], F32, kind="ExternalInput").ap()
    xpre = nc.dram_tensor("xpre", [NW * NT, D], F32, kind="ExternalInput").ap()
    gl2 = nc.dram_tensor("gl2", [16, 1], F32, kind="Internal").ap()
    w_in = nc.dram_tensor("w_in", [D, NIN], F32, kind="ExternalInput").ap()
    w_out = nc.dram_tensor("w_out", [D, D], F32, kind="ExternalInput").ap()
    w_gate = nc.dram_tensor("w_gate", [D, DFF], F32, kind="ExternalInput").ap()
    w_up = nc.dram_tensor("w_up", [D, DFF], F32, kind="ExternalInput").ap()
    w_down = nc.dram_tensor("w_down", [DFF, D], F32, kind="ExternalInput").ap()
    pp_d = nc.dram_tensor("pp", [128, PPW], F32, kind="ExternalInput").ap()
    cst_d = nc.dram_tensor("cst", [128, CW], F32, kind="ExternalInput").ap()
    pw_d = nc.dram_tensor("pw", [128, 2, D], F32, kind="ExternalInput").ap()
    y = nc.dram_tensor("y", [NT, D], F32, kind="ExternalOutput").ap()
    if dbg:
        dbg_d = nc.dram_tensor("dbg", [128, 16, NT], F32, kind="ExternalOutput").ap()
    olocd = nc.dram_tensor("olocd", [16, 128, NT], F32, kind="Internal").ap()
    qhd = nc.dram_tensor("qhd", [8, 128, NT], BF16, kind="Internal").ap()
    acsd = nc.dram_tensor("acsd", [16, NT], F32, kind="Internal").ap()
    gsd = nc.dram_tensor("gsd", [16, NT], F32, kind="Internal").ap()
    x1d = nc.dram_tensor("x1d", [NT, D], F32, kind="Internal").ap()
    gl = nc.dram_tensor("gl", [16, 1], F32, kind="Internal").ap()
    xbuf = nc.dram_tensor("xbuf", [128, XW], F32, kind="Internal").ap()
    xg = nc.dram_tensor("xg", [NCORES * 128, XW], F32, addr_space="Local", kind="Internal").ap()

    w_in_v = w_in.rearrange("(k p) n -> p k n", p=128)

    with ExitStack() as es:
        P = Prog(nc, es)

        uid = [0]

        def sb(name, shape, dt, stack=None):
            uid[0] += 1
            return (stack or es).enter_context(nc.sbuf_tensor(f"{name}_s{uid[0]}", shape, dt))

        pp = sb("pp", [128, PPW], F32)
        cst = sb("cst", [128, CW], F32)
        idb = sb("idb", [128, 128], BF16)
        lbc = sb("lbc", [128, 16], F32)
        sinst = ExitStack()
        Sinb = sb("Sinb", [128, HK, 128], BF16, sinst)
        SinA = sb("SinA", [128, 8, 128], BF16, sinst)
        SinB = sb("SinB", [128, 8, 128], BF16, sinst)
        ps = [es.enter_context(nc.psum_tensor(f"ps{i}", [128, 512], F32)) for i in range(8)]
        psb = ps[7][:].bitcast(BF16)
        ps3b = ps[3][:].bitcast(BF16)
        ident = cst[:, K_ID:K_ID + 128]
        ones = cst[:, K_ONE:K_ONE + 128]

        with ExitStack() as ph:
            P.dma("sp", pp[:], pp_d, "pp", w=["pp"])
            P.dma("sp", cst[:], cst_d, "cst", w=["cst"])
            P.cp(idb[:], ident, r=["cst"], w=["idb"])
            t8 = sb("t8", [128, 32], F32, ph)
            l0, l1 = pp[:, C_L0:C_L0 + 8], pp[:, C_L1:C_L1 + 8]
            P.tt(t8[:, 0:8], l0, l1, ALU.max, r=["pp"], w=["t8"])
            P.tt(t8[:, 8:16], l0, t8[:, 0:8], ALU.subtract, r=["pp", "t8"], w=["t8"])
            P.tt(t8[:, 16:24], l1, t8[:, 0:8], ALU.subtract, r=["pp", "t8"], w=["t8"])
            P.act(t8[:, 8:24], t8[:, 8:24], AF.Exp, r=["t8"], w=["t8"])
            P.tt(t8[:, 24:32], t8[:, 8:16], t8[:, 16:24], ALU.add, r=["t8"], w=["t8"])
            P.recip(t8[:, 24:32], t8[:, 24:32], r=["t8"], w=["t8"])
            P.tt(lbc[:, 0:8], t8[:, 8:16], t8[:, 24:32], ALU.mult, r=["t8"], w=["lbc"])
            P.ts(lbc[:, 8:16], lbc[:, 0:8], -1.0, 1.0, ALU.mult, ALU.add, r=["lbc"], w=["lbc"])
            P.flush()

        with ExitStack() as ph:
            hTp = sb("hTp", [128, 16, NT], BF16, ph)
            T = [sb(f"T{j}", [128, D], F32, ph) for j in range(4)]
            st = sb("stp", [128, 16], F32, ph)
            wb = [sb(f"wbp{i}", [128, 16, 512], BF16, ph) for i in range(2)]
            v_tm = sb("v_tmp", [128, 16, 512], BF16, ph)
            kb = [sb(f"kb{i}", [128, NT], BF16, ph) for i in range(2)]
            ktmP = [sb(f"ktmP{i}", [128, 16, 128], BF16, ph) for i in range(2)]
            Sh = sb("Sh", [128, HK, 128], F32, ph)
            Ss = sb("Ss", [128, 16, 64], F32, ph)
            gsm = sb("gsm", [128, 4], F32, ph)
            ub = [sb(f"ubp{i}", [128, NT + 3], F32, ph) for i in range(2)]
            xwt = [sb(f"xwt{i}", [128, 16, 128], BF16, ph) for i in range(2)]
            wst = sb("wst", [128, 16, 16], F32, ph)
            wdt = sb("wdtp", [128, 16, 16], BF16, ph)
            aneg = sb("anegp", [16, 2], F32, ph)
            dsb = sb("dsbp", [128, 16], F32, ph)
            utail = sb("utail", [128, 10, 3], F32, ph)
            junk = kb[0]
            BT, Btm = kb[1], ktmP[1]
            P.memset(Sh[:], 0.0, w=["Sh"])
            P.memset(Ss[:], 0.0, w=["Ss"])
            P.memset(utail[:], 0.0, w=["utail"])
            P.dma("pool", wdt[:], w_in_v[:, :, 6656:6672], "wdtp", w=["wdtp"])
            P.act(aneg[:, 0:1], pp[0:16, C_ALOG:C_ALOG + 1], AF.Exp, r=["pp"], w=["anegp"])
            P.ts(aneg[:, 1:2], aneg[:, 0:1], -1.0, None, ALU.mult, r=["anegp"], w=["anegp"])
            ones_bc = cst[:, K_ONE:K_ONE + 1].broadcast_to([128, NT])

            def run_tasks(tasks, width=2):
                pending = list(tasks)
                active = [None] * width
                while pending or any(a is not None for a in active):
                    for s in range(width):
                        if active[s] is None and pending:
                            active[s] = pending.pop(0)(s)
                        if active[s] is not None:
                            try:
                                next(active[s])
                            except StopIteration:
                                active[s] = None

            PJ = [(0, 6), (2, 7)]
            AX = [1, 3]

            def v_task(hg):
                def gen(slot):
                    P.dma("pool", wb[0][:], w_in_v[:, :, 1024 + hg * 512:1024 + (hg + 1) * 512], "wbp0", w=["wbp0"])
                    P.dma("pool", wb[1][:], w_in_v[:, :, 2048 + hg * 512:2048 + (hg + 1) * 512], "wbp1", w=["wbp1"])
                    yield
                    for ti in range(16):
                        bk = 4 + ti % 2
                        for kc in range(16):
                            P.mm(ps[bk][:, :], hTp[:, kc, ti * 128:(ti + 1) * 128], wb[1][:, kc, :],
                                 start=(kc == 0), stop=(kc == 15), r=["wbp1", f"hTp_{ti // 4}"], w=[f"ps{bk}"])
                        P.cp(v_tm[:, ti, :], ps[bk][:, :], r=[f"ps{bk}"], w=["v_tmp"], eng=("act" if ti % 2 else "dve"))
                        yield
                return gen

            def transposes_bf(s, src, srcname, dst, dstname):
                ax = AX[s]
                axb = ps[ax][:].bitcast(BF16)
                for t8 in range(2):
                    for tt_ in range(8):
                        ti = t8 * 8 + tt_
                        P.tr(axb[:, tt_ * 128:(tt_ + 1) * 128], src[:, ti * 128:(ti + 1) * 128], idb[:],
                             r=[srcname, "idb"], w=[f"ps{ax}"], inc=(tt_ == 7))
                    P.cp(dst[:, t8 * 8:(t8 + 1) * 8, :], axb.rearrange("p (c n) -> p c n", n=128),
                         r=[f"ps{ax}"], w=[dstname], eng=("act" if t8 else "dve"))
                    yield

            def h_task(h):
                hh = h % 4

                def gen(s):
                    A, C = T[2 * s], T[2 * s + 1]
                    An, Cn = f"T{2 * s}", f"T{2 * s + 1}"
                    gcol = gsm[:, s:s + 1]
                    ax = AX[s]
                    for tg in range(4):
                        bk = PJ[s][tg % 2]
                        for kc in range(16):
                            P.mm(ps[bk][:, :], wb[0][:, kc, hh * 128:(hh + 1) * 128], hTp[:, kc, tg * 512:(tg + 1) * 512],
                                 start=(kc == 0), stop=(kc == 15), r=["wbp0", f"hTp_{tg}"], w=[f"ps{bk}"])
                        P.act(A[:, tg * 512:(tg + 1) * 512], ps[bk][:, :], AF.Sigmoid, r=[f"ps{bk}"], w=[An])
                        yield
                    P.act(A[:], A[:], AF.Ln, r=[An, "lbc"], w=[An], scale=lbc[:, 8 + h:9 + h], bias=lbc[:, h:h + 1])
                    yield
                    P.scan(C[:], ones_bc, A[:], 0.0, ALU.mult, ALU.add, r=["cst", An], w=[Cn])
                    yield
                    P.act(A[:], A[:], AF.Exp, r=[An], w=[An])
                    yield
                    P.act(A[:], A[:], AF.Identity, r=[An], w=[An], scale=-1.0, bias=1.0)
                    yield
                    P.cp(gcol, C[:, NT - 1:NT], r=[Cn], w=[f"gsm{s}"])
                    P.act(C[:], C[:], AF.Exp, r=[Cn, f"gsm{s}"], w=[Cn], scale=-1.0, bias=gcol)
                    yield
                    P.tt(kb[s][:], A[:], C[:], ALU.mult, r=[An, Cn], w=[f"kb{s}"])
                    P.act(gcol, gcol, AF.Exp, r=[f"gsm{s}"], w=[f"gsm{s}"])
                    yield
                    yield from transposes_bf(s, kb[s], f"kb{s}", ktmP[s], f"ktmP{s}")
                    for ti in range(16):
                        P.mm(ps[ax][:, 0:128], ktmP[s][:, ti, :], v_tm[:, ti, hh * 128:(hh + 1) * 128],
                             start=(ti == 0), stop=(ti == 15), r=[f"ktmP{s}", "v_tmp"], w=[f"ps{ax}"])
                    yield
                    P.stt(Sh[:, h, :], Sh[:, h, :], gcol, ps[ax][:, 0:128], ALU.mult, ALU.add,
                          r=[f"Sh{h}", f"gsm{s}", f"ps{ax}"], w=[f"Sh{h}"])
                    yield
                return gen

            def conv_task(w, s_w, cols, jc, kind, j):
                def gen(s):
                    acc = T[2 + s]
                    an = f"T{2 + s}"
                    ubs, ubn = ub[s], f"ubp{s}"
                    ax = AX[s]
                    for tg in range(4):
                        bk = PJ[s][tg % 2]
                        for kc in range(16):
                            P.mm(ps[bk][:, :], wb[s_w][:, kc, cols:cols + 128], hTp[:, kc, tg * 512:(tg + 1) * 512],
                                 start=(kc == 0), stop=(kc == 15), r=[f"wbp{s_w}", f"hTp_{tg}"], w=[f"ps{bk}"])
                        P.cp(ubs[:, 3 + tg * 512:3 + (tg + 1) * 512], ps[bk][:, :], r=[f"ps{bk}"], w=[ubn],
                             eng=("act" if tg % 2 else "dve"))
                        yield
                    P.cp(ubs[:, 0:3], utail[:, jc, :], r=["utail"], w=[ubn])
                    wc = lambda k_: pp[:, C_CONVW + k_ * 12 + jc:C_CONVW + k_ * 12 + jc + 1]
                    P.ts(acc[:], ubs[:, 0:NT], wc(0), pp[:, C_CONVB + jc:C_CONVB + jc + 1], ALU.mult, ALU.add,
                         r=[ubn, "pp"], w=[an])
                    yield
                    for k_ in range(1, 4):
                        P.stt(acc[:], ubs[:, k_:k_ + NT], wc(k_), acc[:], ALU.mult, ALU.add, r=[ubn, "pp", an], w=[an])
                        yield
                    P.cp(utail[:, jc, :], ubs[:, NT:NT + 3], r=[ubn], w=["utail"], eng="act")
                    if kind == "B":
                        P.act(BT[:], acc[:], AF.Silu, r=[an], w=["kb1"])
                        yield
                        yield from transposes_bf(s, BT, "kb1", Btm, "ktmP1")
                        return
                    P.act(acc[:], acc[:], AF.Silu, r=[an], w=[an])
                    yield
                    for t4 in range(4):
                        for c4 in range(4):
                            ti = t4 * 4 + c4
                            P.tr(ps[ax][:, c4 * 128:(c4 + 1) * 128], acc[:, ti * 128:(ti + 1) * 128], ident,
                                 r=[an, "cst"], w=[f"ps{ax}"], inc=(c4 == 3))
                        for c4 in range(4):
                            ti = t4 * 4 + c4
                            for hx in range(2):
                                P.ts(xwt[s][:, ti, hx * 64:(hx + 1) * 64], ps[ax][:, c4 * 128 + hx * 64:c4 * 128 + (hx + 1) * 64],
                                     wst[:, ti, 2 * j + hx:2 * j + hx + 1], None, ALU.mult,
                                     r=[f"ps{ax}", "wst"], w=[f"xwt{s}"], eng=("dve" if t4 % 2 else "pool") if False else "dve")
                        yield
                    for ti in range(16):
                        P.mm(ps[ax][:, 0:128], Btm[:, ti, :], xwt[s][:, ti, :],
                             start=(ti == 0), stop=(ti == 15), r=["ktmP1", f"xwt{s}"], w=[f"ps{ax}"])
                    yield
                    P.tt(Ss[:, 2 * j:2 * j + 2, :], Ss[:, 2 * j:2 * j + 2, :], bc_last(dsb[:, 2 * j:2 * j + 2], 64), ALU.mult,
                         r=[f"Ss{j}", "dsbp"], w=[f"Ss{j}"])
                    P.tt(Ss[:, 2 * j:2 * j + 2, :], Ss[:, 2 * j:2 * j + 2, :],
                         ps[ax][:, 0:128].rearrange("p (t v) -> p t v", v=64), ALU.add,
                         r=[f"Ss{j}", f"ps{ax}"], w=[f"Ss{j}"])
                    yield
                return gen

            evq = 0
            for w in range(NW):
                for g4 in range(4):
                    for j in range(4):
                        ti = w * 16 + g4 * 4 + j
                        xb = f"T{j}"
                        P.dma("sp", T[j][:], xpre[ti * 128:(ti + 1) * 128, :], xb, w=[xb])
                        ss = st[:, 4 * j:4 * j + 1]
                        P.act(junk[:], T[j][:], AF.Square, r=[xb], w=["kb0", f"stp{j}"], accum_out=ss)
                        P.ts(st[:, 4 * j + 1:4 * j + 2], ss, 1.0 / D, EPS, ALU.mult, ALU.add, r=[f"stp{j}"], w=[f"stp{j}"])
                        P.act(st[:, 4 * j + 2:4 * j + 3], st[:, 4 * j + 1:4 * j + 2], AF.Sqrt, r=[f"stp{j}"], w=[f"stp{j}"])
                        P.recip(st[:, 4 * j + 3:4 * j + 4], st[:, 4 * j + 2:4 * j + 3], r=[f"stp{j}"], w=[f"stp{j}"])
                        P.act(T[j][:], T[j][:], AF.Copy, r=[xb, f"stp{j}"], w=[xb], scale=st[:, 4 * j + 3:4 * j + 4])
                    for kc in range(16):
                        bk = kc % 4
                        for j in range(4):
                            P.tr(ps[bk][:, j * 128:(j + 1) * 128], T[j][:, kc * 128:(kc + 1) * 128], ident,
                                 r=[f"T{j}", "cst"], w=[f"ps{bk}"], inc=(j == 3))
                        wcol = pp[:, C_PREMIX + kc:C_PREMIX + kc + 1]
                        o_ap = hTp[:, kc, g4 * 512:(g4 + 1) * 512]
                        if evq % 2 == 0:
                            P.ts(o_ap, ps[bk][:, :], wcol, None, ALU.mult, r=[f"ps{bk}", "pp"], w=[f"hTp_{g4}"])
                        else:
                            P.act(o_ap, ps[bk][:, :], AF.Copy, r=[f"ps{bk}", "pp"], w=[f"hTp_{g4}"], scale=wcol)
                        evq += 1
                if w >= NW - HW and "b" in psec:
                    for hg in range(2):
                        run_tasks([v_task(hg)], width=1)
                        run_tasks([h_task(hg * 4 + hh) for hh in range(4)])
                if "c" not in psec:
                    continue
                P.dma("pool", wb[1][:], w_in_v[:, :, 6144:6656], "wbp1", w=["wbp1"])
                P.dma("pool", wb[0][:], w_in_v[:, :, 5120:5632], "wbp0", w=["wbp0"])
                for tg in range(4):
                    bk = 4 + tg % 2
                    for kc in range(16):
                        P.mm(ps[bk][0:16, :], wdt[:, kc, :], hTp[:, kc, tg * 512:(tg + 1) * 512],
                             start=(kc == 0), stop=(kc == 15), r=["wdtp", f"hTp_{tg}"], w=[f"ps{bk}"])
                    P.act(T[0][0:16, tg * 512:(tg + 1) * 512], ps[bk][0:16, :], AF.Exp, r=[f"ps{bk}", "pp"], w=["T0"],
                          bias=pp[0:16, C_DTB:C_DTB + 1])
                P.act(T[0][0:16, :], T[0][0:16, :], AF.Ln, r=["T0"], w=["T0"], bias=1.0)
                P.ts(T[0][0:16, :], T[0][0:16, :], pp[0:16, C_V + w:C_V + w + 1], None, ALU.mult, r=["T0", "pp"], w=["T0"])
                P.ts(T[1][0:16, :], T[0][0:16, :], aneg[:, 1:2], None, ALU.mult, r=["T0", "anegp"], w=["T1"])
                P.scan(T[0][64:80, :], ones_bc[0:16, :], T[1][0:16, :], 0.0, ALU.mult, ALU.add, r=["cst", "T1"], w=["T0"])
                Gs = T[0][64:80, :]
                P.dma("sp", gl2, Gs[:, NT - 1:NT], "gl2", r=["T0"], w=["gl2"])
                P.cp(gsm[64:80, 2:3], Gs[:, NT - 1:NT], r=["T0"], w=["gsm2"])
                P.act(T[1][64:80, :], Gs, AF.Exp, r=["T0", "gsm2"], w=["T1"], scale=-1.0, bias=gsm[64:80, 2:3])
                P.cp(T[1][0:16, :], T[1][64:80, :], r=["T1"], w=["T1"], eng="act")
                P.tt(T[1][0:16, :], T[1][0:16, :], T[0][0:16, :], ALU.mult, r=["T0", "T1"], w=["T1"])
                for ti in range(16):
                    P.tr(ps[3][:, ti * 16:(ti + 1) * 16], T[1][0:16, ti * 128:(ti + 1) * 128], cst[0:16, K_ID:K_ID + 16],
                         r=["T1", "cst"], w=["ps3"], inc=(ti == 15))
                P.cp(wst[:], ps[3][:, 0:256].rearrange("p (t h) -> p t h", h=16), r=["ps3"], w=["wst"])
                P.dma("sp", dsb[:], bass.AP(gl2.tensor, 0, [[0, 128], [1, 16]]), "dsbp", r=["gl2"], w=["dsbp"])
                P.act(dsb[:], dsb[:], AF.Exp, r=["dsbp"], w=["dsbp"])
                for g in range(2):
                    if g == 1:
                        P.dma("pool", wb[0][:], w_in_v[:, :, 5120 + 512:5120 + 1024], "wbp0", w=["wbp0"])
                    run_tasks([conv_task(w, 1, g * 128, 8 + g, "B", None)], width=1)
                    run_tasks([conv_task(w, 0, jp * 128, 4 * g + jp, "x", 4 * g + jp) for jp in range(4)])
            if dbg == "sin":
                P.dma("sp", dbg_d[:, 0, 0:1024], Sh[:].rearrange("p h v -> p (h v)"), "dbgs", r=[f"Sh{h}" for h in range(8)] + ["Sh"], w=["dbgd"])
                P.dma("sp", dbg_d[:, 1, 0:1024], Ss[:].rearrange("p h v -> p (h v)"), "dbgs", r=[f"Ss{j}" for j in range(8)] + ["Ss"], w=["dbgd"])
                P.flush()
                return nc
            P.cp(Sinb[:], Sh[:], r=[f"Sh{h}" for h in range(8)], w=["Sinb"], eng="act")
            P.memset(SinA[:], 0.0, w=["SinA"], eng="pool")
            P.memset(SinB[:], 0.0, w=["SinB"], eng="pool")
            Ss4 = Ss[:].rearrange("p (j t) v -> p j (t v)", t=2)
            P.cp(SinA[:, :, 0:64], Ss4[:, :, 0:64], r=[f"Ss{j}" for j in range(8)], w=["SinA"], eng="act")
            P.cp(SinB[:, :, 64:128], Ss4[:, :, 64:128], r=[f"Ss{j}" for j in range(8)], w=["SinB"], eng="act")
            P.flush()
            if stop == "P":
                return nc

        mid = ExitStack()
        hT = sb("hT", [128, 16, NT], BF16, mid)
        hTh = sb("hTh", [128, 16, 4], BF16, mid)
        with ExitStack() as ph:
            xt = [sb(f"xt{j}", [128, D], F32, ph) for j in range(8)]
            junk = sb("junk", [128, D], BF16, ph)
            st = sb("st", [128, 32], F32, ph)
            groups = [[0]] + [[1 + 4 * g + i for i in range(4)] for g in range(4)]
            evq = 0
            for gi, grp in enumerate(groups):
                for j0, ti in enumerate(grp):
                    j = (gi % 2) * 4 + j0
                    xb = f"xt{j}"
                    P.dma("sp", xt[j][:], xin[ti * 128:(ti + 1) * 128, :], xb, w=[xb])
                    ss = st[:, 4 * j:4 * j + 1]
                    P.act(junk[:], xt[j][:], AF.Square, r=[xb], w=["junk", f"st{j}"], accum_out=ss)
                    P.ts(st[:, 4 * j + 1:4 * j + 2], ss, 1.0 / D, EPS, ALU.mult, ALU.add, r=[f"st{j}"], w=[f"st{j}"])
                    P.act(st[:, 4 * j + 2:4 * j + 3], st[:, 4 * j + 1:4 * j + 2], AF.Sqrt, r=[f"st{j}"], w=[f"st{j}"])
                    P.recip(st[:, 4 * j + 3:4 * j + 4], st[:, 4 * j + 2:4 * j + 3], r=[f"st{j}"], w=[f"st{j}"])
                    P.act(xt[j][:], xt[j][:], AF.Copy, r=[xb, f"st{j}"], w=[xb], scale=st[:, 4 * j + 3:4 * j + 4])
                n = len(grp)
                for kc in range(16):
                    bk = kc % 8
                    for j0 in range(n):
                        j = (gi % 2) * 4 + j0
                        P.tr(ps[bk][:, j0 * 128:(j0 + 1) * 128], xt[j][:, kc * 128:(kc + 1) * 128], ident,
                             r=[f"xt{j}", "cst"], w=[f"ps{bk}"], inc=(j0 == n - 1))
                    wcol = pp[:, C_PREMIX + kc:C_PREMIX + kc + 1]
                    if gi == 0:
                        o_ap, i_ap, wb_ = hTh[:, kc, 0:4], ps[bk][:, 124:128], "hTh"
                    else:
                        c0 = (gi - 1) * 512
                        o_ap, i_ap, wb_ = hT[:, kc, c0:c0 + 512], ps[bk][:, 0:512], "hT"
                    if evq % 2 == 0:
                        P.ts(o_ap, i_ap, wcol, None, ALU.mult, r=[f"ps{bk}", "pp"], w=[wb_])
                    else:
                        P.act(o_ap, i_ap, AF.Copy, r=[f"ps{bk}", "pp"], w=[wb_], scale=wcol)
                    evq += 1
            P.flush()

        with ExitStack() as ph:
            wb = [sb(f"wb{i}", [128, 16, 512], BF16, ph) for i in range(3)]
            v_tm = sb("v_tm", [128, 16, 512], BF16, ph)
            qf = sb("qf", [128, NT], F32, ph)
            fg = sb("fg", [128, NT], F32, ph)
            kk = sb("kk", [128, NT], F32, ph)
            bb = sb("bb", [128, NT], F32, ph)
            GG = sb("GG", [128, NT], F32, ph)
            qt = [sb(f"qt{i}", [128, NT], BF16, ph) for i in range(2)]
            kt = [sb(f"kt{i}", [128, NT], BF16, ph) for i in range(2)]
            qh = sb("qh", [128, NT], BF16, ph)
            rm = sb("rm", [128, NT], BF16, ph)
            chs = [sb(f"chs{i}", [128, 3, 32], F32, ph) for i in range(2)]
            Sb = [sb(f"Sb{i}", [128, 128], BF16, ph) for i in range(4)]
            Sst = [sb(f"Sst{i}", [128, 128], F32, ph) for i in range(4)]
            scm = [sb(f"scm{i}", [128, 64], BF16, ph) for i in range(4)]
            ktm = [sb(f"ktm{i}", [128, 128], BF16, ph) for i in range(4)]
            ost1_ = sb("ost0", [128, 256], F32, ph)
            ost = [ost1_, ost1_]

            P.memset(rm[:], 1.0, w=["rm"])
            P.memset(rm[:].rearrange("p (c j) -> p c j", j=64)[:, :, 0:1], 0.0, w=["rm"])
            for i in range(4):
                P.memset(ktm[i][:], 0.0, w=[f"ktm{i}"], eng="pool")
            wslot = [0]

            def load_w(c0):
                s = wslot[0] % 3
                wslot[0] += 1
                P.dma("pool", wb[s][:], w_in_v[:, :, c0:c0 + 512], f"wb{s}", w=[f"wb{s}"])
                return s

            pbank = [0]

            def proj_fm(s, cols, tg, w128=128):
                bk = pbank[0] % 2
                pbank[0] += 1
                for kc in range(16):
                    P.mm(ps[bk][0:w128, :], wb[s][:, kc, cols:cols + w128], hT[:, kc, tg * 512:(tg + 1) * 512],
                         start=(kc == 0), stop=(kc == 15), r=[f"wb{s}", "hT"], w=[f"ps{bk}"])
                return bk

            for hg in range(2):
                sq = load_w(hg * 512)
                sf = load_w(1024 + hg * 512)
                si = load_w(2048 + hg * 512)
                for ti in range(16):
                    bk = pbank[0] % 2
                    pbank[0] += 1
                    for kc in range(16):
                        P.mm(ps[bk][:, :], hT[:, kc, ti * 128:(ti + 1) * 128], wb[si][:, kc, :],
                             start=(kc == 0), stop=(kc == 15), r=[f"wb{si}", "hT"], w=[f"ps{bk}"])
                    P.cp(v_tm[:, ti, :], ps[bk][:, :], r=[f"ps{bk}"], w=["v_tm"], eng=("act" if ti % 2 else "dve"))
                def s1_task(h, hh, sq=sq, sf=sf):
                    p = h % 2
                    qt_, kt_, chs_ = qt[p], kt[p], chs[p]
                    qn, kn, cn = f"qt{p}", f"kt{p}", f"chs{p}"

                    def gen():
                        HL = NT // 2
                        for tg in range(4):
                            bk = proj_fm(sq, hh * 128, tg)
                            P.act(qf[:, tg * 512:(tg + 1) * 512], ps[bk][:, :], AF.Silu, r=[f"ps{bk}"], w=[f"qf_{tg // 2}"])
                        for tg in range(4):
                            bk = proj_fm(sf, hh * 128, tg)
                            P.act(fg[:, tg * 512:(tg + 1) * 512], ps[bk][:, :], AF.Sigmoid, r=[f"ps{bk}"], w=[f"fg_{tg // 2}"])
                        for hf in range(2):
                            sl = slice(hf * HL, (hf + 1) * HL)
                            fn_, kn_, bn_, gn_ = f"fg_{hf}", f"kk_{hf}", f"bb_{hf}", f"GG_{hf}"
                            P.ts(fg[:, sl], fg[:, sl], lbc[:, 8 + h:9 + h], lbc[:, h:h + 1], ALU.mult, ALU.add, r=[fn_, "lbc"], w=[fn_])
                            P.ts(kk[:, sl], fg[:, sl], -1.0, 1.0, ALU.mult, ALU.add, r=[fn_], w=[kn_])
                            P.act(fg[:, sl], fg[:, sl], AF.Ln, r=[fn_], w=[fn_])
                            P.scan(bb[:, sl], rm[:, sl], fg[:, sl], 0.0, ALU.mult, ALU.add, r=["rm", fn_], w=[bn_])
                            init = 0.0 if hf == 0 else GG[:, HL - 1:HL]
                            P.scan(GG[:, sl], cst[:, K_ONE:K_ONE + 1].broadcast_to([128, HL]), fg[:, sl], init, ALU.mult, ALU.add,
                                   r=["cst", fn_] + (["GGlast"] if hf else []), w=[gn_] + (["GGlast"] if hf == 0 else []))
                        for hf in range(2):
                            sl = slice(hf * HL, (hf + 1) * HL)
                            gn_ = f"GG_{hf}"
                            P.act(GG[:, sl], GG[:, sl], AF.Exp, r=[gn_] + (["GGlast"] if hf == 0 else []), w=[gn_] + (["GGlast"] if hf == 0 else []))
                            P.stt(qh[:, sl], qf[:, sl], 128.0 ** -0.5, GG[:, sl], ALU.mult, ALU.mult, r=[f"qf_{hf}", gn_], w=[f"qh_{hf}"])
                        P.dma("sp", qhd[h], qh[:], "qhd", r=["qh_0", "qh_1"], w=[f"qhd{h}"])
                        for hf in range(2):
                            sl = slice(hf * HL, (hf + 1) * HL)
                            cs_ = slice(hf * 16, (hf + 1) * 16)
                            fn_, kn_, bn_, gn_ = f"fg_{hf}", f"kk_{hf}", f"bb_{hf}", f"GG_{hf}"
                            cnh = f"{cn}_{hf}"
                            b3 = bb[:, sl].rearrange("p (c j) -> p c j", j=64)
                            P.act(chs_[:, 0, cs_], b3[:, :, 63], AF.Exp, r=[bn_], w=[cnh])
                            P.act(chs_[:, 2, cs_], b3[:, :, 31], AF.Exp, r=[bn_], w=[cnh])
                            f3 = fg[:, sl].rearrange("p (c j) -> p c j", j=64)
                            P.tt(f3, b3, bc_last(b3[:, :, 31], 64), ALU.subtract, r=[bn_], w=[fn_])
                            P.act(chs_[:, 1, cs_], f3[:, :, 63], AF.Exp, r=[fn_], w=[cnh])
                            P.act(bb[:, sl], fg[:, sl], AF.Exp, r=[fn_], w=[bn_])
                            P.act(GG[:, sl], fg[:, sl], AF.Exp, r=[fn_], w=[gn_], scale=-1.0)
                            P.stt(qt_[:, sl], qf[:, sl], 128.0 ** -0.5, bb[:, sl], ALU.mult, ALU.mult, r=[f"qf_{hf}", bn_], w=[f"{qn}_{hf}"])
                            P.tt(kt_[:, sl], kk[:, sl], GG[:, sl], ALU.mult, r=[kn_, gn_], w=[f"{kn}_{hf}"])
                        yield
                    return gen()

                def s2_task(h, hh):
                    p = h % 2
                    qt_, kt_, chs_ = qt[p], kt[p], chs[p]
                    qn, kn, cn = f"qt{p}", f"kt{p}", f"chs{p}"

                    def gen():

                        def stage_a(c):
                            ti, half = c // 2, c % 2
                            t0 = c * 64
                            tsl = ti % 2
                            if half == 0:
                                tb = ps3b[:, 512 + tsl * 128:512 + (tsl + 1) * 128]
                                P.tr(tb, kt_[:, ti * 128:(ti + 1) * 128], idb[:],
                                     r=[f"{kn}_{c // 16}", "idb"], w=[f"ps3b{tsl}"])
                                P.cp(ktm[2 * tsl][0:64, :], ps3b[0:64, 512 + tsl * 128:512 + (tsl + 1) * 128], r=[f"ps3b{tsl}"],
                                     w=[f"ktm{2 * tsl}"], eng="act")
                                P.cp(ktm[2 * tsl + 1][64:128, :], ps3b[64:128, 512 + tsl * 128:512 + (tsl + 1) * 128], r=[f"ps3b{tsl}"],
                                     w=[f"ktm{2 * tsl + 1}"], eng="act")
                            a0 = ((c // 2) % 4) * 64
                            ab = 2 if c % 2 == 0 else 7
                            an_ = f"ps{ab}A{(c // 2) % 4}"
                            P.mm(ps[ab][:, a0:a0 + 64], kt_[:, ti * 128:(ti + 1) * 128], qt_[:, t0:t0 + 64],
                                 r=[f"{kn}_{c // 16}", f"{qn}_{c // 16}"], w=[an_])
                            mcol = K_TRE if half == 0 else K_TRO
                            P.tt(scm[c % 4][:], ps[ab][:, a0:a0 + 64], cst[:, mcol:mcol + 64], ALU.mult,
                                 r=[an_, "cst"], w=[f"scm{c % 4}"])
                            vv = v_tm[:, ti, hh * 128:(hh + 1) * 128]
                            db, d0 = 3 + c % 2, ((c // 2) % 2) * 128
                            P.mm(ps[db][:, d0:d0 + 128], ktm[2 * tsl + half][:], vv, r=[f"ktm{2 * tsl + half}", "v_tm"],
                                 w=[f"ps{db}D{(c // 2) % 2}"])

                        def stage_b(c):
                            ti = c // 2
                            t0 = c * 64
                            cb = 5 + (c // 8) % 2
                            j = c % 8
                            vv = v_tm[:, ti, hh * 128:(hh + 1) * 128]
                            P.mm(ps[cb][:, j * 64:(j + 1) * 64], vv, scm[c % 4][:], start=True, stop=(c == 0),
                                 r=["v_tm", f"scm{c % 4}"], w=[f"ps{cb}"])
                            if c > 0:
                                P.mm(ps[cb][:, j * 64:(j + 1) * 64], Sb[2 * p + c % 2][:], qt_[:, t0:t0 + 64], start=False, stop=True,
                                     r=[f"Sb{2 * p + c % 2}", f"{qn}_{c // 16}"], w=[f"ps{cb}"])
                            db, d0 = 3 + c % 2, ((c // 2) % 2) * 128
                            dn = f"ps{db}D{(c // 2) % 2}"
                            ci, ni = 2 * p + c % 2, 2 * p + (c + 1) % 2
                            Scur, Snxt = Sst[ci][:, :], Sst[ni][:, :]
                            if c == 0:
                                P.ts(Snxt, ps[db][:, d0:d0 + 128], chs_[:, 1, c:c + 1], None, ALU.mult,
                                     r=[dn, f"{cn}_{c // 16}"], w=[f"Sst{ni}"])
                            else:
                                P.ts(Snxt, Scur, chs_[:, 0, c:c + 1], None, ALU.mult, r=[f"Sst{ci}", f"{cn}_{c // 16}"], w=[f"Sst{ni}"])
                                P.stt(Snxt, ps[db][:, d0:d0 + 128], chs_[:, 1, c:c + 1], Snxt, ALU.mult, ALU.add,
                                      r=[dn, f"{cn}_{c // 16}", f"Sst{ni}"], w=[f"Sst{ni}"])
                            if c < 31:
                                P.act(Sb[ni][:], Snxt, AF.Copy, r=[f"Sst{ni}", f"{cn}_{(c + 1) // 16}"], w=[f"Sb{ni}"],
                                      scale=chs_[:, 2, c + 1:c + 2])
                            if j == 7:
                                tg = c // 8
                                for o2 in range(2):
                                    P.cp(ost[0][:], ps[cb][:, o2 * 256:(o2 + 1) * 256], r=[f"ps{cb}"], w=["ost0"], eng="act")
                                    P.dma("sp", olocd[h][:, tg * 512 + o2 * 256:tg * 512 + (o2 + 1) * 256], ost[0][:], "ost0",
                                          r=["ost0"], w=[f"olocd{h}"])

                        LA = 2
                        for c in range(LA):
                            stage_a(c)
                        for c in range(32):
                            if c + LA < 32:
                                stage_a(c + LA)
                            stage_b(c)
                            yield
                    return gen()

                def rr(a, b):
                    while a is not None or b is not None:
                        if a is not None:
                            try:
                                next(a)
                            except StopIteration:
                                a = None
                        if b is not None:
                            try:
                                next(b)
                            except StopIteration:
                                b = None

                rr(s1_task(hg * 4, 0), None)
                for hh in range(4):
                    h = hg * 4 + hh
                    rr(s2_task(h, hh), s1_task(h + 1, hh + 1) if hh < 3 else None)
            P.flush()

        CTall = sb("CTall", [128, 2, NT], BF16, mid)
        with ExitStack() as ph:
            wb = [sb(f"wb{i}", [128, 16, 512], BF16, ph) for i in range(2)]
            wdt = sb("wdt", [128, 16, 16], BF16, ph)
            ub = sb("ub", [128, NT + 3], F32, ph)
            xTp = sb("xTp", [128, NT], F32, ph)
            BT = sb("BT", [128, NT], BF16, ph)
            cbT = sb("cbT", [128, NT], F32, ph)
            Btm = sb("Btm", [128, 16, 128], BF16, ph)
            stA = sb("stA", [96, NT], F32, ph)
            stB = sb("stB", [96, NT], F32, ph)
            tmA = sb("tmA", [128, 16, 96], F32, ph)
            aneg = sb("aneg", [48, 2], F32, ph)
            abc = [[sb(f"abc{a}{b}", [128, 512], F32, ph) for b in range(2)] for a in range(2)]
            cht = [[sb(f"cht{a}{b}", [128, 512], BF16, ph) for b in range(2)] for a in range(2)]
            xdA = [sb(f"xdA{i}", [128, 128], BF16, ph) for i in range(4)]
            xdB = [sb(f"xdB{i}", [128, 128], BF16, ph) for i in range(4)]
            xw = [sb(f"xw{i}", [128, 128], BF16, ph) for i in range(4)]
            Dm = [sb(f"Dm{i}", [128, 128], F32, ph) for i in range(2)]
            Mt = [sb(f"Mt{i}", [128, 128], BF16, ph) for i in range(8)]
            prA = [sb(f"prA{i}", [128, 128], BF16, ph) for i in range(4)]
            prB = [sb(f"prB{i}", [128, 128], BF16, ph) for i in range(4)]
            Spp = [sb(f"Spp{i}", [128, 128], F32, ph) for i in range(4)]
            yst = [sb(f"yst{i}", [128, 512], F32, ph) for i in range(2)]
            ebt = sb("ebt", [128, 512], F32, ph)
            dcs = sb("dcs", [128, 2, 16], F32, ph)

            for i in range(4):
                P.memset(xdA[i][:], 0.0, w=[f"xdA{i}"], eng="pool")
                P.memset(xdB[i][:], 0.0, w=[f"xdB{i}"], eng="pool")
            for i in range(4):
                P.memset(prA[i][:], 0.0, w=[f"prA{i}"], eng="pool")
                P.memset(prB[i][:], 0.0, w=[f"prB{i}"], eng="pool")
            P.memset(stA[:], 0.0, w=["stA"])
            P.memset(stB[:], 0.0, w=["stB"])
            P.dma("pool", wdt[:], w_in_v[:, :, 6656:6672], "wdt", w=["wdt"])
            P.act(aneg[32:48, 0:1], pp[32:48, C_ALOG:C_ALOG + 1], AF.Exp, r=["pp"], w=["aneg"])
            P.ts(aneg[32:48, 1:2], aneg[32:48, 0:1], -1.0, None, ALU.mult, r=["aneg"], w=["aneg"])
            for tg in range(4):
                bk = tg % 2
                for kc in range(16):
                    P.mm(ps[bk][0:16, :], wdt[:, kc, :], hT[:, kc, tg * 512:(tg + 1) * 512],
                         start=(kc == 0), stop=(kc == 15), r=["wdt", "hT"], w=[f"ps{bk}"])
                P.act(stA[32:48, tg * 512:(tg + 1) * 512], ps[bk][0:16, :], AF.Exp, r=[f"ps{bk}", "pp"], w=["stA"],
                      bias=pp[32:48, C_DTB:C_DTB + 1])
            P.act(stA[32:48, :], stA[32:48, :], AF.Ln, r=["stA"], w=["stA"], bias=1.0)
            P.cp(stB[64:80, :], stA[32:48, :], r=["stA"], w=["stB"], eng="act")
            P.ts(stB[32:48, :], stA[32:48, :], aneg[32:48, 1:2], None, ALU.mult, r=["stA", "aneg"], w=["stB"])
            P.scan(stB[0:16, :], cst[32:48, K_ONE:K_ONE + 1].broadcast_to([16, NT]), stB[32:48, :], 0.0, ALU.mult, ALU.add,
                   r=["stB", "cst"], w=["stB"])
            g3 = stB[0:16, :].rearrange("p (c j) -> p c j", j=128)
            a3 = stA[0:16, :].rearrange("p (c j) -> p c j", j=128)
            w3 = stA[64:80, :].rearrange("p (c j) -> p c j", j=128)
            P.cp(stA[0:16, 0:128], stB[0:16, 0:128], r=["stB"], w=["stA"], eng="act")
            P.tt(a3[:, 1:16, :], g3[:, 1:16, :], bc_last(g3[:, 0:15, 127], 128), ALU.subtract, r=["stB"], w=["stA"])
            P.tt(w3, bc_last(a3[:, :, 127], 128), a3, ALU.subtract, r=["stA"], w=["stA"])
            P.act(stA[64:80, :], stA[64:80, :], AF.Exp, r=["stA"], w=["stA"])
            P.tt(stA[64:80, :], stA[64:80, :], stB[64:80, :], ALU.mult, r=["stA", "stB"], w=["stA"])
            P.dma("sp", acsd, stA[0:16, :], "acsd", r=["stA"], w=["acsd"])
            P.dma("sp", gsd, stB[0:16, :], "gsd", r=["stB"], w=["gsd"])
            P.dma("sp", gl, stB[0:16, NT - 1:NT], "gl", r=["stB"], w=["gl"])
            for c in range(16):
                bk = 2 + c % 2
                P.tr(ps[bk][:, 0:96], stA[:, c * 128:(c + 1) * 128], cst[0:96, K_ID:K_ID + 96], r=["stA", "cst"], w=[f"ps{bk}"])
                P.cp(tmA[:, c, :], ps[bk][:, 0:96], r=[f"ps{bk}"], w=["tmA"], eng=("act" if c % 2 else "dve"))

            pbank = [0]

            def conv_chunk(s, cols, jc, dest, dname):
                for tg in range(4):
                    bk = pbank[0] % 2
                    pbank[0] += 1
                    for kc in range(16):
                        P.mm(ps[bk][:, :], wb[s][:, kc, cols:cols + 128], hT[:, kc, tg * 512:(tg + 1) * 512],
                             start=(kc == 0), stop=(kc == 15), r=[f"wb{s}", "hT"], w=[f"ps{bk}"])
                    P.cp(ub[:, 3 + tg * 512:3 + (tg + 1) * 512], ps[bk][:, :], r=[f"ps{bk}"], w=["ub"],
                         eng=("act" if tg % 2 else "dve"))
                bk = pbank[0] % 2
                pbank[0] += 1
                for kc in range(16):
                    P.mm(ps[bk][:, 0:4], wb[s][:, kc, cols:cols + 128], hTh[:, kc, :],
                         start=(kc == 0), stop=(kc == 15), r=[f"wb{s}", "hTh"], w=[f"ps{bk}"])
                P.cp(ub[:, 0:3], ps[bk][:, 1:4], r=[f"ps{bk}"], w=["ub"])
                wc = lambda k: pp[:, C_CONVW + k * 12 + jc:C_CONVW + k * 12 + jc + 1]
                P.ts(xTp[:], ub[:, 0:NT], wc(0), pp[:, C_CONVB + jc:C_CONVB + jc + 1], ALU.mult, ALU.add,
                     r=["ub", "pp"], w=["xTp"])
                for k in range(1, 4):
                    P.stt(xTp[:], ub[:, k:k + NT], wc(k), xTp[:], ALU.mult, ALU.add, r=["ub", "pp", "xTp"], w=["xTp"])
                P.act(dest, xTp[:], AF.Silu, r=["xTp"], w=[dname])

            P.dma("pool", wb[1][:], w_in_v[:, :, 6144:6656], "wb1", w=["wb1"])
            for g in range(2):
                P.dma("pool", wb[0][:], w_in_v[:, :, 5120 + g * 512:5120 + (g + 1) * 512], "wb0", w=["wb0"])
                conv_chunk(1, g * 128, 8 + g, BT[:], "BT")
                conv_chunk(1, 256 + g * 128, 10 + g, CTall[:, g, :], "CT")
                for c4 in range(4):
                    bk = 2 + c4 % 2
                    for cc in range(4):
                        c = c4 * 4 + cc
                        P.mm(ps[bk][:, cc * 128:(cc + 1) * 128], BT[:, c * 128:(c + 1) * 128],
                             CTall[:, g, c * 128:(c + 1) * 128], r=["BT", "CT"], w=[f"ps{bk}"], inc=(cc == 3))
                    P.cp(cbT[:, c4 * 512:(c4 + 1) * 512], ps[bk][:, :], r=[f"ps{bk}"], w=["cbT"],
                         eng=("act" if c4 % 2 else "dve"))
                for c8 in range(2):
                    for cc in range(8):
                        c = c8 * 8 + cc
                        P.tr(psb[:, cc * 128:(cc + 1) * 128], BT[:, c * 128:(c + 1) * 128], idb[:], r=["BT", "idb"],
                             w=["ps7b0", "ps7b1"], inc=(cc == 7))
                    P.cp(Btm[:, c8 * 8:(c8 + 1) * 8, :], psb[:, :].rearrange("p (c n) -> p c n", n=128),
                         r=["ps7b0", "ps7b1"], w=["Btm"], eng="act")
                for jp in range(4):
                    j = 4 * g + jp
                    conv_chunk(0, jp * 128, j, xTp[:], "xTp")
                    h0, h1 = 2 * j, 2 * j + 1
                    jpar = j % 2

                    def stage_a(c, j=j, g=g):
                        cs = slice(c * 128, (c + 1) * 128)
                        q4, c4i = c // 4, c % 4
                        sl = q4 % 2
                        if c4i == 0:
                            for hh in range(2):
                                h = 2 * j + hh
                                src = bass.AP(acsd.tensor, h * NT + q4 * 512, [[0, 128], [1, 512]])
                                P.dma("sp", abc[hh][sl][:], src, f"abc{hh}{sl}", r=["acsd"], w=[f"abc{hh}{sl}"])
                                P.act(ebt[:], abc[hh][sl][:], AF.Exp, r=[f"abc{hh}{sl}"], w=["ebt"])
                                P.tt(cht[hh][sl][:], CTall[:, g, q4 * 512:(q4 + 1) * 512], ebt[:], ALU.mult,
                                     r=["CT", "ebt"], w=[f"cht{hh}{sl}"])
                                P.cp(dcs[:, hh, q4 * 4:(q4 + 1) * 4], ebt[:].rearrange("p (c j) -> p c j", j=128)[:, :, 127],
                                     r=["ebt"], w=["dcs"])
                        k2, k4 = c % 2, c % 4
                        bk = 2 + k2
                        P.tr(ps[bk][:, 0:128], xTp[:, cs], ident, r=["xTp", "cst"], w=[f"ps{bk}"])
                        P.ts(xdA[k4][:, 0:64], ps[bk][:, 0:64], tmA[:, c, 32 + h0:33 + h0], None, ALU.mult,
                             r=[f"ps{bk}", "tmA"], w=[f"xdA{k4}"])
                        P.act(xdB[k4][:, 64:128], ps[bk][:, 64:128], AF.Copy, r=[f"ps{bk}", "tmA"], w=[f"xdB{k4}"],
                              scale=tmA[:, c, 32 + h1:33 + h1])
                        P.act(xw[k4][:, 0:64], ps[bk][:, 0:64], AF.Copy, r=[f"ps{bk}", "tmA"], w=[f"xw{k4}"],
                              scale=tmA[:, c, 64 + h0:65 + h0])
                        P.ts(xw[k4][:, 64:128], ps[bk][:, 64:128], tmA[:, c, 64 + h1:65 + h1], None, ALU.mult,
                             r=[f"ps{bk}", "tmA"], w=[f"xw{k4}"])
                        for hh in range(2):
                            h = 2 * j + hh
                            asl = abc[hh][sl][:, c4i * 128:(c4i + 1) * 128]
                            P.stt(Dm[hh][:], asl, tmA[:, c, h:h + 1], cst[:, K_NEG:K_NEG + 128], ALU.subtract, ALU.add,
                                  r=[f"abc{hh}{sl}", "tmA", "cst"], w=[f"Dm{hh}"])
                            P.act(Dm[hh][:], Dm[hh][:], AF.Exp, r=[f"Dm{hh}"], w=[f"Dm{hh}"])
                            P.tt(Mt[2 * k4 + hh][:], cbT[:, cs], Dm[hh][:], ALU.mult, r=["cbT", f"Dm{hh}"],
                                 w=[f"Mt{2 * k4 + hh}"])
                        P.mm(ps[4][:, k4 * 128:(k4 + 1) * 128], Btm[:, c, :], xw[k4][:], r=["Btm", f"xw{k4}"], w=[f"ps4D{k4}"])

                    def stage_b(c, j=j, g=g):
                        q4, c4i = c // 4, c % 4
                        sl = q4 % 2
                        k2, k4 = c % 2, c % 4
                        yb = 5 + q4 % 2
                        yo = ps[yb][:, c4i * 128:(c4i + 1) * 128]
                        P.mm(yo, xdA[k4][:], Mt[2 * k4][:], start=True, stop=False, r=[f"xdA{k4}", f"Mt{2 * k4}"], w=[f"ps{yb}"])
                        P.mm(yo, xdB[k4][:], Mt[2 * k4 + 1][:], start=False, stop=(c == 0), r=[f"xdB{k4}", f"Mt{2 * k4 + 1}"],
                             w=[f"ps{yb}"])
                        jpar = j % 2
                        ci, ni = 2 * jpar + c % 2, 2 * jpar + (c + 1) % 2
                        Scur, Snxt = Spp[ci][:, :], Spp[ni][:, :]
                        if c > 0:
                            P.mm(yo, prA[ci][:], cht[0][sl][:, c4i * 128:(c4i + 1) * 128], start=False, stop=False,
                                 r=[f"prA{ci}", f"cht0{sl}"], w=[f"ps{yb}"])
                            P.mm(yo, prB[ci][:], cht[1][sl][:, c4i * 128:(c4i + 1) * 128], start=False, stop=True,
                                 r=[f"prB{ci}", f"cht1{sl}"], w=[f"ps{yb}"])
                        if c == 0:
                            P.cp(Snxt, ps[4][:, k4 * 128:(k4 + 1) * 128], r=[f"ps4D{k4}"], w=[f"Spp{ni}"])
                        else:
                            for hh in range(2):
                                P.ts(Snxt[:, hh * 64:(hh + 1) * 64], Scur[:, hh * 64:(hh + 1) * 64], dcs[:, hh, c:c + 1], None,
                                     ALU.mult, r=[f"Spp{ci}", "dcs"], w=[f"Spp{ni}"])
                            P.tt(Snxt, Snxt, ps[4][:, k4 * 128:(k4 + 1) * 128], ALU.add, r=[f"Spp{ni}", f"ps4D{k4}"], w=[f"Spp{ni}"])
                        if c < 15:
                            P.cp(prA[ni][:, 0:64], Snxt[:, 0:64], r=[f"Spp{ni}"], w=[f"prA{ni}"], eng="act")
                            P.cp(prB[ni][:, 64:128], Snxt[:, 64:128], r=[f"Spp{ni}"], w=[f"prB{ni}"], eng="act")
                        if c4i == 3:
                            tsl = slice(q4 * 512, (q4 + 1) * 512)
                            P.stt(yst[q4 % 2][:], xTp[:, tsl], pp[:, C_DSK + j:C_DSK + j + 1], ps[yb][:, :], ALU.mult, ALU.add,
                                  r=["xTp", "pp", f"ps{yb}"], w=[f"yst{q4 % 2}"])
                            P.dma("sp", olocd[8 + j][:, tsl], yst[q4 % 2][:], f"yst{q4 % 2}", r=[f"yst{q4 % 2}"],
                                  w=[f"olocd{8 + j}"])

                    LA = 2
                    for c in range(LA):
                        stage_a(c)
                    for c in range(16):
                        if c + LA < 16:
                            stage_a(c + LA)
                        stage_b(c)
            P.flush()

        mixT = sb("mixT", [128, 16, NT], BF16, mid)
        with ExitStack() as ph:
            wb1_ = sb("wb0", [128, 16, 512], BF16, ph)
            wb = [wb1_, wb1_]
            gs = [sb(f"gs{i}", [128, 512], F32, ph) for i in range(2)]
            ol = [sb(f"ol{i}", [128, 512], F32, ph) for i in range(2)]
            ql = [sb(f"ql{i}", [128, 512], BF16, ph) for i in range(2)]
            sq = [sb(f"sq{i}", [128, 512], F32, ph) for i in range(2)]
            rs = [sb(f"rs{i}", [128, 512], F32, ph) for i in range(2)]
            def run_tasks3(tasks, width=2):
                pending = list(tasks)
                active = [None] * width
                while pending or any(a is not None for a in active):
                    for s_ in range(width):
                        if active[s_] is None and pending:
                            active[s_] = pending.pop(0)(s_)
                        if active[s_] is not None:
                            try:
                                next(active[s_])
                            except StopIteration:
                                active[s_] = None

            def hg_item(h, hh, tg):
                def gen(k):
                    tsl = slice(tg * 512, (tg + 1) * 512)
                    bk = k
                    P.dma("sp", ol[k][:], olocd[h][:, tsl], f"ol{k}", r=[f"olocd{h}"], w=[f"ol{k}"])
                    P.dma("sp", ql[k][:], qhd[h][:, tsl], f"ql{k}", r=[f"qhd{h}"], w=[f"ql{k}"])
                    for kc in range(16):
                        P.mm(ps[bk][:, :], wb[0][:, kc, hh * 128:(hh + 1) * 128], hT[:, kc, tsl],
                             start=(kc == 0), stop=(kc == 15), r=["wb0", "hT"], w=[f"ps{bk}"])
                    yield
                    P.act(gs[k][:], ps[bk][:, :], AF.Silu, r=[f"ps{bk}"], w=[f"gs{k}"])
                    P.mm(ps[2 + k][:, :], Sinb[:, h, :], ql[k][:], r=["Sinb", f"ql{k}"], w=[f"ps{2 + k}"])
                    yield
                    P.tt(ol[k][:], ps[2 + k][:, :], ol[k][:], ALU.add, r=[f"ps{2 + k}", f"ol{k}"], w=[f"ol{k}"])
                    yield
                    P.act(sq[k][:], ol[k][:], AF.Square, r=[f"ol{k}"], w=[f"sq{k}"])
                    yield
                    P.mm(ps[4 + k][:, :], ones, sq[k][:], r=["cst", f"sq{k}"], w=[f"ps{4 + k}"])
                    yield
                    P.ts(rs[k][:], ps[4 + k][:, :], 1.0 / 128, EPS, ALU.mult, ALU.add, r=[f"ps{4 + k}"], w=[f"rs{k}"])
                    yield
                    P.act(rs[k][:], rs[k][:], AF.Sqrt, r=[f"rs{k}"], w=[f"rs{k}"])
                    yield
                    P.recip(rs[k][:], rs[k][:], r=[f"rs{k}"], w=[f"rs{k}"])
                    yield
                    P.tt(ol[k][:], ol[k][:], rs[k][:], ALU.mult, r=[f"ol{k}", f"rs{k}"], w=[f"ol{k}"])
                    yield
                    P.stt(mixT[:, h, tsl], ol[k][:], pp[:, C_HNW + h:C_HNW + h + 1], gs[k][:], ALU.mult, ALU.mult,
                          r=[f"ol{k}", "pp", f"gs{k}"], w=["mixT"])
                    yield
                return gen

            for hg in range(2):
                P.dma("pool", wb[0][:], w_in_v[:, :, 3072 + hg * 512:3072 + (hg + 1) * 512], "wb0", w=["wb0"])
                run_tasks3([hg_item(hg * 4 + hh, hh, tg) for hh in range(4) for tg in range(4)])

            zs = gs
            gb = [[sb(f"gb{a}{i}", [128, 512], F32, ph) for i in range(2)] for a in range(2)]
            ch3 = [sb(f"ch3{i}", [128, 512], BF16, ph) for i in range(4)]
            yz1_ = sb("yz", [128, 4, 512], F32, ph)

            def ssd_item(g, tg, jp, yk):
                j = 4 * g + jp

                def gen(k):
                    tsl = slice(tg * 512, (tg + 1) * 512)
                    bk = k
                    P.dma("sp", ol[k][:], olocd[8 + j][:, tsl], f"ol{k}", r=[f"olocd{8 + j}"], w=[f"ol{k}"])
                    for hh in range(2):
                        h = 2 * j + hh
                        src = bass.AP(gsd.tensor, h * NT + tg * 512, [[0, 128], [1, 512]])
                        P.dma("sp", gb[k][hh][:], src, f"gb{k}{hh}", r=["gsd"], w=[f"gb{k}{hh}"])
                    for kc in range(16):
                        P.mm(ps[bk][:, :], wb[0][:, kc, jp * 128:(jp + 1) * 128], hT[:, kc, tsl],
                             start=(kc == 0), stop=(kc == 15), r=["wb0", "hT"], w=[f"ps{bk}"])
                    yield
                    P.act(zs[k][:], ps[bk][:, :], AF.Silu, r=[f"ps{bk}"], w=[f"gs{k}"])
                    yield
                    for hh in range(2):
                        P.act(gb[k][hh][:], gb[k][hh][:], AF.Exp, r=[f"gb{k}{hh}"], w=[f"gb{k}{hh}"])
                        P.tt(ch3[2 * k + hh][:], CTall[:, g, tsl], gb[k][hh][:], ALU.mult, r=["CT", f"gb{k}{hh}"],
                             w=[f"ch3{2 * k + hh}"])
                        yield
                    P.mm(ps[2 + k][:, :], SinA[:, j, :], ch3[2 * k][:], start=True, stop=False,
                         r=["SinA", f"ch3{2 * k}"], w=[f"ps{2 + k}"])
                    P.mm(ps[2 + k][:, :], SinB[:, j, :], ch3[2 * k + 1][:], start=False, stop=True,
                         r=["SinB", f"ch3{2 * k + 1}"], w=[f"ps{2 + k}"])
                    yield
                    P.tt(ol[k][:], ps[2 + k][:, :], ol[k][:], ALU.add, r=[f"ps{2 + k}", f"ol{k}"], w=[f"ol{k}"])
                    yield
                    P.tt(yz1_[:, jp, :], ol[k][:], zs[k][:], ALU.mult, r=[f"ol{k}", f"gs{k}"], w=[f"yz{jp}"])
                    yield
                    P.act(sq[k][:], yz1_[:, jp, :], AF.Square, r=[f"yz{jp}"], w=[f"sq{k}"])
                    yield
                    P.mm(ps[4 + yk][:, :], ones, sq[k][:], start=(jp == 0), stop=(jp == 3), r=["cst", f"sq{k}"],
                         w=[f"ps{4 + yk}"])
                    yield
                return gen

            for g in range(2):
                P.dma("pool", wb[0][:], w_in_v[:, :, 4096 + g * 512:4096 + (g + 1) * 512], "wb0", w=["wb0"])
                for tg in range(4):
                    tsl = slice(tg * 512, (tg + 1) * 512)
                    yk = tg % 2
                    run_tasks3([ssd_item(g, tg, jp, yk) for jp in range(4)])
                    P.ts(rs[yk][:], ps[4 + yk][:, :], 1.0 / 512, EPS, ALU.mult, ALU.add, r=[f"ps{4 + yk}"], w=[f"rs{yk}"])
                    P.act(rs[yk][:], rs[yk][:], AF.Sqrt, r=[f"rs{yk}"], w=[f"rs{yk}"])
                    P.recip(rs[yk][:], rs[yk][:], r=[f"rs{yk}"], w=[f"rs{yk}"])
                    for jp in range(4):
                        j = 4 * g + jp
                        P.stt(mixT[:, 8 + j, tsl], yz1_[:, jp, :], pp[:, C_SNW + j:C_SNW + j + 1], rs[yk][:], ALU.mult, ALU.mult,
                              r=[f"yz{jp}", "pp", f"rs{yk}"], w=["mixT"])
            P.flush()

        if dbg == "mixT":
            with ExitStack() as ph:
                stg = [sb(f"stg{i}", [128, NT], F32, ph) for i in range(2)]
                for j in range(16):
                    P.cp(stg[j % 2][:], mixT[:, j, :], r=["mixT"], w=[f"stg{j % 2}"])
                    P.dma("sp", dbg_d[:, j, :], stg[j % 2][:], f"stg{j % 2}", r=[f"stg{j % 2}"], w=["dbgd"])
                P.flush()

        with ExitStack() as ph:
            pw0 = sb("pw0", [128, D], F32, ph)
            xt = [sb(f"xr{j}", [128, D], F32, ph) for j in range(2)]
            x1t = [sb(f"x1t{j}", [128, D], F32, ph) for j in range(2)]
            junk = sb("junk", [128, 512], BF16, ph)
            st = sb("st4", [128, 16], F32, ph)
            wo = hT
            w_out_v = w_out.rearrange("(k p) n -> p k n", p=128)
            P.dma("sp", pw0[:], pw_d[:, 0, :], "pw0", w=["pw0"])
            for b4 in range(4):
                P.dma("pool", wo[:, :, b4 * 512:(b4 + 1) * 512], w_out_v[:, :, b4 * 512:(b4 + 1) * 512], f"wo{b4}",
                      r=[], w=["hT"])
            for ti in range(16):
                k2 = ti % 2
                xb = f"xr{k2}"
                P.dma("sp", xt[k2][:], xin[128 + ti * 128:128 + (ti + 1) * 128, :], xb, w=[xb])
                for b4 in range(4):
                    bk = 4 * k2 + b4
                    bname = f"ps{bk}q"
                    bank = ps[bk]
                    for kc in range(16):
                        P.mm(bank[:, :], mixT[:, kc, ti * 128:(ti + 1) * 128], wo[:, kc, b4 * 512:(b4 + 1) * 512],
                             start=(kc == 0), stop=(kc == 15), r=["mixT", "hT"], w=[bname])
                    P.act(junk[:], bank[:, :], AF.Square, r=[bname], w=["junk4", f"st4{k2}"],
                          accum_out=st[:, 8 * k2 + b4:8 * k2 + b4 + 1])
                sv = st[:, 8 * k2:8 * k2 + 8]
                P.tt(sv[:, 4:5], sv[:, 0:1], sv[:, 1:2], ALU.add, r=[f"st4{k2}"], w=[f"st4{k2}"])
                P.tt(sv[:, 5:6], sv[:, 2:3], sv[:, 3:4], ALU.add, r=[f"st4{k2}"], w=[f"st4{k2}"])
                P.tt(sv[:, 4:5], sv[:, 4:5], sv[:, 5:6], ALU.add, r=[f"st4{k2}"], w=[f"st4{k2}"])
                P.ts(sv[:, 5:6], sv[:, 4:5], 1.0 / D, EPS, ALU.mult, ALU.add, r=[f"st4{k2}"], w=[f"st4{k2}"])
                P.act(sv[:, 6:7], sv[:, 5:6], AF.Sqrt, r=[f"st4{k2}"], w=[f"st4{k2}"])
                P.recip(sv[:, 7:8], sv[:, 6:7], r=[f"st4{k2}"], w=[f"st4{k2}"])
                for b4 in range(4):
                    bk = 4 * k2 + b4
                    bname = f"ps{bk}q"
                    bank = ps[bk]
                    csl = slice(b4 * 512, (b4 + 1) * 512)
                    P.stt(x1t[k2][:, csl], bank[:, :], sv[:, 7:8], pw0[:, csl], ALU.mult, ALU.mult,
                          r=[bname, f"st4{k2}", "pw0"], w=[f"x1t{k2}"])
                P.tt(x1t[k2][:], x1t[k2][:], xt[k2][:], ALU.add, r=[f"x1t{k2}", xb], w=[f"x1t{k2}"], eng="pool")
                P.dma("sp", x1d[ti * 128:(ti + 1) * 128, :], x1t[k2][:], f"x1t{k2}", r=[f"x1t{k2}"], w=["x1d"])
            P.flush()
        mid.close()
        sinst.close()

        with ExitStack() as ph:
            TG = 512
            pw1 = sb("pw1", [128, D], F32, ph)
            h2s = [sb(f"h2{i}", [128, 16, TG], BF16, ph) for i in range(2)]
            hid = sb("hid", [128, 44, TG], BF16, ph)
            wr = sb("wr", [128, 4, 16 * 512], BF16, ph)
            ff = [sb(f"ff{i}", [128, D], F32, ph) for i in range(4)]
            xt = [sb(f"xq{j}", [128, D], F32, ph) for j in range(2)]
            xa = xt
            sg = [sb(f"sg{j}", [128, TG], F32, ph) for j in range(2)]
            junk = sb("junk5", [128, D], BF16, ph)
            pjunk = [junk[:, 0:1024], junk[:, 1024:2048]]
            st = sb("st5", [128, 16], F32, ph)
            w_gate_v = w_gate.rearrange("(k p) n -> p k n", p=128)
            w_up_v = w_up.rearrange("(k p) n -> p k n", p=128)
            w_down_v = w_down.rearrange("(f p) n -> p f n", p=128)
            P.dma("sp", pw1[:], pw_d[:, 1, :], "pw1", w=["pw1"])

            def wslot(i):
                return wr[:, i, :].rearrange("p (k n) -> p k n", n=512)

            def dslot(i):
                return wr[:, 2 * i:2 * i + 2, :].rearrange("p a b -> p (a b)")[:, 0:44 * 256].rearrange("p (f n) -> p f n", n=256)

            for tgi in range(NT // TG):
                h2 = h2s[tgi % 2]
                h2n = f"h2{tgi % 2}"
                for rnd in range(2):
                    for jj in range(2):
                        j4 = rnd * 2 + jj
                        ti = tgi * 4 + j4
                        xs = xa[jj]
                        xb = f"xq{jj}"
                        P.dma("sp", xs[:], x1d[ti * 128:(ti + 1) * 128, :], xb, r=["x1d"], w=[xb])
                        ss = st[:, 4 * j4:4 * j4 + 1]
                        for hf in range(2):
                            P.act(pjunk[hf], xs[:, hf * 1024:(hf + 1) * 1024], AF.Square, r=[xb], w=["junk5", f"st5{j4}"],
                                  accum_out=st[:, 4 * j4 + 1 + hf:4 * j4 + 2 + hf])
                        P.tt(ss, st[:, 4 * j4 + 1:4 * j4 + 2], st[:, 4 * j4 + 2:4 * j4 + 3], ALU.add, r=[f"st5{j4}"], w=[f"st5{j4}"])
                        P.ts(st[:, 4 * j4 + 1:4 * j4 + 2], ss, 1.0 / D, EPS, ALU.mult, ALU.add, r=[f"st5{j4}"], w=[f"st5{j4}"])
                        P.act(st[:, 4 * j4 + 2:4 * j4 + 3], st[:, 4 * j4 + 1:4 * j4 + 2], AF.Sqrt, r=[f"st5{j4}"], w=[f"st5{j4}"])
                        P.recip(st[:, 4 * j4 + 3:4 * j4 + 4], st[:, 4 * j4 + 2:4 * j4 + 3], r=[f"st5{j4}"], w=[f"st5{j4}"])
                        P.act(xs[:], xs[:], AF.Copy, r=[xb, f"st5{j4}"], w=[xb], scale=st[:, 4 * j4 + 3:4 * j4 + 4])
                    for k2_ in range(8):
                        bk = k2_ % 4
                        for kk2 in range(2):
                            kc = 2 * k2_ + kk2
                            for jj in range(2):
                                P.tr(ps[bk][:, (kk2 * 2 + jj) * 128:(kk2 * 2 + jj + 1) * 128], xa[jj][:, kc * 128:(kc + 1) * 128], ident,
                                     r=[f"xq{jj}", "cst"], w=[f"ps{bk}"], inc=(kk2 == 1 and jj == 1))
                        for kk2 in range(2):
                            kc = 2 * k2_ + kk2
                            wcol = pp[:, C_PREFFN + kc:C_PREFFN + kc + 1]
                            o_ap = h2[:, kc, rnd * 256:(rnd + 1) * 256]
                            i_ap = ps[bk][:, kk2 * 256:(kk2 + 1) * 256]
                            if kk2 == 0:
                                P.ts(o_ap, i_ap, wcol, None, ALU.mult, r=[f"ps{bk}", "pp"], w=[h2n])
                            else:
                                P.act(o_ap, i_ap, AF.Copy, r=[f"ps{bk}", "pp"], w=[h2n], scale=wcol)
                for blk in range(11):
                    gsl, usl = (blk % 2) * 2, (blk % 2) * 2 + 1
                    P.dma("pool", wslot(gsl), w_gate_v[:, :, blk * 512:(blk + 1) * 512], f"wr{gsl}", w=[f"wr{gsl}"])
                    P.dma("pool", wslot(usl), w_up_v[:, :, blk * 512:(blk + 1) * 512], f"wr{usl}", w=[f"wr{usl}"])
                    for f4 in range(4):
                        fc = blk * 4 + f4
                        k2 = fc % 2
                        ga, ua = ps[k2], ps[2 + k2]
                        for kc in range(16):
                            P.mm(ga[:, :], wslot(gsl)[:, kc, f4 * 128:(f4 + 1) * 128], h2[:, kc, :],
                                 start=(kc == 0), stop=(kc == 15), r=[f"wr{gsl}", h2n], w=[f"ps{k2}"])
                        for kc in range(16):
                            P.mm(ua[:, :], wslot(usl)[:, kc, f4 * 128:(f4 + 1) * 128], h2[:, kc, :],
                                 start=(kc == 0), stop=(kc == 15), r=[f"wr{usl}", h2n], w=[f"ps{2 + k2}"])
                        P.act(sg[k2][:], ga[:, :], AF.Silu, r=[f"ps{k2}"], w=[f"sg{k2}"])
                        P.tt(hid[:, fc, :], sg[k2][:], ua[:, :], ALU.mult, r=[f"sg{k2}", f"ps{2 + k2}"], w=["hid"])
                for db in range(8):
                    ds_ = db % 2
                    P.dma("pool", dslot(ds_), w_down_v[:, :, db * 256:(db + 1) * 256], f"wd{ds_}",
                          w=[f"wr{2 * ds_}", f"wr{2 * ds_ + 1}"])
                    for j4 in range(4):
                        bk = 4 + (db * 4 + j4) % 2
                        for fc in range(44):
                            P.mm(ps[bk][:, 0:256], hid[:, fc, j4 * 128:(j4 + 1) * 128], dslot(ds_)[:, fc, :],
                                 start=(fc == 0), stop=(fc == 43), r=["hid", f"wr{2 * ds_}", f"wr{2 * ds_ + 1}"], w=[f"ps{bk}"])
                        P.cp(ff[j4][:, db * 256:(db + 1) * 256], ps[bk][:, 0:256], r=[f"ps{bk}"], w=[f"ff{j4}"],
                             eng=("act" if j4 % 2 else "dve"))
                for j4 in range(4):
                    ti = tgi * 4 + j4
                    k2 = j4 % 2
                    xb = f"xq{k2}"
                    P.dma("sp", xt[k2][:], x1d[ti * 128:(ti + 1) * 128, :], xb, r=["x1d"], w=[xb])
                    ss = st[:, 4 * j4:4 * j4 + 1]
                    for hf in range(2):
                        P.act(pjunk[hf], ff[j4][:, hf * 1024:(hf + 1) * 1024], AF.Square, r=[f"ff{j4}"], w=["junk5", f"st5{j4}"],
                              accum_out=st[:, 4 * j4 + 1 + hf:4 * j4 + 2 + hf])
                    P.tt(ss, st[:, 4 * j4 + 1:4 * j4 + 2], st[:, 4 * j4 + 2:4 * j4 + 3], ALU.add, r=[f"st5{j4}"], w=[f"st5{j4}"])
                    P.ts(st[:, 4 * j4 + 1:4 * j4 + 2], ss, 1.0 / D, EPS, ALU.mult, ALU.add, r=[f"st5{j4}"], w=[f"st5{j4}"])
                    P.act(st[:, 4 * j4 + 2:4 * j4 + 3], st[:, 4 * j4 + 1:4 * j4 + 2], AF.Sqrt, r=[f"st5{j4}"], w=[f"st5{j4}"])
                    P.recip(st[:, 4 * j4 + 3:4 * j4 + 4], st[:, 4 * j4 + 2:4 * j4 + 3], r=[f"st5{j4}"], w=[f"st5{j4}"])
                    P.stt(ff[j4][:], ff[j4][:], st[:, 4 * j4 + 3:4 * j4 + 4], pw1[:], ALU.mult, ALU.mult,
                          r=[f"ff{j4}", f"st5{j4}", "pw1"], w=[f"ff{j4}"])
                    P.tt(xt[k2][:], xt[k2][:], ff[j4][:], ALU.add, r=[xb, f"ff{j4}"], w=[xb], eng="pool")
                    P.dma("sp", y[ti * 128:(ti + 1) * 128, :], xt[k2][:], xb, r=[xb], w=["y"])
            P.flush()
    return nc


def make_inputs(inputs):
    x = np.asarray(inputs["x"], dtype=np.float32)
    g = lambda k: np.asarray(inputs[k], dtype=np.float32)
    pp = np.zeros((128, PPW), np.float32)
    pm = lambda v, n: np.ascontiguousarray(v.reshape(n, 128).T)
    pp[:, C_PREMIX:C_PREMIX + 16] = pm(g("pre_mix_norm_w")[0], 16)
    pp[:, C_PREFFN:C_PREFFN + 16] = pm(g("pre_ffn_norm_w")[0], 16)
    pp[:, C_L0:C_L0 + 8] = pm(g("lb_logits")[0], 8)
    pp[:, C_L1:C_L1 + 8] = pm(g("lb_logits")[1], 8)
    pp[:, C_HNW:C_HNW + 8] = pm(g("hgrn_norm_w")[0], 8)
    cw = g("conv_w")[0]
    for k in range(4):
        pp[:, C_CONVW + k * 12:C_CONVW + (k + 1) * 12] = pm(cw[k], 12)
    pp[:, C_CONVB:C_CONVB + 12] = pm(g("conv_b")[0], 12)
    pp[:, C_SNW:C_SNW + 8] = pm(g("ssd_norm_w")[0], 8)
    pp[:, C_DSK:C_DSK + 8] = pm(np.repeat(g("d_skip")[0], 64), 8)
    pp[32:48, C_DTB] = g("dt_bias")[0]
    pp[32:48, C_ALOG] = g("a_log")[0]
    pp[0:16, C_DTB] = g("dt_bias")[0]
    pp[0:16, C_ALOG] = g("a_log")[0]
    cst = np.zeros((128, CW), np.float32)
    cst[:, K_ID:K_ID + 128] = np.eye(128, dtype=np.float32)
    s = np.arange(128)[:, None]
    t = np.arange(128)[None, :]
    cst[:, K_NEG:K_NEG + 128] = np.where(s <= t, 0.0, -30000.0)
    tri = (np.arange(64)[:, None] <= np.arange(64)[None, :]).astype(np.float32)
    cst[0:64, K_TRE:K_TRE + 64] = tri
    cst[64:128, K_TRO:K_TRO + 64] = tri
    cst[:, K_ONE:K_ONE + 128] = 1.0
    pw = np.zeros((128, 2, D), np.float32)
    pw[:, 0, :] = g("post_mix_norm_w")[0][None, :]
    pw[:, 1, :] = g("post_ffn_norm_w")[0][None, :]
    shared = {"w_in": g("w_in")[0], "w_out": g("w_out")[0], "w_gate": g("w_gate")[0], "w_up": g("w_up")[0],
              "w_down": g("w_down")[0], "cst": cst, "pw": pw}
    maps = []
    for c in range(NCORES):
        b, q = c // 4, c % 4
        xi = np.zeros((NT + 128, D), np.float32)
        xi[128:] = x[b, q * NT:(q + 1) * NT]
        if q > 0:
            xi[:128] = x[b, q * NT - 128:q * NT]
        ppc = pp.copy()
        for j in range(NCORES):
            m = 1.0 if (j // 4 == b and j % 4 < q) else 0.0
            ppc[:, C_M + j] = m
            ppc[:, C_OM + j] = 1.0 - m
        xp = np.zeros((NW * NT, D), np.float32)
        for w in range(NW):
            qq = q - NW + w
            if qq >= 0:
                xp[w * NT:(w + 1) * NT] = x[b, qq * NT:(qq + 1) * NT]
                ppc[:, C_V + w] = 1.0
        d = dict(shared)
        d["xpre"] = xp
        d["xin"] = xi
        d["pp"] = ppc
        maps.append(d)
    return maps


def kernel(**inputs):
    maps = make_inputs(inputs)
    nc = build()
    res = run_bass_kernel_spmd(nc, maps, core_ids=list(range(NCORES)))
    out = np.zeros((2, 4 * NT, D), np.float32)
    for c in range(NCORES):
        out[c // 4, (c % 4) * NT:(c % 4 + 1) * NT] = res.results[c]["y"]
    return out
```

```python
import heapq
import numpy as np
from contextlib import ExitStack
import concourse.bass as bass
import concourse.mybir as mybir
from concourse.bass_utils import run_bass_kernel_spmd

F32 = mybir.dt.float32
BF16 = mybir.dt.bfloat16
ALU = mybir.AluOpType
AF = mybir.ActivationFunctionType

NCORES = 8
D = 2048
NT = 2048
NIN = 6672
DFF = 5632
EPS = 1e-6
HK = 8
XW = 2080
SAME_ENGINE_SYNC = True

C_PREMIX = 0
C_PREFFN = 16
C_L0 = 32
C_L1 = 40
C_HNW = 48
C_CONVW = 56
C_CONVB = 104
C_SNW = 116
C_DSK = 124
C_DTB = 132
C_ALOG = 133
C_M = 134
C_OM = 142
C_V = 150
PPW = 153
NW = 3
HW = 3
K_ID = 0
K_NEG = 128
K_TRE = 256
K_TRO = 320
K_ONE = 384
CW = 512


class Ev:
    __slots__ = ("key", "sem", "val")

    def __init__(self, key, sem, val):
        self.key, self.sem, self.val = key, sem, val


class Buf:
    __slots__ = ("name", "w", "r")

    def __init__(self, name):
        self.name, self.w, self.r = name, None, {}


class Unit:
    __slots__ = ("idx", "eng", "fns", "preds", "opreds", "dma", "dinc", "cost", "succ", "npred", "start", "finish", "ev")

    def __init__(self, idx, eng):
        self.idx, self.eng = idx, eng
        self.fns, self.preds = [], set()
        self.opreds = set()
        self.dma, self.dinc, self.cost = None, 16, 0.0
        self.succ, self.npred = [], 0
        self.start = self.finish = 0.0
        self.ev = None


class Prog:
    ENG = ("pe", "act", "dve", "pool", "sp")
    LAT_X, LAT_S = 450.0, 200.0

    def __init__(self, nc, es):
        self.nc = nc
        self.esem = {e: es.enter_context(nc.semaphore("sem_" + e)) for e in self.ENG}
        self.cnt = {e: 0 for e in self.ENG}
        self.waited = {e: {} for e in self.ENG}
        self.dsem = {}
        self.bufs = {}
        self.es = es
        self.units = []
        self.open_grp = {e: None for e in self.ENG}
        self.bank_last = {}
        self.all_dma_ev = {}
        self.nidx = 0

    def B(self, name):
        b = self.bufs.get(name)
        if b is None:
            b = self.bufs[name] = Buf(name)
        return b

    def _bl(self, x):
        return [self.B(n) if isinstance(n, str) else n for n in x]

    def op(self, eng, fn, r=(), w=(), inc=True, dma=None, dinc=16, cost=200.0):
        r, w = self._bl(r), self._bl(w)
        u = self.open_grp[eng] if dma is None else None
        if u is None:
            u = Unit(self.nidx, eng)
            self.nidx += 1
            self.units.append(u)
        u.fns.append(fn)
        u.cost += cost
        if dma is not None:
            u.dma, u.dinc = dma, dinc
        else:
            self.open_grp[eng] = None if inc else u

        def add(p):
            if p is not None and p is not u:
                u.preds.add(p)

        banks = []
        for b in r + w:
            if b.name.startswith("ps") and b.name[2:3].isdigit() and b.name[2] not in banks:
                banks.append(b.name[2])
        for b in r:
            add(b.w)
        for b in w:
            add(b.w)
            for p in b.r.values():
                add(p)
        for bn in banks:
            for e2, p in self.bank_last.setdefault(bn, {}).items():
                if e2 != eng:
                    add(p)
                elif p is not u:
                    u.opreds.add(p)
            self.bank_last[bn][eng] = u
        for b in r:
            b.r[u.idx] = u
        for b in w:
            b.w = u
            b.r = {}
        return u

    def flush(self):
        nc = self.nc
        units = self.units
        self.units = []
        self.open_grp = {e: None for e in self.ENG}
        inphase = set(id(u) for u in units)
        for u in units:
            u.preds = [p for p in u.preds if id(p) in inphase]
            u.opreds = [p for p in u.opreds if id(p) in inphase and p not in u.preds]
            u.npred = len(u.preds) + len(u.opreds)
            u.succ = []
        for u in units:
            for p in u.preds:
                p.succ.append(u)
            for p in u.opreds:
                p.succ.append(u)
        bl = {}
        for u in reversed(units):
            best = 0.0
            for v in u.succ:
                lat = self.LAT_S if (v.eng == u.eng and u.dma is None and v.dma is None) else self.LAT_X
                t = lat + bl[id(v)]
                if t > best:
                    best = t
            bl[id(u)] = best + u.cost
        eng_t = {e: 0.0 for e in self.ENG}
        avail = {e: [] for e in self.ENG}
        ready = {}
        for u in units:
            if u.npred == 0:
                ready[id(u)] = 0.0
                avail[u.eng].append(u)
        order = {e: [] for e in self.ENG}
        nleft = len(units)
        while nleft:
            best = None
            for e in self.ENG:
                lst = avail[e]
                if not lst:
                    continue
                t_e = eng_t[e]
                cand = None
                for u in lst:
                    s = ready[id(u)]
                    if s < t_e:
                        s = t_e
                    key = (s, -bl[id(u)], u.idx)
                    if cand is None or key < cand[0]:
                        cand = (key, u)
                if best is None or cand[0] < best[0]:
                    best = (cand[0], cand[1], e)
            key, u, e = best
            s = key[0]
            avail[e].remove(u)
            u.start = s
            if u.dma is not None:
                eng_t[e] = s + 60.0
                u.finish = s + u.cost
            else:
                u.finish = s + u.cost
                eng_t[e] = u.finish
            order[e].append(u)
            nleft -= 1
            for v in u.succ:
                lat = self.LAT_S if (v.eng == u.eng and u.dma is None and v.dma is None) else self.LAT_X
                t = u.finish + lat
                if ready.get(id(v), 0.0) < t:
                    ready[id(v)] = t
                v.npred -= 1
                if v.npred == 0:
                    avail[v.eng].append(v)
        for e in self.ENG:
            for u in order[e]:
                if u.dma is not None:
                    d = self.dsem.get(u.dma)
                    if d is None:
                        d = self.dsem[u.dma] = [self.es.enter_context(nc.semaphore("dq_" + u.dma)), 0]
                    d[1] += u.dinc
                    u.ev = Ev(("d", u.dma), d[0], d[1])
                    self.all_dma_ev[u.dma] = u.ev
                else:
                    self.cnt[e] += 1
                    u.ev = Ev(e, self.esem[e], self.cnt[e])
        streams = {}
        for e in self.ENG:
            lst = []
            for u in order[e]:
                need = {}
                for p in u.preds:
                    if p.dma is None and u.dma is None and p.eng == e and (e == "pe" or not SAME_ENGINE_SYNC):
                        continue
                    ev = p.ev
                    cur = need.get(ev.key)
                    if cur is None or cur[1] < ev.val:
                        need[ev.key] = (ev.sem, ev.val)
                waits = []
                for kk_, (s_, v_) in need.items():
                    if self.waited[e].get(kk_, 0) < v_:
                        self.waited[e][kk_] = v_
                        waits.append((s_, v_))
                lst.append((waits, u))
            streams[e] = lst
        tail = []
        for name, ev in self.all_dma_ev.items():
            if self.waited["sp"].get(ev.key, 0) < ev.val:
                self.waited["sp"][ev.key] = ev.val
                tail.append((ev.sem, ev.val))

        def run(eng, lst, extra=()):
            for waits, u in lst:
                for s_, v_ in waits:
                    eng.wait_ge(s_, v_)
                ins = None
                for fn in u.fns:
                    ins = fn(eng)
                if u.dma is not None and u.dinc == 1:
                    ins.then_inc(u.ev.sem)
                else:
                    ins.then_inc(u.ev.sem, u.dinc if u.dma is not None else 1)
            for s_, v_ in extra:
                eng.wait_ge(s_, v_)

        with nc.Block() as block:
            @block.tensor
            def _(e):
                run(e, streams["pe"])

            @block.scalar
            def _(e):
                run(e, streams["act"])

            @block.vector
            def _(e):
                run(e, streams["dve"])

            @block.gpsimd
            def _(e):
                run(e, streams["pool"])

            @block.sync
            def _(e):
                run(e, streams["sp"], tail)

    @staticmethod
    def _n(ap):
        n = 1
        for d in ap.shape[1:]:
            n *= int(d)
        return n

    def mm(self, out, lhsT, rhs, start=True, stop=True, r=(), w=(), inc=None):
        n = self._n(rhs)
        c = 35.0 + n / 2.0 * (4.0 if rhs.dtype == F32 else 1.0)
        return self.op("pe", lambda e: e.matmul(out, lhsT, rhs, start=start, stop=stop),
                       r, w, stop if inc is None else inc, cost=c)

    def tr(self, out, in_, ident, r=(), w=(), inc=True):
        return self.op("pe", lambda e: e.transpose(out, in_, ident), r, w, inc, cost=110.0)

    def act(self, out, in_, func, r=(), w=(), bias=None, scale=None, accum_out=None, eng="act"):
        kw = {}
        if bias is not None:
            kw["bias"] = bias
        if scale is not None:
            kw["scale"] = scale
        if accum_out is not None:
            kw["accum_out"] = accum_out
        c = 220.0 + 0.85 * self._n(in_) + (100.0 if accum_out is not None else 0.0)
        return self.op(eng, lambda e: e.activation(out, in_, func, **kw), r, w, cost=c)

    def _dc(self, out, n_in=1, eng="dve"):
        n = self._n(out)
        if eng == "pool":
            return 300.0 + 2.0 * n
        return 100.0 + (1.05 if n_in == 1 else 2.1) * n * (0.5 if out.dtype == BF16 and n_in == 1 else 1.0)

    def tt(self, out, in0, in1, op, r=(), w=(), eng="dve"):
        return self.op(eng, lambda e: e.tensor_tensor(out, in0, in1, op), r, w, cost=self._dc(out, 2, eng))

    def ts(self, out, in0, s1, s2, op0, op1=None, r=(), w=(), eng="dve"):
        c = self._dc(out, 1, eng)
        if op1 is None:
            return self.op(eng, lambda e: e.tensor_scalar(out, in0, s1, None, op0), r, w, cost=c)
        return self.op(eng, lambda e: e.tensor_scalar(out, in0, s1, s2, op0, op1), r, w, cost=c)

    def stt(self, out, in0, scalar, in1, op0, op1, r=(), w=()):
        return self.op("dve", lambda e: e.scalar_tensor_tensor(out, in0, scalar, in1, op0, op1), r, w,
                       cost=self._dc(out, 2))

    def scan(self, out, d0, d1, init, op0, op1, r=(), w=()):
        return self.op("dve", lambda e: e.tensor_tensor_scan(out, d0, d1, init, op0, op1), r, w,
                       cost=100.0 + 2.1 * self._n(out))

    def cp(self, out, in_, r=(), w=(), eng="dve"):
        if eng == "act":
            return self.op("act", lambda e: e.activation(out, in_, AF.Copy), r, w, cost=220.0 + 0.85 * self._n(out))
        return self.op(eng, lambda e: e.tensor_copy(out, in_), r, w, cost=self._dc(out, 1, eng))

    def recip(self, out, in_, r=(), w=()):
        return self.op("dve", lambda e: e.reciprocal(out, in_), r, w, cost=self._dc(out, 1))

    def memset(self, ap, val, w=(), eng="dve"):
        return self.op(eng, lambda e: e.memset(ap, val), (), w, cost=self._dc(ap, 1, eng))

    def dma(self, q, out, in_, sem, r=(), w=()):
        nb = 1
        for d in out.shape:
            nb *= int(d)
        c = 2200.0 + nb * (4 if in_.dtype == F32 else 2) / 250.0
        return self.op(q, lambda e: e.dma_start(out=out, in_=in_), r, w, dma=sem, cost=c)


def bc_last(ap2d, n):
    return ap2d.unsqueeze(2).broadcast_to([ap2d.shape[0], ap2d.shape[1], n])


def build(dbg=None, nocc=False, stop=None, psec="abc"):
    nc = bass.Bass("TRN2", target_bir_lowering=False)
    xin = nc.dram_tensor("xin", [NT + 128, D], F32, kind="ExternalInput").ap()
    xpre = nc.dram_tensor("xpre", [NW * NT, D], F32, kind="ExternalInput").ap()
    gl2 = nc.dram_tensor("gl2", [16, 1], F32, kind="Internal").ap()
    w_in = nc.dram_tensor("w_in", [D, NIN], F32, kind="ExternalInput").ap()
    w_out = nc.dram_tensor("w_out", [D, D], F32, kind="ExternalInput").ap()
    w_gate = nc.dram_tensor("w_gate", [D, DFF], F32, kind="ExternalInput").ap()
    w_up = nc.dram_tensor("w_up", [D, DFF], F32, kind="ExternalInput").ap()
    w_down = nc.dram_tensor("w_down", [DFF, D], F32, kind="ExternalInput").ap()
    pp_d = nc.dram_tensor("pp", [128, PPW], F32, kind="ExternalInput").ap()
    cst_d = nc.dram_tensor("cst", [128, CW], F32, kind="ExternalInput").ap()
    pw_d = nc.dram_tensor("pw", [128, 2, D], F32, kind="ExternalInput").ap()
    y = nc.dram_tensor("y", [NT, D], F32, kind="ExternalOutput").ap()
    if dbg:
        dbg_d = nc.dram_tensor("dbg", [128, 16, NT], F32, kind="ExternalOutput").ap()
    olocd = nc.dram_tensor("olocd", [16, 128, NT], F32, kind="Internal").ap()
    qhd = nc.dram_tensor("qhd", [8, 128, NT], BF16, kind="Internal").ap()
    acsd = nc.dram_tensor("acsd", [16, NT], F32, kind="Internal").ap()
    gsd = nc.dram_tensor("gsd", [16, NT], F32, kind="Internal").ap()
    x1d = nc.dram_tensor("x1d", [NT, D], F32, kind="Internal").ap()
    gl = nc.dram_tensor("gl", [16, 1], F32, kind="Internal").ap()
    xbuf = nc.dram_tensor("xbuf", [128, XW], F32, kind="Internal").ap()
    xg = nc.dram_tensor("xg", [NCORES * 128, XW], F32, addr_space="Local", kind="Internal").ap()

    w_in_v = w_in.rearrange("(k p) n -> p k n", p=128)

    with ExitStack() as es:
        P = Prog(nc, es)

        uid = [0]

        def sb(name, shape, dt, stack=None):
            uid[0] += 1
            return (stack or es).enter_context(nc.sbuf_tensor(f"{name}_s{uid[0]}", shape, dt))

        pp = sb("pp", [128, PPW], F32)
        cst = sb("cst", [128, CW], F32)
        idb = sb("idb", [128, 128], BF16)
        lbc = sb("lbc", [128, 16], F32)
        sinst = ExitStack()
        Sinb = sb("Sinb", [128, HK, 128], BF16, sinst)
        SinA = sb("SinA", [128, 8, 128], BF16, sinst)
        SinB = sb("SinB", [128, 8, 128], BF16, sinst)
        ps = [es.enter_context(nc.psum_tensor(f"ps{i}", [128, 512], F32)) for i in range(8)]
        psb = ps[7][:].bitcast(BF16)
        ps3b = ps[3][:].bitcast(BF16)
        ident = cst[:, K_ID:K_ID + 128]
        ones = cst[:, K_ONE:K_ONE + 128]

        with ExitStack() as ph:
            P.dma("sp", pp[:], pp_d, "pp", w=["pp"])
            P.dma("sp", cst[:], cst_d, "cst", w=["cst"])
            P.cp(idb[:], ident, r=["cst"], w=["idb"])
            t8 = sb("t8", [128, 32], F32, ph)
            l0, l1 = pp[:, C_L0:C_L0 + 8], pp[:, C_L1:C_L1 + 8]
            P.tt(t8[:, 0:8], l0, l1, ALU.max, r=["pp"], w=["t8"])
            P.tt(t8[:, 8:16], l0, t8[:, 0:8], ALU.subtract, r=["pp", "t8"], w=["t8"])
            P.tt(t8[:, 16:24], l1, t8[:, 0:8], ALU.subtract, r=["pp", "t8"], w=["t8"])
            P.act(t8[:, 8:24], t8[:, 8:24], AF.Exp, r=["t8"], w=["t8"])
            P.tt(t8[:, 24:32], t8[:, 8:16], t8[:, 16:24], ALU.add, r=["t8"], w=["t8"])
            P.recip(t8[:, 24:32], t8[:, 24:32], r=["t8"], w=["t8"])
            P.tt(lbc[:, 0:8], t8[:, 8:16], t8[:, 24:32], ALU.mult, r=["t8"], w=["lbc"])
            P.ts(lbc[:, 8:16], lbc[:, 0:8], -1.0, 1.0, ALU.mult, ALU.add, r=["lbc"], w=["lbc"])
            P.flush()

        with ExitStack() as ph:
            hTp = sb("hTp", [128, 16, NT], BF16, ph)
            T = [sb(f"T{j}", [128, D], F32, ph) for j in range(4)]
            st = sb("stp", [128, 16], F32, ph)
            wb = [sb(f"wbp{i}", [128, 16, 512], BF16, ph) for i in range(2)]
            v_tm = sb("v_tmp", [128, 16, 512], BF16, ph)
            kb = [sb(f"kb{i}", [128, NT], BF16, ph) for i in range(2)]
            ktmP = [sb(f"ktmP{i}", [128, 16, 128], BF16, ph) for i in range(2)]
            Sh = sb("Sh", [128, HK, 128], F32, ph)
            Ss = sb("Ss", [128, 16, 64], F32, ph)
            gsm = sb("gsm", [128, 4], F32, ph)
            ub = [sb(f"ubp{i}", [128, NT + 3], F32, ph) for i in range(2)]
            xwt = [sb(f"xwt{i}", [128, 16, 128], BF16, ph) for i in range(2)]
            wst = sb("wst", [128, 16, 16], F32, ph)
            wdt = sb("wdtp", [128, 16, 16], BF16, ph)
            aneg = sb("anegp", [16, 2], F32, ph)
            dsb = sb("dsbp", [128, 16], F32, ph)
            utail = sb("utail", [128, 10, 3], F32, ph)
            junk = kb[0]
            BT, Btm = kb[1], ktmP[1]
            P.memset(Sh[:], 0.0, w=["Sh"])
            P.memset(Ss[:], 0.0, w=["Ss"])
            P.memset(utail[:], 0.0, w=["utail"])
            P.dma("pool", wdt[:], w_in_v[:, :, 6656:6672], "wdtp", w=["wdtp"])
            P.act(aneg[:, 0:1], pp[0:16, C_ALOG:C_ALOG + 1], AF.Exp, r=["pp"], w=["anegp"])
            P.ts(aneg[:, 1:2], aneg[:, 0:1], -1.0, None, ALU.mult, r=["anegp"], w=["anegp"])
            ones_bc = cst[:, K_ONE:K_ONE + 1].broadcast_to([128, NT])

            def run_tasks(tasks, width=2):
                pending = list(tasks)
                active = [None] * width
                while pending or any(a is not None for a in active):
                    for s in range(width):
                        if active[s] is None and pending:
                            active[s] = pending.pop(0)(s)
                        if active[s] is not None:
                            try:
                                next(active[s])
                            except StopIteration:
                                active[s] = None

            PJ = [(0, 6), (2, 7)]
            AX = [1, 3]

            def v_task(hg):
                def gen(slot):
                    P.dma("pool", wb[0][:], w_in_v[:, :, 1024 + hg * 512:1024 + (hg + 1) * 512], "wbp0", w=["wbp0"])
                    P.dma("pool", wb[1][:], w_in_v[:, :, 2048 + hg * 512:2048 + (hg + 1) * 512], "wbp1", w=["wbp1"])
                    yield
                    for ti in range(16):
                        bk = 4 + ti % 2
                        for kc in range(16):
                            P.mm(ps[bk][:, :], hTp[:, kc, ti * 128:(ti + 1) * 128], wb[1][:, kc, :],
                                 start=(kc == 0), stop=(kc == 15), r=["wbp1", f"hTp_{ti // 4}"], w=[f"ps{bk}"])
                        P.cp(v_tm[:, ti, :], ps[bk][:, :], r=[f"ps{bk}"], w=["v_tmp"], eng=("act" if ti % 2 else "dve"))
                        yield
                return gen

            def transposes_bf(s, src, srcname, dst, dstname):
                ax = AX[s]
                axb = ps[ax][:].bitcast(BF16)
                for t8 in range(2):
                    for tt_ in range(8):
                        ti = t8 * 8 + tt_
                        P.tr(axb[:, tt_ * 128:(tt_ + 1) * 128], src[:, ti * 128:(ti + 1) * 128], idb[:],
                             r=[srcname, "idb"], w=[f"ps{ax}"], inc=(tt_ == 7))
                    P.cp(dst[:, t8 * 8:(t8 + 1) * 8, :], axb.rearrange("p (c n) -> p c n", n=128),
                         r=[f"ps{ax}"], w=[dstname], eng=("act" if t8 else "dve"))
                    yield

            def h_task(h):
                hh = h % 4

                def gen(s):
                    A, C = T[2 * s], T[2 * s + 1]
                    An, Cn = f"T{2 * s}", f"T{2 * s + 1}"
                    gcol = gsm[:, s:s + 1]
                    ax = AX[s]
                    for tg in range(4):
                        bk = PJ[s][tg % 2]
                        for kc in range(16):
                            P.mm(ps[bk][:, :], wb[0][:, kc, hh * 128:(hh + 1) * 128], hTp[:, kc, tg * 512:(tg + 1) * 512],
                                 start=(kc == 0), stop=(kc == 15), r=["wbp0", f"hTp_{tg}"], w=[f"ps{bk}"])
                        P.act(A[:, tg * 512:(tg + 1) * 512], ps[bk][:, :], AF.Sigmoid, r=[f"ps{bk}"], w=[An])
                        yield
                    P.act(A[:], A[:], AF.Ln, r=[An, "lbc"], w=[An], scale=lbc[:, 8 + h:9 + h], bias=lbc[:, h:h + 1])
                    yield
                    P.scan(C[:], ones_bc, A[:], 0.0, ALU.mult, ALU.add, r=["cst", An], w=[Cn])
                    yield
                    P.act(A[:], A[:], AF.Exp, r=[An], w=[An])
                    yield
                    P.act(A[:], A[:], AF.Identity, r=[An], w=[An], scale=-1.0, bias=1.0)
                    yield
                    P.cp(gcol, C[:, NT - 1:NT], r=[Cn], w=[f"gsm{s}"])
                    P.act(C[:], C[:], AF.Exp, r=[Cn, f"gsm{s}"], w=[Cn], scale=-1.0, bias=gcol)
                    yield
                    P.tt(kb[s][:], A[:], C[:], ALU.mult, r=[An, Cn], w=[f"kb{s}"])
                    P.act(gcol, gcol, AF.Exp, r=[f"gsm{s}"], w=[f"gsm{s}"])
                    yield
                    yield from transposes_bf(s, kb[s], f"kb{s}", ktmP[s], f"ktmP{s}")
                    for ti in range(16):
                        P.mm(ps[ax][:, 0:128], ktmP[s][:, ti, :], v_tm[:, ti, hh * 128:(hh + 1) * 128],
                             start=(ti == 0), stop=(ti == 15), r=[f"ktmP{s}", "v_tmp"], w=[f"ps{ax}"])
                    yield
                    P.stt(Sh[:, h, :], Sh[:, h, :], gcol, ps[ax][:, 0:128], ALU.mult, ALU.add,
                          r=[f"Sh{h}", f"gsm{s}", f"ps{ax}"], w=[f"Sh{h}"])
                    yield
                return gen

            def conv_task(w, s_w, cols, jc, kind, j):
                def gen(s):
                    acc = T[2 + s]
                    an = f"T{2 + s}"
                    ubs, ubn = ub[s], f"ubp{s}"
                    ax = AX[s]
                    for tg in range(4):
                        bk = PJ[s][tg % 2]
                        for kc in range(16):
                            P.mm(ps[bk][:, :], wb[s_w][:, kc, cols:cols + 128], hTp[:, kc, tg * 512:(tg + 1) * 512],
                                 start=(kc == 0), stop=(kc == 15), r=[f"wbp{s_w}", f"hTp_{tg}"], w=[f"ps{bk}"])
                        P.cp(ubs[:, 3 + tg * 512:3 + (tg + 1) * 512], ps[bk][:, :], r=[f"ps{bk}"], w=[ubn],
                             eng=("act" if tg % 2 else "dve"))
                        yield
                    P.cp(ubs[:, 0:3], utail[:, jc, :], r=["utail"], w=[ubn])
                    wc = lambda k_: pp[:, C_CONVW + k_ * 12 + jc:C_CONVW + k_ * 12 + jc + 1]
                    P.ts(acc[:], ubs[:, 0:NT], wc(0), pp[:, C_CONVB + jc:C_CONVB + jc + 1], ALU.mult, ALU.add,
                         r=[ubn, "pp"], w=[an])
                    yield
                    for k_ in range(1, 4):
                        P.stt(acc[:], ubs[:, k_:k_ + NT], wc(k_), acc[:], ALU.mult, ALU.add, r=[ubn, "pp", an], w=[an])
                        yield
                    P.cp(utail[:, jc, :], ubs[:, NT:NT + 3], r=[ubn], w=["utail"], eng="act")
                    if kind == "B":
                        P.act(BT[:], acc[:], AF.Silu, r=[an], w=["kb1"])
                        yield
                        yield from transposes_bf(s, BT, "kb1", Btm, "ktmP1")
                        return
                    P.act(acc[:], acc[:], AF.Silu, r=[an], w=[an])
                    yield
                    for t4 in range(4):
                        for c4 in range(4):
                            ti = t4 * 4 + c4
                            P.tr(ps[ax][:, c4 * 128:(c4 + 1) * 128], acc[:, ti * 128:(ti + 1) * 128], ident,
                                 r=[an, "cst"], w=[f"ps{ax}"], inc=(c4 == 3))
                        for c4 in range(4):
                            ti = t4 * 4 + c4
                            for hx in range(2):
                                P.ts(xwt[s][:, ti, hx * 64:(hx + 1) * 64], ps[ax][:, c4 * 128 + hx * 64:c4 * 128 + (hx + 1) * 64],
                                     wst[:, ti, 2 * j + hx:2 * j + hx + 1], None, ALU.mult,
                                     r=[f"ps{ax}", "wst"], w=[f"xwt{s}"], eng=("dve" if t4 % 2 else "pool") if False else "dve")
                        yield
                    for ti in range(16):
                        P.mm(ps[ax][:, 0:128], Btm[:, ti, :], xwt[s][:, ti, :],
                             start=(ti == 0), stop=(ti == 15), r=["ktmP1", f"xwt{s}"], w=[f"ps{ax}"])
                    yield
                    P.tt(Ss[:, 2 * j:2 * j + 2, :], Ss[:, 2 * j:2 * j + 2, :], bc_last(dsb[:, 2 * j:2 * j + 2], 64), ALU.mult,
                         r=[f"Ss{j}", "dsbp"], w=[f"Ss{j}"])
                    P.tt(Ss[:, 2 * j:2 * j + 2, :], Ss[:, 2 * j:2 * j + 2, :],
                         ps[ax][:, 0:128].rearrange("p (t v) -> p t v", v=64), ALU.add,
                         r=[f"Ss{j}", f"ps{ax}"], w=[f"Ss{j}"])
                    yield
                return gen

            evq = 0
            for w in range(NW):
                for g4 in range(4):
                    for j in range(4):
                        ti = w * 16 + g4 * 4 + j
                        xb = f"T{j}"
                        P.dma("sp", T[j][:], xpre[ti * 128:(ti + 1) * 128, :], xb, w=[xb])
                        ss = st[:, 4 * j:4 * j + 1]
                        P.act(junk[:], T[j][:], AF.Square, r=[xb], w=["kb0", f"stp{j}"], accum_out=ss)
                        P.ts(st[:, 4 * j + 1:4 * j + 2], ss, 1.0 / D, EPS, ALU.mult, ALU.add, r=[f"stp{j}"], w=[f"stp{j}"])
                        P.act(st[:, 4 * j + 2:4 * j + 3], st[:, 4 * j + 1:4 * j + 2], AF.Sqrt, r=[f"stp{j}"], w=[f"stp{j}"])
                        P.recip(st[:, 4 * j + 3:4 * j + 4], st[:, 4 * j + 2:4 * j + 3], r=[f"stp{j}"], w=[f"stp{j}"])
                        P.act(T[j][:], T[j][:], AF.Copy, r=[xb, f"stp{j}"], w=[xb], scale=st[:, 4 * j + 3:4 * j + 4])
                    for kc in range(16):
                        bk = kc % 4
                        for j in range(4):
                            P.tr(ps[bk][:, j * 128:(j + 1) * 128], T[j][:, kc * 128:(kc + 1) * 128], ident,
                                 r=[f"T{j}", "cst"], w=[f"ps{bk}"], inc=(j == 3))
                        wcol = pp[:, C_PREMIX + kc:C_PREMIX + kc + 1]
                        o_ap = hTp[:, kc, g4 * 512:(g4 + 1) * 512]
                        if evq % 2 == 0:
                            P.ts(o_ap, ps[bk][:, :], wcol, None, ALU.mult, r=[f"ps{bk}", "pp"], w=[f"hTp_{g4}"])
                        else:
                            P.act(o_ap, ps[bk][:, :], AF.Copy, r=[f"ps{bk}", "pp"], w=[f"hTp_{g4}"], scale=wcol)
                        evq += 1
                if w >= NW - HW and "b" in psec:
                    for hg in range(2):
                        run_tasks([v_task(hg)], width=1)
                        run_tasks([h_task(hg * 4 + hh) for hh in range(4)])
                if "c" not in psec:
                    continue
                P.dma("pool", wb[1][:], w_in_v[:, :, 6144:6656], "wbp1", w=["wbp1"])
                P.dma("pool", wb[0][:], w_in_v[:, :, 5120:5632], "wbp0", w=["wbp0"])
                for tg in range(4):
                    bk = 4 + tg % 2
                    for kc in range(16):
                        P.mm(ps[bk][0:16, :], wdt[:, kc, :], hTp[:, kc, tg * 512:(tg + 1) * 512],
                             start=(kc == 0), stop=(kc == 15), r=["wdtp", f"hTp_{tg}"], w=[f"ps{bk}"])
                    P.act(T[0][0:16, tg * 512:(tg + 1) * 512], ps[bk][0:16, :], AF.Exp, r=[f"ps{bk}", "pp"], w=["T0"],
                          bias=pp[0:16, C_DTB:C_DTB + 1])
                P.act(T[0][0:16, :], T[0][0:16, :], AF.Ln, r=["T0"], w=["T0"], bias=1.0)
                P.ts(T[0][0:16, :], T[0][0:16, :], pp[0:16, C_V + w:C_V + w + 1], None, ALU.mult, r=["T0", "pp"], w=["T0"])
                P.ts(T[1][0:16, :], T[0][0:16, :], aneg[:, 1:2], None, ALU.mult, r=["T0", "anegp"], w=["T1"])
                P.scan(T[0][64:80, :], ones_bc[0:16, :], T[1][0:16, :], 0.0, ALU.mult, ALU.add, r=["cst", "T1"], w=["T0"])
                Gs = T[0][64:80, :]
                P.dma("sp", gl2, Gs[:, NT - 1:NT], "gl2", r=["T0"], w=["gl2"])
                P.cp(gsm[64:80, 2:3], Gs[:, NT - 1:NT], r=["T0"], w=["gsm2"])
                P.act(T[1][64:80, :], Gs, AF.Exp, r=["T0", "gsm2"], w=["T1"], scale=-1.0, bias=gsm[64:80, 2:3])
                P.cp(T[1][0:16, :], T[1][64:80, :], r=["T1"], w=["T1"], eng="act")
                P.tt(T[1][0:16, :], T[1][0:16, :], T[0][0:16, :], ALU.mult, r=["T0", "T1"], w=["T1"])
                for ti in range(16):
                    P.tr(ps[3][:, ti * 16:(ti + 1) * 16], T[1][0:16, ti * 128:(ti + 1) * 128], cst[0:16, K_ID:K_ID + 16],
                         r=["T1", "cst"], w=["ps3"], inc=(ti == 15))
                P.cp(wst[:], ps[3][:, 0:256].rearrange("p (t h) -> p t h", h=16), r=["ps3"], w=["wst"])
                P.dma("sp", dsb[:], bass.AP(gl2.tensor, 0, [[0, 128], [1, 16]]), "dsbp", r=["gl2"], w=["dsbp"])
                P.act(dsb[:], dsb[:], AF.Exp, r=["dsbp"], w=["dsbp"])
                for g in range(2):
                    if g == 1:
                        P.dma("pool", wb[0][:], w_in_v[:, :, 5120 + 512:5120 + 1024], "wbp0", w=["wbp0"])
                    run_tasks([conv_task(w, 1, g * 128, 8 + g, "B", None)], width=1)
                    run_tasks([conv_task(w, 0, jp * 128, 4 * g + jp, "x", 4 * g + jp) for jp in range(4)])
            if dbg == "sin":
                P.dma("sp", dbg_d[:, 0, 0:1024], Sh[:].rearrange("p h v -> p (h v)"), "dbgs", r=[f"Sh{h}" for h in range(8)] + ["Sh"], w=["dbgd"])
                P.dma("sp", dbg_d[:, 1, 0:1024], Ss[:].rearrange("p h v -> p (h v)"), "dbgs", r=[f"Ss{j}" for j in range(8)] + ["Ss"], w=["dbgd"])
                P.flush()
                return nc
            P.cp(Sinb[:], Sh[:], r=[f"Sh{h}" for h in range(8)], w=["Sinb"], eng="act")
            P.memset(SinA[:], 0.0, w=["SinA"], eng="pool")
            P.memset(SinB[:], 0.0, w=["SinB"], eng="pool")
            Ss4 = Ss[:].rearrange("p (j t) v -> p j (t v)", t=2)
            P.cp(SinA[:, :, 0:64], Ss4[:, :, 0:64], r=[f"Ss{j}" for j in range(8)], w=["SinA"], eng="act")
            P.cp(SinB[:, :, 64:128], Ss4[:, :, 64:128], r=[f"Ss{j}" for j in range(8)], w=["SinB"], eng="act")
            P.flush()
            if stop == "P":
                return nc

        mid = ExitStack()
        hT = sb("hT", [128, 16, NT], BF16, mid)
        hTh = sb("hTh", [128, 16, 4], BF16, mid)
        with ExitStack() as ph:
            xt = [sb(f"xt{j}", [128, D], F32, ph) for j in range(8)]
            junk = sb("junk", [128, D], BF16, ph)
            st = sb("st", [128, 32], F32, ph)
            groups = [[0]] + [[1 + 4 * g + i for i in range(4)] for g in range(4)]
            evq = 0
            for gi, grp in enumerate(groups):
                for j0, ti in enumerate(grp):
                    j = (gi % 2) * 4 + j0
                    xb = f"xt{j}"
                    P.dma("sp", xt[j][:], xin[ti * 128:(ti + 1) * 128, :], xb, w=[xb])
                    ss = st[:, 4 * j:4 * j + 1]
                    P.act(junk[:], xt[j][:], AF.Square, r=[xb], w=["junk", f"st{j}"], accum_out=ss)
                    P.ts(st[:, 4 * j + 1:4 * j + 2], ss, 1.0 / D, EPS, ALU.mult, ALU.add, r=[f"st{j}"], w=[f"st{j}"])
                    P.act(st[:, 4 * j + 2:4 * j + 3], st[:, 4 * j + 1:4 * j + 2], AF.Sqrt, r=[f"st{j}"], w=[f"st{j}"])
                    P.recip(st[:, 4 * j + 3:4 * j + 4], st[:, 4 * j + 2:4 * j + 3], r=[f"st{j}"], w=[f"st{j}"])
                    P.act(xt[j][:], xt[j][:], AF.Copy, r=[xb, f"st{j}"], w=[xb], scale=st[:, 4 * j + 3:4 * j + 4])
                n = len(grp)
                for kc in range(16):
                    bk = kc % 8
                    for j0 in range(n):
                        j = (gi % 2) * 4 + j0
                        P.tr(ps[bk][:, j0 * 128:(j0 + 1) * 128], xt[j][:, kc * 128:(kc + 1) * 128], ident,
                             r=[f"xt{j}", "cst"], w=[f"ps{bk}"], inc=(j0 == n - 1))
                    wcol = pp[:, C_PREMIX + kc:C_PREMIX + kc + 1]
                    if gi == 0:
                        o_ap, i_ap, wb_ = hTh[:, kc, 0:4], ps[bk][:, 124:128], "hTh"
                    else:
                        c0 = (gi - 1) * 512
                        o_ap, i_ap, wb_ = hT[:, kc, c0:c0 + 512], ps[bk][:, 0:512], "hT"
                    if evq % 2 == 0:
                        P.ts(o_ap, i_ap, wcol, None, ALU.mult, r=[f"ps{bk}", "pp"], w=[wb_])
                    else:
                        P.act(o_ap, i_ap, AF.Copy, r=[f"ps{bk}", "pp"], w=[wb_], scale=wcol)
                    evq += 1
            P.flush()

        with ExitStack() as ph:
            wb = [sb(f"wb{i}", [128, 16, 512], BF16, ph) for i in range(3)]
            v_tm = sb("v_tm", [128, 16, 512], BF16, ph)
            qf = sb("qf", [128, NT], F32, ph)
            fg = sb("fg", [128, NT], F32, ph)
            kk = sb("kk", [128, NT], F32, ph)
            bb = sb("bb", [128, NT], F32, ph)
            GG = sb("GG", [128, NT], F32, ph)
            qt = [sb(f"qt{i}", [128, NT], BF16, ph) for i in range(2)]
            kt = [sb(f"kt{i}", [128, NT], BF16, ph) for i in range(2)]
            qh = sb("qh", [128, NT], BF16, ph)
            rm = sb("rm", [128, NT], BF16, ph)
            chs = [sb(f"chs{i}", [128, 3, 32], F32, ph) for i in range(2)]
            Sb = [sb(f"Sb{i}", [128, 128], BF16, ph) for i in range(4)]
            Sst = [sb(f"Sst{i}", [128, 128], F32, ph) for i in range(4)]
            scm = [sb(f"scm{i}", [128, 64], BF16, ph) for i in range(4)]
            ktm = [sb(f"ktm{i}", [128, 128], BF16, ph) for i in range(4)]
            ost1_ = sb("ost0", [128, 256], F32, ph)
            ost = [ost1_, ost1_]

            P.memset(rm[:], 1.0, w=["rm"])
            P.memset(rm[:].rearrange("p (c j) -> p c j", j=64)[:, :, 0:1], 0.0, w=["rm"])
            for i in range(4):
                P.memset(ktm[i][:], 0.0, w=[f"ktm{i}"], eng="pool")
            wslot = [0]

            def load_w(c0):
                s = wslot[0] % 3
                wslot[0] += 1
                P.dma("pool", wb[s][:], w_in_v[:, :, c0:c0 + 512], f"wb{s}", w=[f"wb{s}"])
                return s

            pbank = [0]

            def proj_fm(s, cols, tg, w128=128):
                bk = pbank[0] % 2
                pbank[0] += 1
                for kc in range(16):
                    P.mm(ps[bk][0:w128, :], wb[s][:, kc, cols:cols + w128], hT[:, kc, tg * 512:(tg + 1) * 512],
                         start=(kc == 0), stop=(kc == 15), r=[f"wb{s}", "hT"], w=[f"ps{bk}"])
                return bk

            for hg in range(2):
                sq = load_w(hg * 512)
                sf = load_w(1024 + hg * 512)
                si = load_w(2048 + hg * 512)
                for ti in range(16):
                    bk = pbank[0] % 2
                    pbank[0] += 1
                    for kc in range(16):
                        P.mm(ps[bk][:, :], hT[:, kc, ti * 128:(ti + 1) * 128], wb[si][:, kc, :],
                             start=(kc == 0), stop=(kc == 15), r=[f"wb{si}", "hT"], w=[f"ps{bk}"])
                    P.cp(v_tm[:, ti, :], ps[bk][:, :], r=[f"ps{bk}"], w=["v_tm"], eng=("act" if ti % 2 else "dve"))
                def s1_task(h, hh, sq=sq, sf=sf):
                    p = h % 2
                    qt_, kt_, chs_ = qt[p], kt[p], chs[p]
                    qn, kn, cn = f"qt{p}", f"kt{p}", f"chs{p}"

                    def gen():
                        HL = NT // 2
                        for tg in range(4):
                            bk = proj_fm(sq, hh * 128, tg)
                            P.act(qf[:, tg * 512:(tg + 1) * 512], ps[bk][:, :], AF.Silu, r=[f"ps{bk}"], w=[f"qf_{tg // 2}"])
                        for tg in range(4):
                            bk = proj_fm(sf, hh * 128, tg)
                            P.act(fg[:, tg * 512:(tg + 1) * 512], ps[bk][:, :], AF.Sigmoid, r=[f"ps{bk}"], w=[f"fg_{tg // 2}"])
                        for hf in range(2):
                            sl = slice(hf * HL, (hf + 1) * HL)
                            fn_, kn_, bn_, gn_ = f"fg_{hf}", f"kk_{hf}", f"bb_{hf}", f"GG_{hf}"
                            P.ts(fg[:, sl], fg[:, sl], lbc[:, 8 + h:9 + h], lbc[:, h:h + 1], ALU.mult, ALU.add, r=[fn_, "lbc"], w=[fn_])
                            P.ts(kk[:, sl], fg[:, sl], -1.0, 1.0, ALU.mult, ALU.add, r=[fn_], w=[kn_])
                            P.act(fg[:, sl], fg[:, sl], AF.Ln, r=[fn_], w=[fn_])
                            P.scan(bb[:, sl], rm[:, sl], fg[:, sl], 0.0, ALU.mult, ALU.add, r=["rm", fn_], w=[bn_])
                            init = 0.0 if hf == 0 else GG[:, HL - 1:HL]
                            P.scan(GG[:, sl], cst[:, K_ONE:K_ONE + 1].broadcast_to([128, HL]), fg[:, sl], init, ALU.mult, ALU.add,
                                   r=["cst", fn_] + (["GGlast"] if hf else []), w=[gn_] + (["GGlast"] if hf == 0 else []))
                        for hf in range(2):
                            sl = slice(hf * HL, (hf + 1) * HL)
                            gn_ = f"GG_{hf}"
                            P.act(GG[:, sl], GG[:, sl], AF.Exp, r=[gn_] + (["GGlast"] if hf == 0 else []), w=[gn_] + (["GGlast"] if hf == 0 else []))
                            P.stt(qh[:, sl], qf[:, sl], 128.0 ** -0.5, GG[:, sl], ALU.mult, ALU.mult, r=[f"qf_{hf}", gn_], w=[f"qh_{hf}"])
                        P.dma("sp", qhd[h], qh[:], "qhd", r=["qh_0", "qh_1"], w=[f"qhd{h}"])
                        for hf in range(2):
                            sl = slice(hf * HL, (hf + 1) * HL)
                            cs_ = slice(hf * 16, (hf + 1) * 16)
                            fn_, kn_, bn_, gn_ = f"fg_{hf}", f"kk_{hf}", f"bb_{hf}", f"GG_{hf}"
                            cnh = f"{cn}_{hf}"
                            b3 = bb[:, sl].rearrange("p (c j) -> p c j", j=64)
                            P.act(chs_[:, 0, cs_], b3[:, :, 63], AF.Exp, r=[bn_], w=[cnh])
                            P.act(chs_[:, 2, cs_], b3[:, :, 31], AF.Exp, r=[bn_], w=[cnh])
                            f3 = fg[:, sl].rearrange("p (c j) -> p c j", j=64)
                            P.tt(f3, b3, bc_last(b3[:, :, 31], 64), ALU.subtract, r=[bn_], w=[fn_])
                            P.act(chs_[:, 1, cs_], f3[:, :, 63], AF.Exp, r=[fn_], w=[cnh])
                            P.act(bb[:, sl], fg[:, sl], AF.Exp, r=[fn_], w=[bn_])
                            P.act(GG[:, sl], fg[:, sl], AF.Exp, r=[fn_], w=[gn_], scale=-1.0)
                            P.stt(qt_[:, sl], qf[:, sl], 128.0 ** -0.5, bb[:, sl], ALU.mult, ALU.mult, r=[f"qf_{hf}", bn_], w=[f"{qn}_{hf}"])
                            P.tt(kt_[:, sl], kk[:, sl], GG[:, sl], ALU.mult, r=[kn_, gn_], w=[f"{kn}_{hf}"])
                        yield
                    return gen()

                def s2_task(h, hh):
                    p = h % 2
                    qt_, kt_, chs_ = qt[p], kt[p], chs[p]
                    qn, kn, cn = f"qt{p}", f"kt{p}", f"chs{p}"

                    def gen():

                        def stage_a(c):
                            ti, half = c // 2, c % 2
                            t0 = c * 64
                            tsl = ti % 2
                            if half == 0:
                                tb = ps3b[:, 512 + tsl * 128:512 + (tsl + 1) * 128]
                                P.tr(tb, kt_[:, ti * 128:(ti + 1) * 128], idb[:],
                                     r=[f"{kn}_{c // 16}", "idb"], w=[f"ps3b{tsl}"])
                                P.cp(ktm[2 * tsl][0:64, :], ps3b[0:64, 512 + tsl * 128:512 + (tsl + 1) * 128], r=[f"ps3b{tsl}"],
                                     w=[f"ktm{2 * tsl}"], eng="act")
                                P.cp(ktm[2 * tsl + 1][64:128, :], ps3b[64:128, 512 + tsl * 128:512 + (tsl + 1) * 128], r=[f"ps3b{tsl}"],
                                     w=[f"ktm{2 * tsl + 1}"], eng="act")
                            a0 = ((c // 2) % 4) * 64
                            ab = 2 if c % 2 == 0 else 7
                            an_ = f"ps{ab}A{(c // 2) % 4}"
                            P.mm(ps[ab][:, a0:a0 + 64], kt_[:, ti * 128:(ti + 1) * 128], qt_[:, t0:t0 + 64],
                                 r=[f"{kn}_{c // 16}", f"{qn}_{c // 16}"], w=[an_])
                            mcol = K_TRE if half == 0 else K_TRO
                            P.tt(scm[c % 4][:], ps[ab][:, a0:a0 + 64], cst[:, mcol:mcol + 64], ALU.mult,
                                 r=[an_, "cst"], w=[f"scm{c % 4}"])
                            vv = v_tm[:, ti, hh * 128:(hh + 1) * 128]
                            db, d0 = 3 + c % 2, ((c // 2) % 2) * 128
                            P.mm(ps[db][:, d0:d0 + 128], ktm[2 * tsl + half][:], vv, r=[f"ktm{2 * tsl + half}", "v_tm"],
                                 w=[f"ps{db}D{(c // 2) % 2}"])

                        def stage_b(c):
                            ti = c // 2
                            t0 = c * 64
                            cb = 5 + (c // 8) % 2
                            j = c % 8
                            vv = v_tm[:, ti, hh * 128:(hh + 1) * 128]
                            P.mm(ps[cb][:, j * 64:(j + 1) * 64], vv, scm[c % 4][:], start=True, stop=(c == 0),
                                 r=["v_tm", f"scm{c % 4}"], w=[f"ps{cb}"])
                            if c > 0:
                                P.mm(ps[cb][:, j * 64:(j + 1) * 64], Sb[2 * p + c % 2][:], qt_[:, t0:t0 + 64], start=False, stop=True,
                                     r=[f"Sb{2 * p + c % 2}", f"{qn}_{c // 16}"], w=[f"ps{cb}"])
                            db, d0 = 3 + c % 2, ((c // 2) % 2) * 128
                            dn = f"ps{db}D{(c // 2) % 2}"
                            ci, ni = 2 * p + c % 2, 2 * p + (c + 1) % 2
                            Scur, Snxt = Sst[ci][:, :], Sst[ni][:, :]
                            if c == 0:
                                P.ts(Snxt, ps[db][:, d0:d0 + 128], chs_[:, 1, c:c + 1], None, ALU.mult,
                                     r=[dn, f"{cn}_{c // 16}"], w=[f"Sst{ni}"])
                            else:
                                P.ts(Snxt, Scur, chs_[:, 0, c:c + 1], None, ALU.mult, r=[f"Sst{ci}", f"{cn}_{c // 16}"], w=[f"Sst{ni}"])
                                P.stt(Snxt, ps[db][:, d0:d0 + 128], chs_[:, 1, c:c + 1], Snxt, ALU.mult, ALU.add,
                                      r=[dn, f"{cn}_{c // 16}", f"Sst{ni}"], w=[f"Sst{ni}"])
                            if c < 31:
                                P.act(Sb[ni][:], Snxt, AF.Copy, r=[f"Sst{ni}", f"{cn}_{(c + 1) // 16}"], w=[f"Sb{ni}"],
                                      scale=chs_[:, 2, c + 1:c + 2])
                            if j == 7:
                                tg = c // 8
                                for o2 in range(2):
                                    P.cp(ost[0][:], ps[cb][:, o2 * 256:(o2 + 1) * 256], r=[f"ps{cb}"], w=["ost0"], eng="act")
                                    P.dma("sp", olocd[h][:, tg * 512 + o2 * 256:tg * 512 + (o2 + 1) * 256], ost[0][:], "ost0",
                                          r=["ost0"], w=[f"olocd{h}"])

                        LA = 2
                        for c in range(LA):
                            stage_a(c)
                        for c in range(32):
                            if c + LA < 32:
                                stage_a(c + LA)
                            stage_b(c)
                            yield
                    return gen()

                def rr(a, b):
                    while a is not None or b is not None:
                        if a is not None:
                            try:
                                next(a)
                            except StopIteration:
                                a = None
                        if b is not None:
                            try:
                                next(b)
                            except StopIteration:
                                b = None

                rr(s1_task(hg * 4, 0), None)
                for hh in range(4):
                    h = hg * 4 + hh
                    rr(s2_task(h, hh), s1_task(h + 1, hh + 1) if hh < 3 else None)
            P.flush()

        CTall = sb("CTall", [128, 2, NT], BF16, mid)
        with ExitStack() as ph:
            wb = [sb(f"wb{i}", [128, 16, 512], BF16, ph) for i in range(2)]
            wdt = sb("wdt", [128, 16, 16], BF16, ph)
            ub = sb("ub", [128, NT + 3], F32, ph)
            xTp = sb("xTp", [128, NT], F32, ph)
            BT = sb("BT", [128, NT], BF16, ph)
            cbT = sb("cbT", [128, NT], F32, ph)
            Btm = sb("Btm", [128, 16, 128], BF16, ph)
            stA = sb("stA", [96, NT], F32, ph)
            stB = sb("stB", [96, NT], F32, ph)
            tmA = sb("tmA", [128, 16, 96], F32, ph)
            aneg = sb("aneg", [48, 2], F32, ph)
            abc = [[sb(f"abc{a}{b}", [128, 512], F32, ph) for b in range(2)] for a in range(2)]
            cht = [[sb(f"cht{a}{b}", [128, 512], BF16, ph) for b in range(2)] for a in range(2)]
            xdA = [sb(f"xdA{i}", [128, 128], BF16, ph) for i in range(4)]
            xdB = [sb(f"xdB{i}", [128, 128], BF16, ph) for i in range(4)]
            xw = [sb(f"xw{i}", [128, 128], BF16, ph) for i in range(4)]
            Dm = [sb(f"Dm{i}", [128, 128], F32, ph) for i in range(2)]
            Mt = [sb(f"Mt{i}", [128, 128], BF16, ph) for i in range(8)]
            prA = [sb(f"prA{i}", [128, 128], BF16, ph) for i in range(4)]
            prB = [sb(f"prB{i}", [128, 128], BF16, ph) for i in range(4)]
            Spp = [sb(f"Spp{i}", [128, 128], F32, ph) for i in range(4)]
            yst = [sb(f"yst{i}", [128, 512], F32, ph) for i in range(2)]
            ebt = sb("ebt", [128, 512], F32, ph)
            dcs = sb("dcs", [128, 2, 16], F32, ph)

            for i in range(4):
                P.memset(xdA[i][:], 0.0, w=[f"xdA{i}"], eng="pool")
                P.memset(xdB[i][:], 0.0, w=[f"xdB{i}"], eng="pool")
            for i in range(4):
                P.memset(prA[i][:], 0.0, w=[f"prA{i}"], eng="pool")
                P.memset(prB[i][:], 0.0, w=[f"prB{i}"], eng="pool")
            P.memset(stA[:], 0.0, w=["stA"])
            P.memset(stB[:], 0.0, w=["stB"])
            P.dma("pool", wdt[:], w_in_v[:, :, 6656:6672], "wdt", w=["wdt"])
            P.act(aneg[32:48, 0:1], pp[32:48, C_ALOG:C_ALOG + 1], AF.Exp, r=["pp"], w=["aneg"])
            P.ts(aneg[32:48, 1:2], aneg[32:48, 0:1], -1.0, None, ALU.mult, r=["aneg"], w=["aneg"])
            for tg in range(4):
                bk = tg % 2
                for kc in range(16):
                    P.mm(ps[bk][0:16, :], wdt[:, kc, :], hT[:, kc, tg * 512:(tg + 1) * 512],
                         start=(kc == 0), stop=(kc == 15), r=["wdt", "hT"], w=[f"ps{bk}"])
                P.act(stA[32:48, tg * 512:(tg + 1) * 512], ps[bk][0:16, :], AF.Exp, r=[f"ps{bk}", "pp"], w=["stA"],
                      bias=pp[32:48, C_DTB:C_DTB + 1])
            P.act(stA[32:48, :], stA[32:48, :], AF.Ln, r=["stA"], w=["stA"], bias=1.0)
            P.cp(stB[64:80, :], stA[32:48, :], r=["stA"], w=["stB"], eng="act")
            P.ts(stB[32:48, :], stA[32:48, :], aneg[32:48, 1:2], None, ALU.mult, r=["stA", "aneg"], w=["stB"])
            P.scan(stB[0:16, :], cst[32:48, K_ONE:K_ONE + 1].broadcast_to([16, NT]), stB[32:48, :], 0.0, ALU.mult, ALU.add,
                   r=["stB", "cst"], w=["stB"])
            g3 = stB[0:16, :].rearrange("p (c j) -> p c j", j=128)
            a3 = stA[0:16, :].rearrange("p (c j) -> p c j", j=128)
            w3 = stA[64:80, :].rearrange("p (c j) -> p c j", j=128)
            P.cp(stA[0:16, 0:128], stB[0:16, 0:128], r=["stB"], w=["stA"], eng="act")
            P.tt(a3[:, 1:16, :], g3[:, 1:16, :], bc_last(g3[:, 0:15, 127], 128), ALU.subtract, r=["stB"], w=["stA"])
            P.tt(w3, bc_last(a3[:, :, 127], 128), a3, ALU.subtract, r=["stA"], w=["stA"])
            P.act(stA[64:80, :], stA[64:80, :], AF.Exp, r=["stA"], w=["stA"])
            P.tt(stA[64:80, :], stA[64:80, :], stB[64:80, :], ALU.mult, r=["stA", "stB"], w=["stA"])
            P.dma("sp", acsd, stA[0:16, :], "acsd", r=["stA"], w=["acsd"])
            P.dma("sp", gsd, stB[0:16, :], "gsd", r=["stB"], w=["gsd"])
            P.dma("sp", gl, stB[0:16, NT - 1:NT], "gl", r=["stB"], w=["gl"])
            for c in range(16):
                bk = 2 + c % 2
                P.tr(ps[bk][:, 0:96], stA[:, c * 128:(c + 1) * 128], cst[0:96, K_ID:K_ID + 96], r=["stA", "cst"], w=[f"ps{bk}"])
                P.cp(tmA[:, c, :], ps[bk][:, 0:96], r=[f"ps{bk}"], w=["tmA"], eng=("act" if c % 2 else "dve"))

            pbank = [0]

            def conv_chunk(s, cols, jc, dest, dname):
                for tg in range(4):
                    bk = pbank[0] % 2
                    pbank[0] += 1
                    for kc in range(16):
                        P.mm(ps[bk][:, :], wb[s][:, kc, cols:cols + 128], hT[:, kc, tg * 512:(tg + 1) * 512],
                             start=(kc == 0), stop=(kc == 15), r=[f"wb{s}", "hT"], w=[f"ps{bk}"])
                    P.cp(ub[:, 3 + tg * 512:3 + (tg + 1) * 512], ps[bk][:, :], r=[f"ps{bk}"], w=["ub"],
                         eng=("act" if tg % 2 else "dve"))
                bk = pbank[0] % 2
                pbank[0] += 1
                for kc in range(16):
                    P.mm(ps[bk][:, 0:4], wb[s][:, kc, cols:cols + 128], hTh[:, kc, :],
                         start=(kc == 0), stop=(kc == 15), r=[f"wb{s}", "hTh"], w=[f"ps{bk}"])
                P.cp(ub[:, 0:3], ps[bk][:, 1:4], r=[f"ps{bk}"], w=["ub"])
                wc = lambda k: pp[:, C_CONVW + k * 12 + jc:C_CONVW + k * 12 + jc + 1]
                P.ts(xTp[:], ub[:, 0:NT], wc(0), pp[:, C_CONVB + jc:C_CONVB + jc + 1], ALU.mult, ALU.add,
                     r=["ub", "pp"], w=["xTp"])
                for k in range(1, 4):
                    P.stt(xTp[:], ub[:, k:k + NT], wc(k), xTp[:], ALU.mult, ALU.add, r=["ub", "pp", "xTp"], w=["xTp"])
                P.act(dest, xTp[:], AF.Silu, r=["xTp"], w=[dname])

            P.dma("pool", wb[1][:], w_in_v[:, :, 6144:6656], "wb1", w=["wb1"])
            for g in range(2):
                P.dma("pool", wb[0][:], w_in_v[:, :, 5120 + g * 512:5120 + (g + 1) * 512], "wb0", w=["wb0"])
                conv_chunk(1, g * 128, 8 + g, BT[:], "BT")
                conv_chunk(1, 256 + g * 128, 10 + g, CTall[:, g, :], "CT")
                for c4 in range(4):
                    bk = 2 + c4 % 2
                    for cc in range(4):
                        c = c4 * 4 + cc
                        P.mm(ps[bk][:, cc * 128:(cc + 1) * 128], BT[:, c * 128:(c + 1) * 128],
                             CTall[:, g, c * 128:(c + 1) * 128], r=["BT", "CT"], w=[f"ps{bk}"], inc=(cc == 3))
                    P.cp(cbT[:, c4 * 512:(c4 + 1) * 512], ps[bk][:, :], r=[f"ps{bk}"], w=["cbT"],
                         eng=("act" if c4 % 2 else "dve"))
                for c8 in range(2):
                    for cc in range(8):
                        c = c8 * 8 + cc
                        P.tr(psb[:, cc * 128:(cc + 1) * 128], BT[:, c * 128:(c + 1) * 128], idb[:], r=["BT", "idb"],
                             w=["ps7b0", "ps7b1"], inc=(cc == 7))
                    P.cp(Btm[:, c8 * 8:(c8 + 1) * 8, :], psb[:, :].rearrange("p (c n) -> p c n", n=128),
                         r=["ps7b0", "ps7b1"], w=["Btm"], eng="act")
                for jp in range(4):
                    j = 4 * g + jp
                    conv_chunk(0, jp * 128, j, xTp[:], "xTp")
                    h0, h1 = 2 * j, 2 * j + 1
                    jpar = j % 2

                    def stage_a(c, j=j, g=g):
                        cs = slice(c * 128, (c + 1) * 128)
                        q4, c4i = c // 4, c % 4
                        sl = q4 % 2
                        if c4i == 0:
                            for hh in range(2):
                                h = 2 * j + hh
                                src = bass.AP(acsd.tensor, h * NT + q4 * 512, [[0, 128], [1, 512]])
                                P.dma("sp", abc[hh][sl][:], src, f"abc{hh}{sl}", r=["acsd"], w=[f"abc{hh}{sl}"])
                                P.act(ebt[:], abc[hh][sl][:], AF.Exp, r=[f"abc{hh}{sl}"], w=["ebt"])
                                P.tt(cht[hh][sl][:], CTall[:, g, q4 * 512:(q4 + 1) * 512], ebt[:], ALU.mult,
                                     r=["CT", "ebt"], w=[f"cht{hh}{sl}"])
                                P.cp(dcs[:, hh, q4 * 4:(q4 + 1) * 4], ebt[:].rearrange("p (c j) -> p c j", j=128)[:, :, 127],
                                     r=["ebt"], w=["dcs"])
                        k2, k4 = c % 2, c % 4
                        bk = 2 + k2
                        P.tr(ps[bk][:, 0:128], xTp[:, cs], ident, r=["xTp", "cst"], w=[f"ps{bk}"])
                        P.ts(xdA[k4][:, 0:64], ps[bk][:, 0:64], tmA[:, c, 32 + h0:33 + h0], None, ALU.mult,
                             r=[f"ps{bk}", "tmA"], w=[f"xdA{k4}"])
                        P.act(xdB[k4][:, 64:128], ps[bk][:, 64:128], AF.Copy, r=[f"ps{bk}", "tmA"], w=[f"xdB{k4}"],
                              scale=tmA[:, c, 32 + h1:33 + h1])
                        P.act(xw[k4][:, 0:64], ps[bk][:, 0:64], AF.Copy, r=[f"ps{bk}", "tmA"], w=[f"xw{k4}"],
                              scale=tmA[:, c, 64 + h0:65 + h0])
                        P.ts(xw[k4][:, 64:128], ps[bk][:, 64:128], tmA[:, c, 64 + h1:65 + h1], None, ALU.mult,
                             r=[f"ps{bk}", "tmA"], w=[f"xw{k4}"])
                        for hh in range(2):
                            h = 2 * j + hh
                            asl = abc[hh][sl][:, c4i * 128:(c4i + 1) * 128]
                            P.stt(Dm[hh][:], asl, tmA[:, c, h:h + 1], cst[:, K_NEG:K_NEG + 128], ALU.subtract, ALU.add,
                                  r=[f"abc{hh}{sl}", "tmA", "cst"], w=[f"Dm{hh}"])
                            P.act(Dm[hh][:], Dm[hh][:], AF.Exp, r=[f"Dm{hh}"], w=[f"Dm{hh}"])
                            P.tt(Mt[2 * k4 + hh][:], cbT[:, cs], Dm[hh][:], ALU.mult, r=["cbT", f"Dm{hh}"],
                                 w=[f"Mt{2 * k4 + hh}"])
                        P.mm(ps[4][:, k4 * 128:(k4 + 1) * 128], Btm[:, c, :], xw[k4][:], r=["Btm", f"xw{k4}"], w=[f"ps4D{k4}"])

                    def stage_b(c, j=j, g=g):
                        q4, c4i = c // 4, c % 4
                        sl = q4 % 2
                        k2, k4 = c % 2, c % 4
                        yb = 5 + q4 % 2
                        yo = ps[yb][:, c4i * 128:(c4i + 1) * 128]
                        P.mm(yo, xdA[k4][:], Mt[2 * k4][:], start=True, stop=False, r=[f"xdA{k4}", f"Mt{2 * k4}"], w=[f"ps{yb}"])
                        P.mm(yo, xdB[k4][:], Mt[2 * k4 + 1][:], start=False, stop=(c == 0), r=[f"xdB{k4}", f"Mt{2 * k4 + 1}"],
                             w=[f"ps{yb}"])
                        jpar = j % 2
                        ci, ni = 2 * jpar + c % 2, 2 * jpar + (c + 1) % 2
                        Scur, Snxt = Spp[ci][:, :], Spp[ni][:, :]
                        if c > 0:
                            P.mm(yo, prA[ci][:], cht[0][sl][:, c4i * 128:(c4i + 1) * 128], start=False, stop=False,
                                 r=[f"prA{ci}", f"cht0{sl}"], w=[f"ps{yb}"])
                            P.mm(yo, prB[ci][:], cht[1][sl][:, c4i * 128:(c4i + 1) * 128], start=False, stop=True,
                                 r=[f"prB{ci}", f"cht1{sl}"], w=[f"ps{yb}"])
                        if c == 0:
                            P.cp(Snxt, ps[4][:, k4 * 128:(k4 + 1) * 128], r=[f"ps4D{k4}"], w=[f"Spp{ni}"])
                        else:
                            for hh in range(2):
                                P.ts(Snxt[:, hh * 64:(hh + 1) * 64], Scur[:, hh * 64:(hh + 1) * 64], dcs[:, hh, c:c + 1], None,
                                     ALU.mult, r=[f"Spp{ci}", "dcs"], w=[f"Spp{ni}"])
                            P.tt(Snxt, Snxt, ps[4][:, k4 * 128:(k4 + 1) * 128], ALU.add, r=[f"Spp{ni}", f"ps4D{k4}"], w=[f"Spp{ni}"])
                        if c < 15:
                            P.cp(prA[ni][:, 0:64], Snxt[:, 0:64], r=[f"Spp{ni}"], w=[f"prA{ni}"], eng="act")
                            P.cp(prB[ni][:, 64:128], Snxt[:, 64:128], r=[f"Spp{ni}"], w=[f"prB{ni}"], eng="act")
                        if c4i == 3:
                            tsl = slice(q4 * 512, (q4 + 1) * 512)
                            P.stt(yst[q4 % 2][:], xTp[:, tsl], pp[:, C_DSK + j:C_DSK + j + 1], ps[yb][:, :], ALU.mult, ALU.add,
                                  r=["xTp", "pp", f"ps{yb}"], w=[f"yst{q4 % 2}"])
                            P.dma("sp", olocd[8 + j][:, tsl], yst[q4 % 2][:], f"yst{q4 % 2}", r=[f"yst{q4 % 2}"],
                                  w=[f"olocd{8 + j}"])

                    LA = 2
                    for c in range(LA):
                        stage_a(c)
                    for c in range(16):
                        if c + LA < 16:
                            stage_a(c + LA)
                        stage_b(c)
            P.flush()

        mixT = sb("mixT", [128, 16, NT], BF16, mid)
        with ExitStack() as ph:
            wb1_ = sb("wb0", [128, 16, 512], BF16, ph)
            wb = [wb1_, wb1_]
            gs = [sb(f"gs{i}", [128, 512], F32, ph) for i in range(2)]
            ol = [sb(f"ol{i}", [128, 512], F32, ph) for i in range(2)]
            ql = [sb(f"ql{i}", [128, 512], BF16, ph) for i in range(2)]
            sq = [sb(f"sq{i}", [128, 512], F32, ph) for i in range(2)]
            rs = [sb(f"rs{i}", [128, 512], F32, ph) for i in range(2)]
            def run_tasks3(tasks, width=2):
                pending = list(tasks)
                active = [None] * width
                while pending or any(a is not None for a in active):
                    for s_ in range(width):
                        if active[s_] is None and pending:
                            active[s_] = pending.pop(0)(s_)
                        if active[s_] is not None:
                            try:
                                next(active[s_])
                            except StopIteration:
                                active[s_] = None

            def hg_item(h, hh, tg):
                def gen(k):
                    tsl = slice(tg * 512, (tg + 1) * 512)
                    bk = k
                    P.dma("sp", ol[k][:], olocd[h][:, tsl], f"ol{k}", r=[f"olocd{h}"], w=[f"ol{k}"])
                    P.dma("sp", ql[k][:], qhd[h][:, tsl], f"ql{k}", r=[f"qhd{h}"], w=[f"ql{k}"])
                    for kc in range(16):
                        P.mm(ps[bk][:, :], wb[0][:, kc, hh * 128:(hh + 1) * 128], hT[:, kc, tsl],
                             start=(kc == 0), stop=(kc == 15), r=["wb0", "hT"], w=[f"ps{bk}"])
                    yield
                    P.act(gs[k][:], ps[bk][:, :], AF.Silu, r=[f"ps{bk}"], w=[f"gs{k}"])
                    P.mm(ps[2 + k][:, :], Sinb[:, h, :], ql[k][:], r=["Sinb", f"ql{k}"], w=[f"ps{2 + k}"])
                    yield
                    P.tt(ol[k][:], ps[2 + k][:, :], ol[k][:], ALU.add, r=[f"ps{2 + k}", f"ol{k}"], w=[f"ol{k}"])
                    yield
                    P.act(sq[k][:], ol[k][:], AF.Square, r=[f"ol{k}"], w=[f"sq{k}"])
                    yield
                    P.mm(ps[4 + k][:, :], ones, sq[k][:], r=["cst", f"sq{k}"], w=[f"ps{4 + k}"])
                    yield
                    P.ts(rs[k][:], ps[4 + k][:, :], 1.0 / 128, EPS, ALU.mult, ALU.add, r=[f"ps{4 + k}"], w=[f"rs{k}"])
                    yield
                    P.act(rs[k][:], rs[k][:], AF.Sqrt, r=[f"rs{k}"], w=[f"rs{k}"])
                    yield
                    P.recip(rs[k][:], rs[k][:], r=[f"rs{k}"], w=[f"rs{k}"])
                    yield
                    P.tt(ol[k][:], ol[k][:], rs[k][:], ALU.mult, r=[f"ol{k}", f"rs{k}"], w=[f"ol{k}"])
                    yield
                    P.stt(mixT[:, h, tsl], ol[k][:], pp[:, C_HNW + h:C_HNW + h + 1], gs[k][:], ALU.mult, ALU.mult,
                          r=[f"ol{k}", "pp", f"gs{k}"], w=["mixT"])
                    yield
                return gen

            for hg in range(2):
                P.dma("pool", wb[0][:], w_in_v[:, :, 3072 + hg * 512:3072 + (hg + 1) * 512], "wb0", w=["wb0"])
                run_tasks3([hg_item(hg * 4 + hh, hh, tg) for hh in range(4) for tg in range(4)])

            zs = gs
            gb = [[sb(f"gb{a}{i}", [128, 512], F32, ph) for i in range(2)] for a in range(2)]
            ch3 = [sb(f"ch3{i}", [128, 512], BF16, ph) for i in range(4)]
            yz1_ = sb("yz", [128, 4, 512], F32, ph)

            def ssd_item(g, tg, jp, yk):
                j = 4 * g + jp

                def gen(k):
                    tsl = slice(tg * 512, (tg + 1) * 512)
                    bk = k
                    P.dma("sp", ol[k][:], olocd[8 + j][:, tsl], f"ol{k}", r=[f"olocd{8 + j}"], w=[f"ol{k}"])
                    for hh in range(2):
                        h = 2 * j + hh
                        src = bass.AP(gsd.tensor, h * NT + tg * 512, [[0, 128], [1, 512]])
                        P.dma("sp", gb[k][hh][:], src, f"gb{k}{hh}", r=["gsd"], w=[f"gb{k}{hh}"])
                    for kc in range(16):
                        P.mm(ps[bk][:, :], wb[0][:, kc, jp * 128:(jp + 1) * 128], hT[:, kc, tsl],
                             start=(kc == 0), stop=(kc == 15), r=["wb0", "hT"], w=[f"ps{bk}"])
                    yield
                    P.act(zs[k][:], ps[bk][:, :], AF.Silu, r=[f"ps{bk}"], w=[f"gs{k}"])
                    yield
                    for hh in range(2):
                        P.act(gb[k][hh][:], gb[k][hh][:], AF.Exp, r=[f"gb{k}{hh}"], w=[f"gb{k}{hh}"])
                        P.tt(ch3[2 * k + hh][:], CTall[:, g, tsl], gb[k][hh][:], ALU.mult, r=["CT", f"gb{k}{hh}"],
                             w=[f"ch3{2 * k + hh}"])
                        yield
                    P.mm(ps[2 + k][:, :], SinA[:, j, :], ch3[2 * k][:], start=True, stop=False,
                         r=["SinA", f"ch3{2 * k}"], w=[f"ps{2 + k}"])
                    P.mm(ps[2 + k][:, :], SinB[:, j, :], ch3[2 * k + 1][:], start=False, stop=True,
                         r=["SinB", f"ch3{2 * k + 1}"], w=[f"ps{2 + k}"])
                    yield
                    P.tt(ol[k][:], ps[2 + k][:, :], ol[k][:], ALU.add, r=[f"ps{2 + k}", f"ol{k}"], w=[f"ol{k}"])
                    yield
                    P.tt(yz1_[:, jp, :], ol[k][:], zs[k][:], ALU.mult, r=[f"ol{k}", f"gs{k}"], w=[f"yz{jp}"])
                    yield
                    P.act(sq[k][:], yz1_[:, jp, :], AF.Square, r=[f"yz{jp}"], w=[f"sq{k}"])
                    yield
                    P.mm(ps[4 + yk][:, :], ones, sq[k][:], start=(jp == 0), stop=(jp == 3), r=["cst", f"sq{k}"],
                         w=[f"ps{4 + yk}"])
                    yield
                return gen

            for g in range(2):
                P.dma("pool", wb[0][:], w_in_v[:, :, 4096 + g * 512:4096 + (g + 1) * 512], "wb0", w=["wb0"])
                for tg in range(4):
                    tsl = slice(tg * 512, (tg + 1) * 512)
                    yk = tg % 2
                    run_tasks3([ssd_item(g, tg, jp, yk) for jp in range(4)])
                    P.ts(rs[yk][:], ps[4 + yk][:, :], 1.0 / 512, EPS, ALU.mult, ALU.add, r=[f"ps{4 + yk}"], w=[f"rs{yk}"])
                    P.act(rs[yk][:], rs[yk][:], AF.Sqrt, r=[f"rs{yk}"], w=[f"rs{yk}"])
                    P.recip(rs[yk][:], rs[yk][:], r=[f"rs{yk}"], w=[f"rs{yk}"])
                    for jp in range(4):
                        j = 4 * g + jp
                        P.stt(mixT[:, 8 + j, tsl], yz1_[:, jp, :], pp[:, C_SNW + j:C_SNW + j + 1], rs[yk][:], ALU.mult, ALU.mult,
                              r=[f"yz{jp}", "pp", f"rs{yk}"], w=["mixT"])
            P.flush()

        if dbg == "mixT":
            with ExitStack() as ph:
                stg = [sb(f"stg{i}", [128, NT], F32, ph) for i in range(2)]
                for j in range(16):
                    P.cp(stg[j % 2][:], mixT[:, j, :], r=["mixT"], w=[f"stg{j % 2}"])
                    P.dma("sp", dbg_d[:, j, :], stg[j % 2][:], f"stg{j % 2}", r=[f"stg{j % 2}"], w=["dbgd"])
                P.flush()

        with ExitStack() as ph:
            pw0 = sb("pw0", [128, D], F32, ph)
            xt = [sb(f"xr{j}", [128, D], F32, ph) for j in range(2)]
            x1t = [sb(f"x1t{j}", [128, D], F32, ph) for j in range(2)]
            junk = sb("junk", [128, 512], BF16, ph)
            st = sb("st4", [128, 16], F32, ph)
            wo = hT
            w_out_v = w_out.rearrange("(k p) n -> p k n", p=128)
            P.dma("sp", pw0[:], pw_d[:, 0, :], "pw0", w=["pw0"])
            for b4 in range(4):
                P.dma("pool", wo[:, :, b4 * 512:(b4 + 1) * 512], w_out_v[:, :, b4 * 512:(b4 + 1) * 512], f"wo{b4}",
                      r=[], w=["hT"])
            for ti in range(16):
                k2 = ti % 2
                xb = f"xr{k2}"
                P.dma("sp", xt[k2][:], xin[128 + ti * 128:128 + (ti + 1) * 128, :], xb, w=[xb])
                for b4 in range(4):
                    bk = 4 * k2 + b4
                    bname = f"ps{bk}q"
                    bank = ps[bk]
                    for kc in range(16):
                        P.mm(bank[:, :], mixT[:, kc, ti * 128:(ti + 1) * 128], wo[:, kc, b4 * 512:(b4 + 1) * 512],
                             start=(kc == 0), stop=(kc == 15), r=["mixT", "hT"], w=[bname])
                    P.act(junk[:], bank[:, :], AF.Square, r=[bname], w=["junk4", f"st4{k2}"],
                          accum_out=st[:, 8 * k2 + b4:8 * k2 + b4 + 1])
                sv = st[:, 8 * k2:8 * k2 + 8]
                P.tt(sv[:, 4:5], sv[:, 0:1], sv[:, 1:2], ALU.add, r=[f"st4{k2}"], w=[f"st4{k2}"])
                P.tt(sv[:, 5:6], sv[:, 2:3], sv[:, 3:4], ALU.add, r=[f"st4{k2}"], w=[f"st4{k2}"])
                P.tt(sv[:, 4:5], sv[:, 4:5], sv[:, 5:6], ALU.add, r=[f"st4{k2}"], w=[f"st4{k2}"])
                P.ts(sv[:, 5:6], sv[:, 4:5], 1.0 / D, EPS, ALU.mult, ALU.add, r=[f"st4{k2}"], w=[f"st4{k2}"])
                P.act(sv[:, 6:7], sv[:, 5:6], AF.Sqrt, r=[f"st4{k2}"], w=[f"st4{k2}"])
                P.recip(sv[:, 7:8], sv[:, 6:7], r=[f"st4{k2}"], w=[f"st4{k2}"])
                for b4 in range(4):
                    bk = 4 * k2 + b4
                    bname = f"ps{bk}q"
                    bank = ps[bk]
                    csl = slice(b4 * 512, (b4 + 1) * 512)
                    P.stt(x1t[k2][:, csl], bank[:, :], sv[:, 7:8], pw0[:, csl], ALU.mult, ALU.mult,
                          r=[bname, f"st4{k2}", "pw0"], w=[f"x1t{k2}"])
                P.tt(x1t[k2][:], x1t[k2][:], xt[k2][:], ALU.add, r=[f"x1t{k2}", xb], w=[f"x1t{k2}"], eng="pool")
                P.dma("sp", x1d[ti * 128:(ti + 1) * 128, :], x1t[k2][:], f"x1t{k2}", r=[f"x1t{k2}"], w=["x1d"])
            P.flush()
        mid.close()
        sinst.close()

        with ExitStack() as ph:
            TG = 512
            pw1 = sb("pw1", [128, D], F32, ph)
            h2s = [sb(f"h2{i}", [128, 16, TG], BF16, ph) for i in range(2)]
            hid = sb("hid", [128, 44, TG], BF16, ph)
            wr = sb("wr", [128, 4, 16 * 512], BF16, ph)
            ff = [sb(f"ff{i}", [128, D], F32, ph) for i in range(4)]
            xt = [sb(f"xq{j}", [128, D], F32, ph) for j in range(2)]
            xa = xt
            sg = [sb(f"sg{j}", [128, TG], F32, ph) for j in range(2)]
            junk = sb("junk5", [128, D], BF16, ph)
            pjunk = [junk[:, 0:1024], junk[:, 1024:2048]]
            st = sb("st5", [128, 16], F32, ph)
            w_gate_v = w_gate.rearrange("(k p) n -> p k n", p=128)
            w_up_v = w_up.rearrange("(k p) n -> p k n", p=128)
            w_down_v = w_down.rearrange("(f p) n -> p f n", p=128)
            P.dma("sp", pw1[:], pw_d[:, 1, :], "pw1", w=["pw1"])

            def wslot(i):
                return wr[:, i, :].rearrange("p (k n) -> p k n", n=512)

            def dslot(i):
                return wr[:, 2 * i:2 * i + 2, :].rearrange("p a b -> p (a b)")[:, 0:44 * 256].rearrange("p (f n) -> p f n", n=256)

            for tgi in range(NT // TG):
                h2 = h2s[tgi % 2]
                h2n = f"h2{tgi % 2}"
                for rnd in range(2):
                    for jj in range(2):
                        j4 = rnd * 2 + jj
                        ti = tgi * 4 + j4
                        xs = xa[jj]
                        xb = f"xq{jj}"
                        P.dma("sp", xs[:], x1d[ti * 128:(ti + 1) * 128, :], xb, r=["x1d"], w=[xb])
                        ss = st[:, 4 * j4:4 * j4 + 1]
                        for hf in range(2):
                            P.act(pjunk[hf], xs[:, hf * 1024:(hf + 1) * 1024], AF.Square, r=[xb], w=["junk5", f"st5{j4}"],
                                  accum_out=st[:, 4 * j4 + 1 + hf:4 * j4 + 2 + hf])
                        P.tt(ss, st[:, 4 * j4 + 1:4 * j4 + 2], st[:, 4 * j4 + 2:4 * j4 + 3], ALU.add, r=[f"st5{j4}"], w=[f"st5{j4}"])
                        P.ts(st[:, 4 * j4 + 1:4 * j4 + 2], ss, 1.0 / D, EPS, ALU.mult, ALU.add, r=[f"st5{j4}"], w=[f"st5{j4}"])
                        P.act(st[:, 4 * j4 + 2:4 * j4 + 3], st[:, 4 * j4 + 1:4 * j4 + 2], AF.Sqrt, r=[f"st5{j4}"], w=[f"st5{j4}"])
                        P.recip(st[:, 4 * j4 + 3:4 * j4 + 4], st[:, 4 * j4 + 2:4 * j4 + 3], r=[f"st5{j4}"], w=[f"st5{j4}"])
                        P.act(xs[:], xs[:], AF.Copy, r=[xb, f"st5{j4}"], w=[xb], scale=st[:, 4 * j4 + 3:4 * j4 + 4])
                    for k2_ in range(8):
                        bk = k2_ % 4
                        for kk2 in range(2):
                            kc = 2 * k2_ + kk2
                            for jj in range(2):
                                P.tr(ps[bk][:, (kk2 * 2 + jj) * 128:(kk2 * 2 + jj + 1) * 128], xa[jj][:, kc * 128:(kc + 1) * 128], ident,
                                     r=[f"xq{jj}", "cst"], w=[f"ps{bk}"], inc=(kk2 == 1 and jj == 1))
                        for kk2 in range(2):
                            kc = 2 * k2_ + kk2
                            wcol = pp[:, C_PREFFN + kc:C_PREFFN + kc + 1]
                            o_ap = h2[:, kc, rnd * 256:(rnd + 1) * 256]
                            i_ap = ps[bk][:, kk2 * 256:(kk2 + 1) * 256]
                            if kk2 == 0:
                                P.ts(o_ap, i_ap, wcol, None, ALU.mult, r=[f"ps{bk}", "pp"], w=[h2n])
                            else:
                                P.act(o_ap, i_ap, AF.Copy, r=[f"ps{bk}", "pp"], w=[h2n], scale=wcol)
                for blk in range(11):
                    gsl, usl = (blk % 2) * 2, (blk % 2) * 2 + 1
                    P.dma("pool", wslot(gsl), w_gate_v[:, :, blk * 512:(blk + 1) * 512], f"wr{gsl}", w=[f"wr{gsl}"])
                    P.dma("pool", wslot(usl), w_up_v[:, :, blk * 512:(blk + 1) * 512], f"wr{usl}", w=[f"wr{usl}"])
                    for f4 in range(4):
                        fc = blk * 4 + f4
                        k2 = fc % 2
                        ga, ua = ps[k2], ps[2 + k2]
                        for kc in range(16):
                            P.mm(ga[:, :], wslot(gsl)[:, kc, f4 * 128:(f4 + 1) * 128], h2[:, kc, :],
                                 start=(kc == 0), stop=(kc == 15), r=[f"wr{gsl}", h2n], w=[f"ps{k2}"])
                        for kc in range(16):
                            P.mm(ua[:, :], wslot(usl)[:, kc, f4 * 128:(f4 + 1) * 128], h2[:, kc, :],
                                 start=(kc == 0), stop=(kc == 15), r=[f"wr{usl}", h2n], w=[f"ps{2 + k2}"])
                        P.act(sg[k2][:], ga[:, :], AF.Silu, r=[f"ps{k2}"], w=[f"sg{k2}"])
                        P.tt(hid[:, fc, :], sg[k2][:], ua[:, :], ALU.mult, r=[f"sg{k2}", f"ps{2 + k2}"], w=["hid"])
                for db in range(8):
                    ds_ = db % 2
                    P.dma("pool", dslot(ds_), w_down_v[:, :, db * 256:(db + 1) * 256], f"wd{ds_}",
                          w=[f"wr{2 * ds_}", f"wr{2 * ds_ + 1}"])
                    for j4 in range(4):
                        bk = 4 + (db * 4 + j4) % 2
                        for fc in range(44):
                            P.mm(ps[bk][:, 0:256], hid[:, fc, j4 * 128:(j4 + 1) * 128], dslot(ds_)[:, fc, :],
                                 start=(fc == 0), stop=(fc == 43), r=["hid", f"wr{2 * ds_}", f"wr{2 * ds_ + 1}"], w=[f"ps{bk}"])
                        P.cp(ff[j4][:, db * 256:(db + 1) * 256], ps[bk][:, 0:256], r=[f"ps{bk}"], w=[f"ff{j4}"],
                             eng=("act" if j4 % 2 else "dve"))
                for j4 in range(4):
                    ti = tgi * 4 + j4
                    k2 = j4 % 2
                    xb = f"xq{k2}"
                    P.dma("sp", xt[k2][:], x1d[ti * 128:(ti + 1) * 128, :], xb, r=["x1d"], w=[xb])
                    ss = st[:, 4 * j4:4 * j4 + 1]
                    for hf in range(2):
                        P.act(pjunk[hf], ff[j4][:, hf * 1024:(hf + 1) * 1024], AF.Square, r=[f"ff{j4}"], w=["junk5", f"st5{j4}"],
                              accum_out=st[:, 4 * j4 + 1 + hf:4 * j4 + 2 + hf])
                    P.tt(ss, st[:, 4 * j4 + 1:4 * j4 + 2], st[:, 4 * j4 + 2:4 * j4 + 3], ALU.add, r=[f"st5{j4}"], w=[f"st5{j4}"])
                    P.ts(st[:, 4 * j4 + 1:4 * j4 + 2], ss, 1.0 / D, EPS, ALU.mult, ALU.add, r=[f"st5{j4}"], w=[f"st5{j4}"])
                    P.act(st[:, 4 * j4 + 2:4 * j4 + 3], st[:, 4 * j4 + 1:4 * j4 + 2], AF.Sqrt, r=[f"st5{j4}"], w=[f"st5{j4}"])
                    P.recip(st[:, 4 * j4 + 3:4 * j4 + 4], st[:, 4 * j4 + 2:4 * j4 + 3], r=[f"st5{j4}"], w=[f"st5{j4}"])
                    P.stt(ff[j4][:], ff[j4][:], st[:, 4 * j4 + 3:4 * j4 + 4], pw1[:], ALU.mult, ALU.mult,
                          r=[f"ff{j4}", f"st5{j4}", "pw1"], w=[f"ff{j4}"])
                    P.tt(xt[k2][:], xt[k2][:], ff[j4][:], ALU.add, r=[xb, f"ff{j4}"], w=[xb], eng="pool")
                    P.dma("sp", y[ti * 128:(ti + 1) * 128, :], xt[k2][:], xb, r=[xb], w=["y"])
            P.flush()
    return nc


def make_inputs(inputs):
    x = np.asarray(inputs["x"], dtype=np.float32)
    g = lambda k: np.asarray(inputs[k], dtype=np.float32)
    pp = np.zeros((128, PPW), np.float32)
    pm = lambda v, n: np.ascontiguousarray(v.reshape(n, 128).T)
    pp[:, C_PREMIX:C_PREMIX + 16] = pm(g("pre_mix_norm_w")[0], 16)
    pp[:, C_PREFFN:C_PREFFN + 16] = pm(g("pre_ffn_norm_w")[0], 16)
    pp[:, C_L0:C_L0 + 8] = pm(g("lb_logits")[0], 8)
    pp[:, C_L1:C_L1 + 8] = pm(g("lb_logits")[1], 8)
    pp[:, C_HNW:C_HNW + 8] = pm(g("hgrn_norm_w")[0], 8)
    cw = g("conv_w")[0]
    for k in range(4):
        pp[:, C_CONVW + k * 12:C_CONVW + (k + 1) * 12] = pm(cw[k], 12)
    pp[:, C_CONVB:C_CONVB + 12] = pm(g("conv_b")[0], 12)
    pp[:, C_SNW:C_SNW + 8] = pm(g("ssd_norm_w")[0], 8)
    pp[:, C_DSK:C_DSK + 8] = pm(np.repeat(g("d_skip")[0], 64), 8)
    pp[32:48, C_DTB] = g("dt_bias")[0]
    pp[32:48, C_ALOG] = g("a_log")[0]
    pp[0:16, C_DTB] = g("dt_bias")[0]
    pp[0:16, C_ALOG] = g("a_log")[0]
    cst = np.zeros((128, CW), np.float32)
    cst[:, K_ID:K_ID + 128] = np.eye(128, dtype=np.float32)
    s = np.arange(128)[:, None]
    t = np.arange(128)[None, :]
    cst[:, K_NEG:K_NEG + 128] = np.where(s <= t, 0.0, -30000.0)
    tri = (np.arange(64)[:, None] <= np.arange(64)[None, :]).astype(np.float32)
    cst[0:64, K_TRE:K_TRE + 64] = tri
    cst[64:128, K_TRO:K_TRO + 64] = tri
    cst[:, K_ONE:K_ONE + 128] = 1.0
    pw = np.zeros((128, 2, D), np.float32)
    pw[:, 0, :] = g("post_mix_norm_w")[0][None, :]
    pw[:, 1, :] = g("post_ffn_norm_w")[0][None, :]
    shared = {"w_in": g("w_in")[0], "w_out": g("w_out")[0], "w_gate": g("w_gate")[0], "w_up": g("w_up")[0],
              "w_down": g("w_down")[0], "cst": cst, "pw": pw}
    maps = []
    for c in range(NCORES):
        b, q = c // 4, c % 4
        xi = np.zeros((NT + 128, D), np.float32)
        xi[128:] = x[b, q * NT:(q + 1) * NT]
        if q > 0:
            xi[:128] = x[b, q * NT - 128:q * NT]
        ppc = pp.copy()
        for j in range(NCORES):
            m = 1.0 if (j // 4 == b and j % 4 < q) else 0.0
            ppc[:, C_M + j] = m
            ppc[:, C_OM + j] = 1.0 - m
        xp = np.zeros((NW * NT, D), np.float32)
        for w in range(NW):
            qq = q - NW + w
            if qq >= 0:
                xp[w * NT:(w + 1) * NT] = x[b, qq * NT:(qq + 1) * NT]
                ppc[:, C_V + w] = 1.0
        d = dict(shared)
        d["xpre"] = xp
        d["xin"] = xi
        d["pp"] = ppc
        maps.append(d)
    return maps


def kernel(**inputs):
    maps = make_inputs(inputs)
    nc = build()
    res = run_bass_kernel_spmd(nc, maps, core_ids=list(range(NCORES)))
    out = np.zeros((2, 4 * NT, D), np.float32)
    for c in range(NCORES):
        out[c // 4, (c % 4) * NT:(c % 4 + 1) * NT] = res.results[c]["y"]
    return out
```

```python
import heapq
import numpy as np
from contextlib import ExitStack
import concourse.bass as bass
import concourse.mybir as mybir
from concourse.bass_utils import run_bass_kernel_spmd

F32 = mybir.dt.float32
BF16 = mybir.dt.bfloat16
ALU = mybir.AluOpType
AF = mybir.ActivationFunctionType

NCORES = 8
D = 2048
NT = 2048
NIN = 6672
DFF = 5632
EPS = 1e-6
HK = 8
XW = 2080
SAME_ENGINE_SYNC = True

C_PREMIX = 0
C_PREFFN = 16
C_L0 = 32
C_L1 = 40
C_HNW = 48
C_CONVW = 56
C_CONVB = 104
C_SNW = 116
C_DSK = 124
C_DTB = 132
C_ALOG = 133
C_M = 134
C_OM = 142
C_V = 150
PPW = 153
NW = 3
HW = 1
K_ID = 0
K_NEG = 128
K_TRE = 256
K_TRO = 320
K_ONE = 384
CW = 512


class Ev:
    __slots__ = ("key", "sem", "val")

    def __init__(self, key, sem, val):
        self.key, self.sem, self.val = key, sem, val


class Buf:
    __slots__ = ("name", "w", "r")

    def __init__(self, name):
        self.name, self.w, self.r = name, None, {}


class Unit:
    __slots__ = ("idx", "eng", "fns", "preds", "opreds", "dma", "dinc", "cost", "succ", "npred", "start", "finish", "ev")

    def __init__(self, idx, eng):
        self.idx, self.eng = idx, eng
        self.fns, self.preds = [], set()
        self.opreds = set()
        self.dma, self.dinc, self.cost = None, 16, 0.0
        self.succ, self.npred = [], 0
        self.start = self.finish = 0.0
        self.ev = None


class Prog:
    ENG = ("pe", "act", "dve", "pool", "sp")
    LAT_X, LAT_S = 700.0, 300.0

    def __init__(self, nc, es):
        self.nc = nc
        self.esem = {e: es.enter_context(nc.semaphore("sem_" + e)) for e in self.ENG}
        self.cnt = {e: 0 for e in self.ENG}
        self.waited = {e: {} for e in self.ENG}
        self.dsem = {}
        self.bufs = {}
        self.es = es
        self.units = []
        self.open_grp = {e: None for e in self.ENG}
        self.bank_last = {}
        self.all_dma_ev = {}
        self.nidx = 0

    def B(self, name):
        b = self.bufs.get(name)
        if b is None:
            b = self.bufs[name] = Buf(name)
        return b

    def _bl(self, x):
        return [self.B(n) if isinstance(n, str) else n for n in x]

    def op(self, eng, fn, r=(), w=(), inc=True, dma=None, dinc=16, cost=200.0):
        r, w = self._bl(r), self._bl(w)
        u = self.open_grp[eng] if dma is None else None
        if u is None:
            u = Unit(self.nidx, eng)
            self.nidx += 1
            self.units.append(u)
        u.fns.append(fn)
        u.cost += cost
        if dma is not None:
            u.dma, u.dinc = dma, dinc
        else:
            self.open_grp[eng] = None if inc else u

        def add(p):
            if p is not None and p is not u:
                u.preds.add(p)

        banks = []
        for b in r + w:
            if b.name.startswith("ps") and b.name[2:3].isdigit() and b.name[2] not in banks:
                banks.append(b.name[2])
        for b in r:
            add(b.w)
        for b in w:
            add(b.w)
            for p in b.r.values():
                add(p)
        for bn in banks:
            for e2, p in self.bank_last.setdefault(bn, {}).items():
                if e2 != eng:
                    add(p)
                elif p is not u:
                    u.opreds.add(p)
            self.bank_last[bn][eng] = u
        for b in r:
            b.r[u.idx] = u
        for b in w:
            b.w = u
            b.r = {}
        return u

    def flush(self):
        nc = self.nc
        units = self.units
        self.units = []
        self.open_grp = {e: None for e in self.ENG}
        inphase = set(id(u) for u in units)
        for u in units:
            u.preds = [p for p in u.preds if id(p) in inphase]
            u.opreds = [p for p in u.opreds if id(p) in inphase and p not in u.preds]
            u.npred = len(u.preds) + len(u.opreds)
            u.succ = []
        for u in units:
            for p in u.preds:
                p.succ.append(u)
            for p in u.opreds:
                p.succ.append(u)
        bl = {}
        for u in reversed(units):
            best = 0.0
            for v in u.succ:
                lat = self.LAT_S if (v.eng == u.eng and u.dma is None and v.dma is None) else self.LAT_X
                t = lat + bl[id(v)]
                if t > best:
                    best = t
            bl[id(u)] = best + u.cost
        eng_t = {e: 0.0 for e in self.ENG}
        avail = {e: [] for e in self.ENG}
        ready = {}
        for u in units:
            if u.npred == 0:
                ready[id(u)] = 0.0
                avail[u.eng].append(u)
        order = {e: [] for e in self.ENG}
        nleft = len(units)
        while nleft:
            best = None
            for e in self.ENG:
                lst = avail[e]
                if not lst:
                    continue
                t_e = eng_t[e]
                cand = None
                for u in lst:
                    s = ready[id(u)]
                    if s < t_e:
                        s = t_e
                    key = (s, -bl[id(u)], u.idx)
                    if cand is None or key < cand[0]:
                        cand = (key, u)
                if best is None or cand[0] < best[0]:
                    best = (cand[0], cand[1], e)
            key, u, e = best
            s = key[0]
            avail[e].remove(u)
            u.start = s
            if u.dma is not None:
                eng_t[e] = s + 60.0
                u.finish = s + u.cost
            else:
                u.finish = s + u.cost
                eng_t[e] = u.finish
            order[e].append(u)
            nleft -= 1
            for v in u.succ:
                lat = self.LAT_S if (v.eng == u.eng and u.dma is None and v.dma is None) else self.LAT_X
                t = u.finish + lat
                if ready.get(id(v), 0.0) < t:
                    ready[id(v)] = t
                v.npred -= 1
                if v.npred == 0:
                    avail[v.eng].append(v)
        for e in self.ENG:
            for u in order[e]:
                if u.dma is not None:
                    d = self.dsem.get(u.dma)
                    if d is None:
                        d = self.dsem[u.dma] = [self.es.enter_context(nc.semaphore("dq_" + u.dma)), 0]
                    d[1] += u.dinc
                    u.ev = Ev(("d", u.dma), d[0], d[1])
                    self.all_dma_ev[u.dma] = u.ev
                else:
                    self.cnt[e] += 1
                    u.ev = Ev(e, self.esem[e], self.cnt[e])
        streams = {}
        for e in self.ENG:
            lst = []
            for u in order[e]:
                need = {}
                for p in u.preds:
                    if p.dma is None and u.dma is None and p.eng == e and (e == "pe" or not SAME_ENGINE_SYNC):
                        continue
                    ev = p.ev
                    cur = need.get(ev.key)
                    if cur is None or cur[1] < ev.val:
                        need[ev.key] = (ev.sem, ev.val)
                waits = []
                for kk_, (s_, v_) in need.items():
                    if self.waited[e].get(kk_, 0) < v_:
                        self.waited[e][kk_] = v_
                        waits.append((s_, v_))
                lst.append((waits, u))
            streams[e] = lst
        tail = []
        for name, ev in self.all_dma_ev.items():
            if self.waited["sp"].get(ev.key, 0) < ev.val:
                self.waited["sp"][ev.key] = ev.val
                tail.append((ev.sem, ev.val))

        def run(eng, lst, extra=()):
            for waits, u in lst:
                for s_, v_ in waits:
                    eng.wait_ge(s_, v_)
                ins = None
                for fn in u.fns:
                    ins = fn(eng)
                if u.dma is not None and u.dinc == 1:
                    ins.then_inc(u.ev.sem)
                else:
                    ins.then_inc(u.ev.sem, u.dinc if u.dma is not None else 1)
            for s_, v_ in extra:
                eng.wait_ge(s_, v_)

        with nc.Block() as block:
            @block.tensor
            def _(e):
                run(e, streams["pe"])

            @block.scalar
            def _(e):
                run(e, streams["act"])

            @block.vector
            def _(e):
                run(e, streams["dve"])

            @block.gpsimd
            def _(e):
                run(e, streams["pool"])

            @block.sync
            def _(e):
                run(e, streams["sp"], tail)

    @staticmethod
    def _n(ap):
        n = 1
        for d in ap.shape[1:]:
            n *= int(d)
        return n

    def mm(self, out, lhsT, rhs, start=True, stop=True, r=(), w=(), inc=None):
        n = self._n(rhs)
        c = 35.0 + n / 2.0 * (4.0 if rhs.dtype == F32 else 1.0)
        return self.op("pe", lambda e: e.matmul(out, lhsT, rhs, start=start, stop=stop),
                       r, w, stop if inc is None else inc, cost=c)

    def tr(self, out, in_, ident, r=(), w=(), inc=True):
        return self.op("pe", lambda e: e.transpose(out, in_, ident), r, w, inc, cost=110.0)

    def act(self, out, in_, func, r=(), w=(), bias=None, scale=None, accum_out=None, eng="act"):
        kw = {}
        if bias is not None:
            kw["bias"] = bias
        if scale is not None:
            kw["scale"] = scale
        if accum_out is not None:
            kw["accum_out"] = accum_out
        c = 220.0 + 0.85 * self._n(in_) + (100.0 if accum_out is not None else 0.0)
        return self.op(eng, lambda e: e.activation(out, in_, func, **kw), r, w, cost=c)

    def _dc(self, out, n_in=1, eng="dve"):
        n = self._n(out)
        if eng == "pool":
            return 300.0 + 2.0 * n
        return 100.0 + (1.05 if n_in == 1 else 2.1) * n * (0.5 if out.dtype == BF16 and n_in == 1 else 1.0)

    def tt(self, out, in0, in1, op, r=(), w=(), eng="dve"):
        return self.op(eng, lambda e: e.tensor_tensor(out, in0, in1, op), r, w, cost=self._dc(out, 2, eng))

    def ts(self, out, in0, s1, s2, op0, op1=None, r=(), w=(), eng="dve"):
        c = self._dc(out, 1, eng)
        if op1 is None:
            return self.op(eng, lambda e: e.tensor_scalar(out, in0, s1, None, op0), r, w, cost=c)
        return self.op(eng, lambda e: e.tensor_scalar(out, in0, s1, s2, op0, op1), r, w, cost=c)

    def stt(self, out, in0, scalar, in1, op0, op1, r=(), w=()):
        return self.op("dve", lambda e: e.scalar_tensor_tensor(out, in0, scalar, in1, op0, op1), r, w,
                       cost=self._dc(out, 2))

    def scan(self, out, d0, d1, init, op0, op1, r=(), w=()):
        return self.op("dve", lambda e: e.tensor_tensor_scan(out, d0, d1, init, op0, op1), r, w,
                       cost=100.0 + 2.1 * self._n(out))

    def cp(self, out, in_, r=(), w=(), eng="dve"):
        if eng == "act":
            return self.op("act", lambda e: e.activation(out, in_, AF.Copy), r, w, cost=220.0 + 0.85 * self._n(out))
        return self.op(eng, lambda e: e.tensor_copy(out, in_), r, w, cost=self._dc(out, 1, eng))

    def recip(self, out, in_, r=(), w=()):
        return self.op("dve", lambda e: e.reciprocal(out, in_), r, w, cost=self._dc(out, 1))

    def memset(self, ap, val, w=(), eng="dve"):
        return self.op(eng, lambda e: e.memset(ap, val), (), w, cost=self._dc(ap, 1, eng))

    def dma(self, q, out, in_, sem, r=(), w=()):
        nb = 1
        for d in out.shape:
            nb *= int(d)
        c = 2200.0 + nb * (4 if in_.dtype == F32 else 2) / 250.0
        return self.op(q, lambda e: e.dma_start(out=out, in_=in_), r, w, dma=sem, cost=c)


def bc_last(ap2d, n):
    return ap2d.unsqueeze(2).broadcast_to([ap2d.shape[0], ap2d.shape[1], n])


def build(dbg=None, nocc=False, stop=None, psec="abc"):
    nc = bass.Bass("TRN2", target_bir_lowering=False)
    xin = nc.dram_tensor("xin", [NT + 128, D], F32, kind="ExternalInput").ap()
    xpre = nc.dram_tensor("xpre", [NW * NT, D], F32, kind="ExternalInput").ap()
    gl2 = nc.dram_tensor("gl2", [16, 1], F32, kind="Internal").ap()
    w_in = nc.dram_tensor("w_in", [D, NIN], F32, kind="ExternalInput").ap()
    w_out = nc.dram_tensor("w_out", [D, D], F32, kind="ExternalInput").ap()
    w_gate = nc.dram_tensor("w_gate", [D, DFF], F32, kind="ExternalInput").ap()
    w_up = nc.dram_tensor("w_up", [D, DFF], F32, kind="ExternalInput").ap()
    w_down = nc.dram_tensor("w_down", [DFF, D], F32, kind="ExternalInput").ap()
    pp_d = nc.dram_tensor("pp", [128, PPW], F32, kind="ExternalInput").ap()
    cst_d = nc.dram_tensor("cst", [128, CW], F32, kind="ExternalInput").ap()
    pw_d = nc.dram_tensor("pw", [128, 2, D], F32, kind="ExternalInput").ap()
    y = nc.dram_tensor("y", [NT, D], F32, kind="ExternalOutput").ap()
    if dbg:
        dbg_d = nc.dram_tensor("dbg", [128, 16, NT], F32, kind="ExternalOutput").ap()
    olocd = nc.dram_tensor("olocd", [16, 128, NT], F32, kind="Internal").ap()
    qhd = nc.dram_tensor("qhd", [8, 128, NT], BF16, kind="Internal").ap()
    acsd = nc.dram_tensor("acsd", [16, NT], F32, kind="Internal").ap()
    gsd = nc.dram_tensor("gsd", [16, NT], F32, kind="Internal").ap()
    x1d = nc.dram_tensor("x1d", [NT, D], F32, kind="Internal").ap()
    gl = nc.dram_tensor("gl", [16, 1], F32, kind="Internal").ap()
    xbuf = nc.dram_tensor("xbuf", [128, XW], F32, kind="Internal").ap()
    xg = nc.dram_tensor("xg", [NCORES * 128, XW], F32, addr_space="Local", kind="Internal").ap()

    w_in_v = w_in.rearrange("(k p) n -> p k n", p=128)

    with ExitStack() as es:
        P = Prog(nc, es)

        uid = [0]

        def sb(name, shape, dt, stack=None):
            uid[0] += 1
            return (stack or es).enter_context(nc.sbuf_tensor(f"{name}_s{uid[0]}", shape, dt))

        pp = sb("pp", [128, PPW], F32)
        cst = sb("cst", [128, CW], F32)
        idb = sb("idb", [128, 128], BF16)
        lbc = sb("lbc", [128, 16], F32)
        sinst = ExitStack()
        Sinb = sb("Sinb", [128, HK, 128], BF16, sinst)
        SinA = sb("SinA", [128, 8, 128], BF16, sinst)
        SinB = sb("SinB", [128, 8, 128], BF16, sinst)
        ps = [es.enter_context(nc.psum_tensor(f"ps{i}", [128, 512], F32)) for i in range(8)]
        psb = ps[7][:].bitcast(BF16)
        ps3b = ps[3][:].bitcast(BF16)
        ident = cst[:, K_ID:K_ID + 128]
        ones = cst[:, K_ONE:K_ONE + 128]

        with ExitStack() as ph:
            P.dma("sp", pp[:], pp_d, "pp", w=["pp"])
            P.dma("sp", cst[:], cst_d, "cst", w=["cst"])
            P.cp(idb[:], ident, r=["cst"], w=["idb"])
            t8 = sb("t8", [128, 32], F32, ph)
            l0, l1 = pp[:, C_L0:C_L0 + 8], pp[:, C_L1:C_L1 + 8]
            P.tt(t8[:, 0:8], l0, l1, ALU.max, r=["pp"], w=["t8"])
            P.tt(t8[:, 8:16], l0, t8[:, 0:8], ALU.subtract, r=["pp", "t8"], w=["t8"])
            P.tt(t8[:, 16:24], l1, t8[:, 0:8], ALU.subtract, r=["pp", "t8"], w=["t8"])
            P.act(t8[:, 8:24], t8[:, 8:24], AF.Exp, r=["t8"], w=["t8"])
            P.tt(t8[:, 24:32], t8[:, 8:16], t8[:, 16:24], ALU.add, r=["t8"], w=["t8"])
            P.recip(t8[:, 24:32], t8[:, 24:32], r=["t8"], w=["t8"])
            P.tt(lbc[:, 0:8], t8[:, 8:16], t8[:, 24:32], ALU.mult, r=["t8"], w=["lbc"])
            P.ts(lbc[:, 8:16], lbc[:, 0:8], -1.0, 1.0, ALU.mult, ALU.add, r=["lbc"], w=["lbc"])
            P.flush()

        with ExitStack() as ph:
            hTp = sb("hTp", [128, 16, NT], BF16, ph)
            T = [sb(f"T{j}", [128, D], F32, ph) for j in range(4)]
            st = sb("stp", [128, 16], F32, ph)
            wb = [sb(f"wbp{i}", [128, 16, 512], BF16, ph) for i in range(2)]
            v_tm = sb("v_tmp", [128, 16, 512], BF16, ph)
            kb = [sb(f"kb{i}", [128, NT], BF16, ph) for i in range(2)]
            ktmP = [sb(f"ktmP{i}", [128, 16, 128], BF16, ph) for i in range(2)]
            Sh = sb("Sh", [128, HK, 128], F32, ph)
            Ss = sb("Ss", [128, 16, 64], F32, ph)
            gsm = sb("gsm", [128, 4], F32, ph)
            ub = [sb(f"ubp{i}", [128, NT + 3], F32, ph) for i in range(2)]
            xwt = [sb(f"xwt{i}", [128, 16, 128], BF16, ph) for i in range(2)]
            wst = sb("wst", [128, 16, 16], F32, ph)
            wdt = sb("wdtp", [128, 16, 16], BF16, ph)
            aneg = sb("anegp", [16, 2], F32, ph)
            dsb = sb("dsbp", [128, 16], F32, ph)
            utail = sb("utail", [128, 10, 3], F32, ph)
            junk = kb[0]
            BT, Btm = kb[1], ktmP[1]
            P.memset(Sh[:], 0.0, w=["Sh"])
            P.memset(Ss[:], 0.0, w=["Ss"])
            P.memset(utail[:], 0.0, w=["utail"])
            P.dma("pool", wdt[:], w_in_v[:, :, 6656:6672], "wdtp", w=["wdtp"])
            P.act(aneg[:, 0:1], pp[0:16, C_ALOG:C_ALOG + 1], AF.Exp, r=["pp"], w=["anegp"])
            P.ts(aneg[:, 1:2], aneg[:, 0:1], -1.0, None, ALU.mult, r=["anegp"], w=["anegp"])
            ones_bc = cst[:, K_ONE:K_ONE + 1].broadcast_to([128, NT])

            def run_tasks(tasks, width=2):
                pending = list(tasks)
                active = [None] * width
                while pending or any(a is not None for a in active):
                    for s in range(width):
                        if active[s] is None and pending:
                            active[s] = pending.pop(0)(s)
                        if active[s] is not None:
                            try:
                                next(active[s])
                            except StopIteration:
                                active[s] = None

            PJ = [(0, 6), (2, 7)]
            AX = [1, 3]

            def v_task(hg):
                def gen(slot):
                    P.dma("pool", wb[0][:], w_in_v[:, :, 1024 + hg * 512:1024 + (hg + 1) * 512], "wbp0", w=["wbp0"])
                    P.dma("pool", wb[1][:], w_in_v[:, :, 2048 + hg * 512:2048 + (hg + 1) * 512], "wbp1", w=["wbp1"])
                    yield
                    for ti in range(16):
                        bk = 4 + ti % 2
                        for kc in range(16):
                            P.mm(ps[bk][:, :], hTp[:, kc, ti * 128:(ti + 1) * 128], wb[1][:, kc, :],
                                 start=(kc == 0), stop=(kc == 15), r=["wbp1", f"hTp_{ti // 4}"], w=[f"ps{bk}"])
                        P.cp(v_tm[:, ti, :], ps[bk][:, :], r=[f"ps{bk}"], w=["v_tmp"], eng=("act" if ti % 2 else "dve"))
                        yield
                return gen

            def transposes_bf(s, src, srcname, dst, dstname):
                ax = AX[s]
                axb = ps[ax][:].bitcast(BF16)
                for t8 in range(2):
                    for tt_ in range(8):
                        ti = t8 * 8 + tt_
                        P.tr(axb[:, tt_ * 128:(tt_ + 1) * 128], src[:, ti * 128:(ti + 1) * 128], idb[:],
                             r=[srcname, "idb"], w=[f"ps{ax}"], inc=(tt_ == 7))
                    P.cp(dst[:, t8 * 8:(t8 + 1) * 8, :], axb.rearrange("p (c n) -> p c n", n=128),
                         r=[f"ps{ax}"], w=[dstname], eng=("act" if t8 else "dve"))
                    yield

            def h_task(h):
                hh = h % 4

                def gen(s):
                    A, C = T[2 * s], T[2 * s + 1]
                    An, Cn = f"T{2 * s}", f"T{2 * s + 1}"
                    gcol = gsm[:, s:s + 1]
                    ax = AX[s]
                    for tg in range(4):
                        bk = PJ[s][tg % 2]
                        for kc in range(16):
                            P.mm(ps[bk][:, :], wb[0][:, kc, hh * 128:(hh + 1) * 128], hTp[:, kc, tg * 512:(tg + 1) * 512],
                                 start=(kc == 0), stop=(kc == 15), r=["wbp0", f"hTp_{tg}"], w=[f"ps{bk}"])
                        P.act(A[:, tg * 512:(tg + 1) * 512], ps[bk][:, :], AF.Sigmoid, r=[f"ps{bk}"], w=[An])
                        yield
                    P.act(A[:], A[:], AF.Ln, r=[An, "lbc"], w=[An], scale=lbc[:, 8 + h:9 + h], bias=lbc[:, h:h + 1])
                    yield
                    P.scan(C[:], ones_bc, A[:], 0.0, ALU.mult, ALU.add, r=["cst", An], w=[Cn])
                    yield
                    P.act(A[:], A[:], AF.Exp, r=[An], w=[An])
                    yield
                    P.act(A[:], A[:], AF.Identity, r=[An], w=[An], scale=-1.0, bias=1.0)
                    yield
                    P.cp(gcol, C[:, NT - 1:NT], r=[Cn], w=[f"gsm{s}"])
                    P.act(C[:], C[:], AF.Exp, r=[Cn, f"gsm{s}"], w=[Cn], scale=-1.0, bias=gcol)
                    yield
                    P.tt(kb[s][:], A[:], C[:], ALU.mult, r=[An, Cn], w=[f"kb{s}"])
                    P.act(gcol, gcol, AF.Exp, r=[f"gsm{s}"], w=[f"gsm{s}"])
                    yield
                    yield from transposes_bf(s, kb[s], f"kb{s}", ktmP[s], f"ktmP{s}")
                    for ti in range(16):
                        P.mm(ps[ax][:, 0:128], ktmP[s][:, ti, :], v_tm[:, ti, hh * 128:(hh + 1) * 128],
                             start=(ti == 0), stop=(ti == 15), r=[f"ktmP{s}", "v_tmp"], w=[f"ps{ax}"])
                    yield
                    P.stt(Sh[:, h, :], Sh[:, h, :], gcol, ps[ax][:, 0:128], ALU.mult, ALU.add,
                          r=[f"Sh{h}", f"gsm{s}", f"ps{ax}"], w=[f"Sh{h}"])
                    yield
                return gen

            def conv_task(w, s_w, cols, jc, kind, j):
                def gen(s):
                    acc = T[2 + s]
                    an = f"T{2 + s}"
                    ubs, ubn = ub[s], f"ubp{s}"
                    ax = AX[s]
                    for tg in range(4):
                        bk = PJ[s][tg % 2]
                        for kc in range(16):
                            P.mm(ps[bk][:, :], wb[s_w][:, kc, cols:cols + 128], hTp[:, kc, tg * 512:(tg + 1) * 512],
                                 start=(kc == 0), stop=(kc == 15), r=[f"wbp{s_w}", f"hTp_{tg}"], w=[f"ps{bk}"])
                        P.cp(ubs[:, 3 + tg * 512:3 + (tg + 1) * 512], ps[bk][:, :], r=[f"ps{bk}"], w=[ubn],
                             eng=("act" if tg % 2 else "dve"))
                        yield
                    P.cp(ubs[:, 0:3], utail[:, jc, :], r=["utail"], w=[ubn])
                    wc = lambda k_: pp[:, C_CONVW + k_ * 12 + jc:C_CONVW + k_ * 12 + jc + 1]
                    P.ts(acc[:], ubs[:, 0:NT], wc(0), pp[:, C_CONVB + jc:C_CONVB + jc + 1], ALU.mult, ALU.add,
                         r=[ubn, "pp"], w=[an])
                    yield
                    for k_ in range(1, 4):
                        P.stt(acc[:], ubs[:, k_:k_ + NT], wc(k_), acc[:], ALU.mult, ALU.add, r=[ubn, "pp", an], w=[an])
                        yield
                    P.cp(utail[:, jc, :], ubs[:, NT:NT + 3], r=[ubn], w=["utail"], eng="act")
                    if kind == "B":
                        P.act(BT[:], acc[:], AF.Silu, r=[an], w=["kb1"])
                        yield
                        yield from transposes_bf(s, BT, "kb1", Btm, "ktmP1")
                        return
                    P.act(acc[:], acc[:], AF.Silu, r=[an], w=[an])
                    yield
                    for t4 in range(4):
                        for c4 in range(4):
                            ti = t4 * 4 + c4
                            P.tr(ps[ax][:, c4 * 128:(c4 + 1) * 128], acc[:, ti * 128:(ti + 1) * 128], ident,
                                 r=[an, "cst"], w=[f"ps{ax}"], inc=(c4 == 3))
                        for c4 in range(4):
                            ti = t4 * 4 + c4
                            for hx in range(2):
                                P.ts(xwt[s][:, ti, hx * 64:(hx + 1) * 64], ps[ax][:, c4 * 128 + hx * 64:c4 * 128 + (hx + 1) * 64],
                                     wst[:, ti, 2 * j + hx:2 * j + hx + 1], None, ALU.mult,
                                     r=[f"ps{ax}", "wst"], w=[f"xwt{s}"], eng=("dve" if t4 % 2 else "pool") if False else "dve")
                        yield
                    for ti in range(16):
                        P.mm(ps[ax][:, 0:128], Btm[:, ti, :], xwt[s][:, ti, :],
                             start=(ti == 0), stop=(ti == 15), r=["ktmP1", f"xwt{s}"], w=[f"ps{ax}"])
                    yield
                    P.tt(Ss[:, 2 * j:2 * j + 2, :], Ss[:, 2 * j:2 * j + 2, :], bc_last(dsb[:, 2 * j:2 * j + 2], 64), ALU.mult,
                         r=[f"Ss{j}", "dsbp"], w=[f"Ss{j}"])
                    P.tt(Ss[:, 2 * j:2 * j + 2, :], Ss[:, 2 * j:2 * j + 2, :],
                         ps[ax][:, 0:128].rearrange("p (t v) -> p t v", v=64), ALU.add,
                         r=[f"Ss{j}", f"ps{ax}"], w=[f"Ss{j}"])
                    yield
                return gen

            evq = 0
            for w in range(NW):
                for g4 in range(4):
                    for j in range(4):
                        ti = w * 16 + g4 * 4 + j
                        xb = f"T{j}"
                        P.dma("sp", T[j][:], xpre[ti * 128:(ti + 1) * 128, :], xb, w=[xb])
                        ss = st[:, 4 * j:4 * j + 1]
                        P.act(junk[:], T[j][:], AF.Square, r=[xb], w=["kb0", f"stp{j}"], accum_out=ss)
                        P.ts(st[:, 4 * j + 1:4 * j + 2], ss, 1.0 / D, EPS, ALU.mult, ALU.add, r=[f"stp{j}"], w=[f"stp{j}"])
                        P.act(st[:, 4 * j + 2:4 * j + 3], st[:, 4 * j + 1:4 * j + 2], AF.Sqrt, r=[f"stp{j}"], w=[f"stp{j}"])
                        P.recip(st[:, 4 * j + 3:4 * j + 4], st[:, 4 * j + 2:4 * j + 3], r=[f"stp{j}"], w=[f"stp{j}"])
                        P.act(T[j][:], T[j][:], AF.Copy, r=[xb, f"stp{j}"], w=[xb], scale=st[:, 4 * j + 3:4 * j + 4])
                    for kc in range(16):
                        bk = kc % 4
                        for j in range(4):
                            P.tr(ps[bk][:, j * 128:(j + 1) * 128], T[j][:, kc * 128:(kc + 1) * 128], ident,
                                 r=[f"T{j}", "cst"], w=[f"ps{bk}"], inc=(j == 3))
                        wcol = pp[:, C_PREMIX + kc:C_PREMIX + kc + 1]
                        o_ap = hTp[:, kc, g4 * 512:(g4 + 1) * 512]
                        if evq % 2 == 0:
                            P.ts(o_ap, ps[bk][:, :], wcol, None, ALU.mult, r=[f"ps{bk}", "pp"], w=[f"hTp_{g4}"])
                        else:
                            P.act(o_ap, ps[bk][:, :], AF.Copy, r=[f"ps{bk}", "pp"], w=[f"hTp_{g4}"], scale=wcol)
                        evq += 1
                if w >= NW - HW and "b" in psec:
                    for hg in range(2):
                        run_tasks([v_task(hg)], width=1)
                        run_tasks([h_task(hg * 4 + hh) for hh in range(4)])
                if "c" not in psec:
                    continue
                P.dma("pool", wb[1][:], w_in_v[:, :, 6144:6656], "wbp1", w=["wbp1"])
                P.dma("pool", wb[0][:], w_in_v[:, :, 5120:5632], "wbp0", w=["wbp0"])
                for tg in range(4):
                    bk = 4 + tg % 2
                    for kc in range(16):
                        P.mm(ps[bk][0:16, :], wdt[:, kc, :], hTp[:, kc, tg * 512:(tg + 1) * 512],
                             start=(kc == 0), stop=(kc == 15), r=["wdtp", f"hTp_{tg}"], w=[f"ps{bk}"])
                    P.act(T[0][0:16, tg * 512:(tg + 1) * 512], ps[bk][0:16, :], AF.Exp, r=[f"ps{bk}", "pp"], w=["T0"],
                          bias=pp[0:16, C_DTB:C_DTB + 1])
                P.act(T[0][0:16, :], T[0][0:16, :], AF.Ln, r=["T0"], w=["T0"], bias=1.0)
                P.ts(T[0][0:16, :], T[0][0:16, :], pp[0:16, C_V + w:C_V + w + 1], None, ALU.mult, r=["T0", "pp"], w=["T0"])
                P.ts(T[1][0:16, :], T[0][0:16, :], aneg[:, 1:2], None, ALU.mult, r=["T0", "anegp"], w=["T1"])
                P.scan(T[0][64:80, :], ones_bc[0:16, :], T[1][0:16, :], 0.0, ALU.mult, ALU.add, r=["cst", "T1"], w=["T0"])
                Gs = T[0][64:80, :]
                P.dma("sp", gl2, Gs[:, NT - 1:NT], "gl2", r=["T0"], w=["gl2"])
                P.cp(gsm[64:80, 2:3], Gs[:, NT - 1:NT], r=["T0"], w=["gsm2"])
                P.act(T[1][64:80, :], Gs, AF.Exp, r=["T0", "gsm2"], w=["T1"], scale=-1.0, bias=gsm[64:80, 2:3])
                P.cp(T[1][0:16, :], T[1][64:80, :], r=["T1"], w=["T1"], eng="act")
                P.tt(T[1][0:16, :], T[1][0:16, :], T[0][0:16, :], ALU.mult, r=["T0", "T1"], w=["T1"])
                for ti in range(16):
                    P.tr(ps[3][:, ti * 16:(ti + 1) * 16], T[1][0:16, ti * 128:(ti + 1) * 128], cst[0:16, K_ID:K_ID + 16],
                         r=["T1", "cst"], w=["ps3"], inc=(ti == 15))
                P.cp(wst[:], ps[3][:, 0:256].rearrange("p (t h) -> p t h", h=16), r=["ps3"], w=["wst"])
                P.dma("sp", dsb[:], bass.AP(gl2.tensor, 0, [[0, 128], [1, 16]]), "dsbp", r=["gl2"], w=["dsbp"])
                P.act(dsb[:], dsb[:], AF.Exp, r=["dsbp"], w=["dsbp"])
                for g in range(2):
                    if g == 1:
                        P.dma("pool", wb[0][:], w_in_v[:, :, 5120 + 512:5120 + 1024], "wbp0", w=["wbp0"])
                    run_tasks([conv_task(w, 1, g * 128, 8 + g, "B", None)], width=1)
                    run_tasks([conv_task(w, 0, jp * 128, 4 * g + jp, "x", 4 * g + jp) for jp in range(4)])
            if dbg == "sin":
                P.dma("sp", dbg_d[:, 0, 0:1024], Sh[:].rearrange("p h v -> p (h v)"), "dbgs", r=[f"Sh{h}" for h in range(8)] + ["Sh"], w=["dbgd"])
                P.dma("sp", dbg_d[:, 1, 0:1024], Ss[:].rearrange("p h v -> p (h v)"), "dbgs", r=[f"Ss{j}" for j in range(8)] + ["Ss"], w=["dbgd"])
                P.flush()
                return nc
            P.cp(Sinb[:], Sh[:], r=[f"Sh{h}" for h in range(8)], w=["Sinb"], eng="act")
            P.memset(SinA[:], 0.0, w=["SinA"], eng="pool")
            P.memset(SinB[:], 0.0, w=["SinB"], eng="pool")
            Ss4 = Ss[:].rearrange("p (j t) v -> p j (t v)", t=2)
            P.cp(SinA[:, :, 0:64], Ss4[:, :, 0:64], r=[f"Ss{j}" for j in range(8)], w=["SinA"], eng="act")
            P.cp(SinB[:, :, 64:128], Ss4[:, :, 64:128], r=[f"Ss{j}" for j in range(8)], w=["SinB"], eng="act")
            P.flush()
            if stop == "P":
                return nc

        mid = ExitStack()
        hT = sb("hT", [128, 16, NT], BF16, mid)
        hTh = sb("hTh", [128, 16, 4], BF16, mid)
        with ExitStack() as ph:
            xt = [sb(f"xt{j}", [128, D], F32, ph) for j in range(8)]
            junk = sb("junk", [128, D], BF16, ph)
            st = sb("st", [128, 32], F32, ph)
            groups = [[0]] + [[1 + 4 * g + i for i in range(4)] for g in range(4)]
            evq = 0
            for gi, grp in enumerate(groups):
                for j0, ti in enumerate(grp):
                    j = (gi % 2) * 4 + j0
                    xb = f"xt{j}"
                    P.dma("sp", xt[j][:], xin[ti * 128:(ti + 1) * 128, :], xb, w=[xb])
                    ss = st[:, 4 * j:4 * j + 1]
                    P.act(junk[:], xt[j][:], AF.Square, r=[xb], w=["junk", f"st{j}"], accum_out=ss)
                    P.ts(st[:, 4 * j + 1:4 * j + 2], ss, 1.0 / D, EPS, ALU.mult, ALU.add, r=[f"st{j}"], w=[f"st{j}"])
                    P.act(st[:, 4 * j + 2:4 * j + 3], st[:, 4 * j + 1:4 * j + 2], AF.Sqrt, r=[f"st{j}"], w=[f"st{j}"])
                    P.recip(st[:, 4 * j + 3:4 * j + 4], st[:, 4 * j + 2:4 * j + 3], r=[f"st{j}"], w=[f"st{j}"])
                    P.act(xt[j][:], xt[j][:], AF.Copy, r=[xb, f"st{j}"], w=[xb], scale=st[:, 4 * j + 3:4 * j + 4])
                n = len(grp)
                for kc in range(16):
                    bk = kc % 8
                    for j0 in range(n):
                        j = (gi % 2) * 4 + j0
                        P.tr(ps[bk][:, j0 * 128:(j0 + 1) * 128], xt[j][:, kc * 128:(kc + 1) * 128], ident,
                             r=[f"xt{j}", "cst"], w=[f"ps{bk}"], inc=(j0 == n - 1))
                    wcol = pp[:, C_PREMIX + kc:C_PREMIX + kc + 1]
                    if gi == 0:
                        o_ap, i_ap, wb_ = hTh[:, kc, 0:4], ps[bk][:, 124:128], "hTh"
                    else:
                        c0 = (gi - 1) * 512
                        o_ap, i_ap, wb_ = hT[:, kc, c0:c0 + 512], ps[bk][:, 0:512], "hT"
                    if evq % 2 == 0:
                        P.ts(o_ap, i_ap, wcol, None, ALU.mult, r=[f"ps{bk}", "pp"], w=[wb_])
                    else:
                        P.act(o_ap, i_ap, AF.Copy, r=[f"ps{bk}", "pp"], w=[wb_], scale=wcol)
                    evq += 1
            P.flush()

        with ExitStack() as ph:
            wb = [sb(f"wb{i}", [128, 16, 512], BF16, ph) for i in range(3)]
            v_tm = sb("v_tm", [128, 16, 512], BF16, ph)
            qf = sb("qf", [128, NT], F32, ph)
            fg = sb("fg", [128, NT], F32, ph)
            kk = sb("kk", [128, NT], F32, ph)
            bb = sb("bb", [128, NT], F32, ph)
            GG = sb("GG", [128, NT], F32, ph)
            qt = [sb(f"qt{i}", [128, NT], BF16, ph) for i in range(2)]
            kt = [sb(f"kt{i}", [128, NT], BF16, ph) for i in range(2)]
            qh = sb("qh", [128, NT], BF16, ph)
            rm = sb("rm", [128, NT], BF16, ph)
            chs = [sb(f"chs{i}", [128, 3, 32], F32, ph) for i in range(2)]
            Sb = [sb(f"Sb{i}", [128, 128], BF16, ph) for i in range(4)]
            Sst = [sb(f"Sst{i}", [128, 128], F32, ph) for i in range(4)]
            scm = [sb(f"scm{i}", [128, 64], BF16, ph) for i in range(4)]
            ktm = [sb(f"ktm{i}", [128, 128], BF16, ph) for i in range(4)]
            ost1_ = sb("ost0", [128, 256], F32, ph)
            ost = [ost1_, ost1_]

            P.memset(rm[:], 1.0, w=["rm"])
            P.memset(rm[:].rearrange("p (c j) -> p c j", j=64)[:, :, 0:1], 0.0, w=["rm"])
            for i in range(4):
                P.memset(ktm[i][:], 0.0, w=[f"ktm{i}"], eng="pool")
            wslot = [0]

            def load_w(c0):
                s = wslot[0] % 3
                wslot[0] += 1
                P.dma("pool", wb[s][:], w_in_v[:, :, c0:c0 + 512], f"wb{s}", w=[f"wb{s}"])
                return s

            pbank = [0]

            def proj_fm(s, cols, tg, w128=128):
                bk = pbank[0] % 2
                pbank[0] += 1
                for kc in range(16):
                    P.mm(ps[bk][0:w128, :], wb[s][:, kc, cols:cols + w128], hT[:, kc, tg * 512:(tg + 1) * 512],
                         start=(kc == 0), stop=(kc == 15), r=[f"wb{s}", "hT"], w=[f"ps{bk}"])
                return bk

            for hg in range(2):
                sq = load_w(hg * 512)
                sf = load_w(1024 + hg * 512)
                si = load_w(2048 + hg * 512)
                for ti in range(16):
                    bk = pbank[0] % 2
                    pbank[0] += 1
                    for kc in range(16):
                        P.mm(ps[bk][:, :], hT[:, kc, ti * 128:(ti + 1) * 128], wb[si][:, kc, :],
                             start=(kc == 0), stop=(kc == 15), r=[f"wb{si}", "hT"], w=[f"ps{bk}"])
                    P.cp(v_tm[:, ti, :], ps[bk][:, :], r=[f"ps{bk}"], w=["v_tm"], eng=("act" if ti % 2 else "dve"))
                def s1_task(h, hh, sq=sq, sf=sf):
                    p = h % 2
                    qt_, kt_, chs_ = qt[p], kt[p], chs[p]
                    qn, kn, cn = f"qt{p}", f"kt{p}", f"chs{p}"

                    def gen():
                        HL = NT // 2
                        for tg in range(4):
                            bk = proj_fm(sq, hh * 128, tg)
                            P.act(qf[:, tg * 512:(tg + 1) * 512], ps[bk][:, :], AF.Silu, r=[f"ps{bk}"], w=[f"qf_{tg // 2}"])
                        for tg in range(4):
                            bk = proj_fm(sf, hh * 128, tg)
                            P.act(fg[:, tg * 512:(tg + 1) * 512], ps[bk][:, :], AF.Sigmoid, r=[f"ps{bk}"], w=[f"fg_{tg // 2}"])
                        for hf in range(2):
                            sl = slice(hf * HL, (hf + 1) * HL)
                            fn_, kn_, bn_, gn_ = f"fg_{hf}", f"kk_{hf}", f"bb_{hf}", f"GG_{hf}"
                            P.ts(fg[:, sl], fg[:, sl], lbc[:, 8 + h:9 + h], lbc[:, h:h + 1], ALU.mult, ALU.add, r=[fn_, "lbc"], w=[fn_])
                            P.ts(kk[:, sl], fg[:, sl], -1.0, 1.0, ALU.mult, ALU.add, r=[fn_], w=[kn_])
                            P.act(fg[:, sl], fg[:, sl], AF.Ln, r=[fn_], w=[fn_])
                            P.scan(bb[:, sl], rm[:, sl], fg[:, sl], 0.0, ALU.mult, ALU.add, r=["rm", fn_], w=[bn_])
                            init = 0.0 if hf == 0 else GG[:, HL - 1:HL]
                            P.scan(GG[:, sl], cst[:, K_ONE:K_ONE + 1].broadcast_to([128, HL]), fg[:, sl], init, ALU.mult, ALU.add,
                                   r=["cst", fn_] + (["GGlast"] if hf else []), w=[gn_] + (["GGlast"] if hf == 0 else []))
                        for hf in range(2):
                            sl = slice(hf * HL, (hf + 1) * HL)
                            gn_ = f"GG_{hf}"
                            P.act(GG[:, sl], GG[:, sl], AF.Exp, r=[gn_] + (["GGlast"] if hf == 0 else []), w=[gn_] + (["GGlast"] if hf == 0 else []))
                            P.stt(qh[:, sl], qf[:, sl], 128.0 ** -0.5, GG[:, sl], ALU.mult, ALU.mult, r=[f"qf_{hf}", gn_], w=[f"qh_{hf}"])
                        P.dma("sp", qhd[h], qh[:], "qhd", r=["qh_0", "qh_1"], w=[f"qhd{h}"])
                        for hf in range(2):
                            sl = slice(hf * HL, (hf + 1) * HL)
                            cs_ = slice(hf * 16, (hf + 1) * 16)
                            fn_, kn_, bn_, gn_ = f"fg_{hf}", f"kk_{hf}", f"bb_{hf}", f"GG_{hf}"
                            cnh = f"{cn}_{hf}"
                            b3 = bb[:, sl].rearrange("p (c j) -> p c j", j=64)
                            P.act(chs_[:, 0, cs_], b3[:, :, 63], AF.Exp, r=[bn_], w=[cnh])
                            P.act(chs_[:, 2, cs_], b3[:, :, 31], AF.Exp, r=[bn_], w=[cnh])
                            f3 = fg[:, sl].rearrange("p (c j) -> p c j", j=64)
                            P.tt(f3, b3, bc_last(b3[:, :, 31], 64), ALU.subtract, r=[bn_], w=[fn_])
                            P.act(chs_[:, 1, cs_], f3[:, :, 63], AF.Exp, r=[fn_], w=[cnh])
                            P.act(bb[:, sl], fg[:, sl], AF.Exp, r=[fn_], w=[bn_])
                            P.act(GG[:, sl], fg[:, sl], AF.Exp, r=[fn_], w=[gn_], scale=-1.0)
                            P.stt(qt_[:, sl], qf[:, sl], 128.0 ** -0.5, bb[:, sl], ALU.mult, ALU.mult, r=[f"qf_{hf}", bn_], w=[f"{qn}_{hf}"])
                            P.tt(kt_[:, sl], kk[:, sl], GG[:, sl], ALU.mult, r=[kn_, gn_], w=[f"{kn}_{hf}"])
                        yield
                    return gen()

                def s2_task(h, hh):
                    p = h % 2
                    qt_, kt_, chs_ = qt[p], kt[p], chs[p]
                    qn, kn, cn = f"qt{p}", f"kt{p}", f"chs{p}"

                    def gen():

                        def stage_a(c):
                            ti, half = c // 2, c % 2
                            t0 = c * 64
                            tsl = ti % 2
                            if half == 0:
                                tb = ps3b[:, 512 + tsl * 128:512 + (tsl + 1) * 128]
                                P.tr(tb, kt_[:, ti * 128:(ti + 1) * 128], idb[:],
                                     r=[f"{kn}_{c // 16}", "idb"], w=[f"ps3b{tsl}"])
                                P.cp(ktm[2 * tsl][0:64, :], ps3b[0:64, 512 + tsl * 128:512 + (tsl + 1) * 128], r=[f"ps3b{tsl}"],
                                     w=[f"ktm{2 * tsl}"], eng="act")
                                P.cp(ktm[2 * tsl + 1][64:128, :], ps3b[64:128, 512 + tsl * 128:512 + (tsl + 1) * 128], r=[f"ps3b{tsl}"],
                                     w=[f"ktm{2 * tsl + 1}"], eng="act")
                            a0 = ((c // 2) % 4) * 64
                            ab = 2 if c % 2 == 0 else 7
                            an_ = f"ps{ab}A{(c // 2) % 4}"
                            P.mm(ps[ab][:, a0:a0 + 64], kt_[:, ti * 128:(ti + 1) * 128], qt_[:, t0:t0 + 64],
                                 r=[f"{kn}_{c // 16}", f"{qn}_{c // 16}"], w=[an_])
                            mcol = K_TRE if half == 0 else K_TRO
                            P.tt(scm[c % 4][:], ps[ab][:, a0:a0 + 64], cst[:, mcol:mcol + 64], ALU.mult,
                                 r=[an_, "cst"], w=[f"scm{c % 4}"])
                            vv = v_tm[:, ti, hh * 128:(hh + 1) * 128]
                            db, d0 = 3 + c % 2, ((c // 2) % 2) * 128
                            P.mm(ps[db][:, d0:d0 + 128], ktm[2 * tsl + half][:], vv, r=[f"ktm{2 * tsl + half}", "v_tm"],
                                 w=[f"ps{db}D{(c // 2) % 2}"])

                        def stage_b(c):
                            ti = c // 2
                            t0 = c * 64
                            cb = 5 + (c // 8) % 2
                            j = c % 8
                            vv = v_tm[:, ti, hh * 128:(hh + 1) * 128]
                            P.mm(ps[cb][:, j * 64:(j + 1) * 64], vv, scm[c % 4][:], start=True, stop=(c == 0),
                                 r=["v_tm", f"scm{c % 4}"], w=[f"ps{cb}"])
                            if c > 0:
                                P.mm(ps[cb][:, j * 64:(j + 1) * 64], Sb[2 * p + c % 2][:], qt_[:, t0:t0 + 64], start=False, stop=True,
                                     r=[f"Sb{2 * p + c % 2}", f"{qn}_{c // 16}"], w=[f"ps{cb}"])
                            db, d0 = 3 + c % 2, ((c // 2) % 2) * 128
                            dn = f"ps{db}D{(c // 2) % 2}"
                            ci, ni = 2 * p + c % 2, 2 * p + (c + 1) % 2
                            Scur, Snxt = Sst[ci][:, :], Sst[ni][:, :]
                            if c == 0:
                                P.ts(Snxt, ps[db][:, d0:d0 + 128], chs_[:, 1, c:c + 1], None, ALU.mult,
                                     r=[dn, f"{cn}_{c // 16}"], w=[f"Sst{ni}"])
                            else:
                                P.ts(Snxt, Scur, chs_[:, 0, c:c + 1], None, ALU.mult, r=[f"Sst{ci}", f"{cn}_{c // 16}"], w=[f"Sst{ni}"])
                                P.stt(Snxt, ps[db][:, d0:d0 + 128], chs_[:, 1, c:c + 1], Snxt, ALU.mult, ALU.add,
                                      r=[dn, f"{cn}_{c // 16}", f"Sst{ni}"], w=[f"Sst{ni}"])
                            if c < 31:
                                P.act(Sb[ni][:], Snxt, AF.Copy, r=[f"Sst{ni}", f"{cn}_{(c + 1) // 16}"], w=[f"Sb{ni}"],
                                      scale=chs_[:, 2, c + 1:c + 2])
                            if j == 7:
                                tg = c // 8
                                for o2 in range(2):
                                    P.cp(ost[0][:], ps[cb][:, o2 * 256:(o2 + 1) * 256], r=[f"ps{cb}"], w=["ost0"], eng="act")
                                    P.dma("sp", olocd[h][:, tg * 512 + o2 * 256:tg * 512 + (o2 + 1) * 256], ost[0][:], "ost0",
                                          r=["ost0"], w=[f"olocd{h}"])

                        LA = 2
                        for c in range(LA):
                            stage_a(c)
                        for c in range(32):
                            if c + LA < 32:
                                stage_a(c + LA)
                            stage_b(c)
                            yield
                    return gen()

                def rr(a, b):
                    while a is not None or b is not None:
                        if a is not None:
                            try:
                                next(a)
                            except StopIteration:
                                a = None
                        if b is not None:
                            try:
                                next(b)
                            except StopIteration:
                                b = None

                rr(s1_task(hg * 4, 0), None)
                for hh in range(4):
                    h = hg * 4 + hh
                    rr(s2_task(h, hh), s1_task(h + 1, hh + 1) if hh < 3 else None)
            P.flush()

        CTall = sb("CTall", [128, 2, NT], BF16, mid)
        with ExitStack() as ph:
            wb = [sb(f"wb{i}", [128, 16, 512], BF16, ph) for i in range(2)]
            wdt = sb("wdt", [128, 16, 16], BF16, ph)
            ub = sb("ub", [128, NT + 3], F32, ph)
            xTp = sb("xTp", [128, NT], F32, ph)
            BT = sb("BT", [128, NT], BF16, ph)
            cbT = sb("cbT", [128, NT], F32, ph)
            Btm = sb("Btm", [128, 16, 128], BF16, ph)
            stA = sb("stA", [96, NT], F32, ph)
            stB = sb("stB", [96, NT], F32, ph)
            tmA = sb("tmA", [128, 16, 96], F32, ph)
            aneg = sb("aneg", [48, 2], F32, ph)
            abc = [[sb(f"abc{a}{b}", [128, 512], F32, ph) for b in range(2)] for a in range(2)]
            cht = [[sb(f"cht{a}{b}", [128, 512], BF16, ph) for b in range(2)] for a in range(2)]
            xdA = [sb(f"xdA{i}", [128, 128], BF16, ph) for i in range(4)]
            xdB = [sb(f"xdB{i}", [128, 128], BF16, ph) for i in range(4)]
            xw = [sb(f"xw{i}", [128, 128], BF16, ph) for i in range(4)]
            Dm = [sb(f"Dm{i}", [128, 128], F32, ph) for i in range(2)]
            Mt = [sb(f"Mt{i}", [128, 128], BF16, ph) for i in range(8)]
            prA = [sb(f"prA{i}", [128, 128], BF16, ph) for i in range(4)]
            prB = [sb(f"prB{i}", [128, 128], BF16, ph) for i in range(4)]
            Spp = [sb(f"Spp{i}", [128, 128], F32, ph) for i in range(4)]
            yst = [sb(f"yst{i}", [128, 512], F32, ph) for i in range(2)]
            ebt = sb("ebt", [128, 512], F32, ph)
            dcs = sb("dcs", [128, 2, 16], F32, ph)

            for i in range(4):
                P.memset(xdA[i][:], 0.0, w=[f"xdA{i}"], eng="pool")
                P.memset(xdB[i][:], 0.0, w=[f"xdB{i}"], eng="pool")
            for i in range(4):
                P.memset(prA[i][:], 0.0, w=[f"prA{i}"], eng="pool")
                P.memset(prB[i][:], 0.0, w=[f"prB{i}"], eng="pool")
            P.memset(stA[:], 0.0, w=["stA"])
            P.memset(stB[:], 0.0, w=["stB"])
            P.dma("pool", wdt[:], w_in_v[:, :, 6656:6672], "wdt", w=["wdt"])
            P.act(aneg[32:48, 0:1], pp[32:48, C_ALOG:C_ALOG + 1], AF.Exp, r=["pp"], w=["aneg"])
            P.ts(aneg[32:48, 1:2], aneg[32:48, 0:1], -1.0, None, ALU.mult, r=["aneg"], w=["aneg"])
            for tg in range(4):
                bk = tg % 2
                for kc in range(16):
                    P.mm(ps[bk][0:16, :], wdt[:, kc, :], hT[:, kc, tg * 512:(tg + 1) * 512],
                         start=(kc == 0), stop=(kc == 15), r=["wdt", "hT"], w=[f"ps{bk}"])
                P.act(stA[32:48, tg * 512:(tg + 1) * 512], ps[bk][0:16, :], AF.Exp, r=[f"ps{bk}", "pp"], w=["stA"],
                      bias=pp[32:48, C_DTB:C_DTB + 1])
            P.act(stA[32:48, :], stA[32:48, :], AF.Ln, r=["stA"], w=["stA"], bias=1.0)
            P.cp(stB[64:80, :], stA[32:48, :], r=["stA"], w=["stB"], eng="act")
            P.ts(stB[32:48, :], stA[32:48, :], aneg[32:48, 1:2], None, ALU.mult, r=["stA", "aneg"], w=["stB"])
            P.scan(stB[0:16, :], cst[32:48, K_ONE:K_ONE + 1].broadcast_to([16, NT]), stB[32:48, :], 0.0, ALU.mult, ALU.add,
                   r=["stB", "cst"], w=["stB"])
            g3 = stB[0:16, :].rearrange("p (c j) -> p c j", j=128)
            a3 = stA[0:16, :].rearrange("p (c j) -> p c j", j=128)
            w3 = stA[64:80, :].rearrange("p (c j) -> p c j", j=128)
            P.cp(stA[0:16, 0:128], stB[0:16, 0:128], r=["stB"], w=["stA"], eng="act")
            P.tt(a3[:, 1:16, :], g3[:, 1:16, :], bc_last(g3[:, 0:15, 127], 128), ALU.subtract, r=["stB"], w=["stA"])
            P.tt(w3, bc_last(a3[:, :, 127], 128), a3, ALU.subtract, r=["stA"], w=["stA"])
            P.act(stA[64:80, :], stA[64:80, :], AF.Exp, r=["stA"], w=["stA"])
            P.tt(stA[64:80, :], stA[64:80, :], stB[64:80, :], ALU.mult, r=["stA", "stB"], w=["stA"])
            P.dma("sp", acsd, stA[0:16, :], "acsd", r=["stA"], w=["acsd"])
            P.dma("sp", gsd, stB[0:16, :], "gsd", r=["stB"], w=["gsd"])
            P.dma("sp", gl, stB[0:16, NT - 1:NT], "gl", r=["stB"], w=["gl"])
            for c in range(16):
                bk = 2 + c % 2
                P.tr(ps[bk][:, 0:96], stA[:, c * 128:(c + 1) * 128], cst[0:96, K_ID:K_ID + 96], r=["stA", "cst"], w=[f"ps{bk}"])
                P.cp(tmA[:, c, :], ps[bk][:, 0:96], r=[f"ps{bk}"], w=["tmA"], eng=("act" if c % 2 else "dve"))

            pbank = [0]

            def conv_chunk(s, cols, jc, dest, dname):
                for tg in range(4):
                    bk = pbank[0] % 2
                    pbank[0] += 1
                    for kc in range(16):
                        P.mm(ps[bk][:, :], wb[s][:, kc, cols:cols + 128], hT[:, kc, tg * 512:(tg + 1) * 512],
                             start=(kc == 0), stop=(kc == 15), r=[f"wb{s}", "hT"], w=[f"ps{bk}"])
                    P.cp(ub[:, 3 + tg * 512:3 + (tg + 1) * 512], ps[bk][:, :], r=[f"ps{bk}"], w=["ub"],
                         eng=("act" if tg % 2 else "dve"))
                bk = pbank[0] % 2
                pbank[0] += 1
                for kc in range(16):
                    P.mm(ps[bk][:, 0:4], wb[s][:, kc, cols:cols + 128], hTh[:, kc, :],
                         start=(kc == 0), stop=(kc == 15), r=[f"wb{s}", "hTh"], w=[f"ps{bk}"])
                P.cp(ub[:, 0:3], ps[bk][:, 1:4], r=[f"ps{bk}"], w=["ub"])
                wc = lambda k: pp[:, C_CONVW + k * 12 + jc:C_CONVW + k * 12 + jc + 1]
                P.ts(xTp[:], ub[:, 0:NT], wc(0), pp[:, C_CONVB + jc:C_CONVB + jc + 1], ALU.mult, ALU.add,
                     r=["ub", "pp"], w=["xTp"])
                for k in range(1, 4):
                    P.stt(xTp[:], ub[:, k:k + NT], wc(k), xTp[:], ALU.mult, ALU.add, r=["ub", "pp", "xTp"], w=["xTp"])
                P.act(dest, xTp[:], AF.Silu, r=["xTp"], w=[dname])

            P.dma("pool", wb[1][:], w_in_v[:, :, 6144:6656], "wb1", w=["wb1"])
            for g in range(2):
                P.dma("pool", wb[0][:], w_in_v[:, :, 5120 + g * 512:5120 + (g + 1) * 512], "wb0", w=["wb0"])
                conv_chunk(1, g * 128, 8 + g, BT[:], "BT")
                conv_chunk(1, 256 + g * 128, 10 + g, CTall[:, g, :], "CT")
                for c4 in range(4):
                    bk = 2 + c4 % 2
                    for cc in range(4):
                        c = c4 * 4 + cc
                        P.mm(ps[bk][:, cc * 128:(cc + 1) * 128], BT[:, c * 128:(c + 1) * 128],
                             CTall[:, g, c * 128:(c + 1) * 128], r=["BT", "CT"], w=[f"ps{bk}"], inc=(cc == 3))
                    P.cp(cbT[:, c4 * 512:(c4 + 1) * 512], ps[bk][:, :], r=[f"ps{bk}"], w=["cbT"],
                         eng=("act" if c4 % 2 else "dve"))
                for c8 in range(2):
                    for cc in range(8):
                        c = c8 * 8 + cc
                        P.tr(psb[:, cc * 128:(cc + 1) * 128], BT[:, c * 128:(c + 1) * 128], idb[:], r=["BT", "idb"],
                             w=["ps7b0", "ps7b1"], inc=(cc == 7))
                    P.cp(Btm[:, c8 * 8:(c8 + 1) * 8, :], psb[:, :].rearrange("p (c n) -> p c n", n=128),
                         r=["ps7b0", "ps7b1"], w=["Btm"], eng="act")
                for jp in range(4):
                    j = 4 * g + jp
                    conv_chunk(0, jp * 128, j, xTp[:], "xTp")
                    h0, h1 = 2 * j, 2 * j + 1
                    jpar = j % 2

                    def stage_a(c, j=j, g=g):
                        cs = slice(c * 128, (c + 1) * 128)
                        q4, c4i = c // 4, c % 4
                        sl = q4 % 2
                        if c4i == 0:
                            for hh in range(2):
                                h = 2 * j + hh
                                src = bass.AP(acsd.tensor, h * NT + q4 * 512, [[0, 128], [1, 512]])
                                P.dma("sp", abc[hh][sl][:], src, f"abc{hh}{sl}", r=["acsd"], w=[f"abc{hh}{sl}"])
                                P.act(ebt[:], abc[hh][sl][:], AF.Exp, r=[f"abc{hh}{sl}"], w=["ebt"])
                                P.tt(cht[hh][sl][:], CTall[:, g, q4 * 512:(q4 + 1) * 512], ebt[:], ALU.mult,
                                     r=["CT", "ebt"], w=[f"cht{hh}{sl}"])
                                P.cp(dcs[:, hh, q4 * 4:(q4 + 1) * 4], ebt[:].rearrange("p (c j) -> p c j", j=128)[:, :, 127],
                                     r=["ebt"], w=["dcs"])
                        k2, k4 = c % 2, c % 4
                        bk = 2 + k2
                        P.tr(ps[bk][:, 0:128], xTp[:, cs], ident, r=["xTp", "cst"], w=[f"ps{bk}"])
                        P.ts(xdA[k4][:, 0:64], ps[bk][:, 0:64], tmA[:, c, 32 + h0:33 + h0], None, ALU.mult,
                             r=[f"ps{bk}", "tmA"], w=[f"xdA{k4}"])
                        P.act(xdB[k4][:, 64:128], ps[bk][:, 64:128], AF.Copy, r=[f"ps{bk}", "tmA"], w=[f"xdB{k4}"],
                              scale=tmA[:, c, 32 + h1:33 + h1])
                        P.act(xw[k4][:, 0:64], ps[bk][:, 0:64], AF.Copy, r=[f"ps{bk}", "tmA"], w=[f"xw{k4}"],
                              scale=tmA[:, c, 64 + h0:65 + h0])
                        P.ts(xw[k4][:, 64:128], ps[bk][:, 64:128], tmA[:, c, 64 + h1:65 + h1], None, ALU.mult,
                             r=[f"ps{bk}", "tmA"], w=[f"xw{k4}"])
                        for hh in range(2):
                            h = 2 * j + hh
                            asl = abc[hh][sl][:, c4i * 128:(c4i + 1) * 128]
                            P.stt(Dm[hh][:], asl, tmA[:, c, h:h + 1], cst[:, K_NEG:K_NEG + 128], ALU.subtract, ALU.add,
                                  r=[f"abc{hh}{sl}", "tmA", "cst"], w=[f"Dm{hh}"])
                            P.act(Dm[hh][:], Dm[hh][:], AF.Exp, r=[f"Dm{hh}"], w=[f"Dm{hh}"])
                            P.tt(Mt[2 * k4 + hh][:], cbT[:, cs], Dm[hh][:], ALU.mult, r=["cbT", f"Dm{hh}"],
                                 w=[f"Mt{2 * k4 + hh}"])
                        P.mm(ps[4][:, k4 * 128:(k4 + 1) * 128], Btm[:, c, :], xw[k4][:], r=["Btm", f"xw{k4}"], w=[f"ps4D{k4}"])

                    def stage_b(c, j=j, g=g):
                        q4, c4i = c // 4, c % 4
                        sl = q4 % 2
                        k2, k4 = c % 2, c % 4
                        yb = 5 + q4 % 2
                        yo = ps[yb][:, c4i * 128:(c4i + 1) * 128]
                        P.mm(yo, xdA[k4][:], Mt[2 * k4][:], start=True, stop=False, r=[f"xdA{k4}", f"Mt{2 * k4}"], w=[f"ps{yb}"])
                        P.mm(yo, xdB[k4][:], Mt[2 * k4 + 1][:], start=False, stop=(c == 0), r=[f"xdB{k4}", f"Mt{2 * k4 + 1}"],
                             w=[f"ps{yb}"])
                        jpar = j % 2
                        ci, ni = 2 * jpar + c % 2, 2 * jpar + (c + 1) % 2
                        Scur, Snxt = Spp[ci][:, :], Spp[ni][:, :]
                        if c > 0:
                            P.mm(yo, prA[ci][:], cht[0][sl][:, c4i * 128:(c4i + 1) * 128], start=False, stop=False,
                                 r=[f"prA{ci}", f"cht0{sl}"], w=[f"ps{yb}"])
                            P.mm(yo, prB[ci][:], cht[1][sl][:, c4i * 128:(c4i + 1) * 128], start=False, stop=True,
                                 r=[f"prB{ci}", f"cht1{sl}"], w=[f"ps{yb}"])
                        if c == 0:
                            P.cp(Snxt, ps[4][:, k4 * 128:(k4 + 1) * 128], r=[f"ps4D{k4}"], w=[f"Spp{ni}"])
                        else:
                            for hh in range(2):
                                P.ts(Snxt[:, hh * 64:(hh + 1) * 64], Scur[:, hh * 64:(hh + 1) * 64], dcs[:, hh, c:c + 1], None,
                                     ALU.mult, r=[f"Spp{ci}", "dcs"], w=[f"Spp{ni}"])
                            P.tt(Snxt, Snxt, ps[4][:, k4 * 128:(k4 + 1) * 128], ALU.add, r=[f"Spp{ni}", f"ps4D{k4}"], w=[f"Spp{ni}"])
                        if c < 15:
                            P.cp(prA[ni][:, 0:64], Snxt[:, 0:64], r=[f"Spp{ni}"], w=[f"prA{ni}"], eng="act")
                            P.cp(prB[ni][:, 64:128], Snxt[:, 64:128], r=[f"Spp{ni}"], w=[f"prB{ni}"], eng="act")
                        if c4i == 3:
                            tsl = slice(q4 * 512, (q4 + 1) * 512)
                            P.stt(yst[q4 % 2][:], xTp[:, tsl], pp[:, C_DSK + j:C_DSK + j + 1], ps[yb][:, :], ALU.mult, ALU.add,
                                  r=["xTp", "pp", f"ps{yb}"], w=[f"yst{q4 % 2}"])
                            P.dma("sp", olocd[8 + j][:, tsl], yst[q4 % 2][:], f"yst{q4 % 2}", r=[f"yst{q4 % 2}"],
                                  w=[f"olocd{8 + j}"])

                    LA = 2
                    for c in range(LA):
                        stage_a(c)
                    for c in range(16):
                        if c + LA < 16:
                            stage_a(c + LA)
                        stage_b(c)
            P.flush()

        mixT = sb("mixT", [128, 16, NT], BF16, mid)
        with ExitStack() as ph:
            wb1_ = sb("wb0", [128, 16, 512], BF16, ph)
            wb = [wb1_, wb1_]
            gs = [sb(f"gs{i}", [128, 512], F32, ph) for i in range(2)]
            ol = [sb(f"ol{i}", [128, 512], F32, ph) for i in range(2)]
            ql = [sb(f"ql{i}", [128, 512], BF16, ph) for i in range(2)]
            sq = [sb(f"sq{i}", [128, 512], F32, ph) for i in range(2)]
            rs = [sb(f"rs{i}", [128, 512], F32, ph) for i in range(2)]
            def run_tasks3(tasks, width=2):
                pending = list(tasks)
                active = [None] * width
                while pending or any(a is not None for a in active):
                    for s_ in range(width):
                        if active[s_] is None and pending:
                            active[s_] = pending.pop(0)(s_)
                        if active[s_] is not None:
                            try:
                                next(active[s_])
                            except StopIteration:
                                active[s_] = None

            def hg_item(h, hh, tg):
                def gen(k):
                    tsl = slice(tg * 512, (tg + 1) * 512)
                    bk = k
                    P.dma("sp", ol[k][:], olocd[h][:, tsl], f"ol{k}", r=[f"olocd{h}"], w=[f"ol{k}"])
                    P.dma("sp", ql[k][:], qhd[h][:, tsl], f"ql{k}", r=[f"qhd{h}"], w=[f"ql{k}"])
                    for kc in range(16):
                        P.mm(ps[bk][:, :], wb[0][:, kc, hh * 128:(hh + 1) * 128], hT[:, kc, tsl],
                             start=(kc == 0), stop=(kc == 15), r=["wb0", "hT"], w=[f"ps{bk}"])
                    yield
                    P.act(gs[k][:], ps[bk][:, :], AF.Silu, r=[f"ps{bk}"], w=[f"gs{k}"])
                    P.mm(ps[2 + k][:, :], Sinb[:, h, :], ql[k][:], r=["Sinb", f"ql{k}"], w=[f"ps{2 + k}"])
                    yield
                    P.tt(ol[k][:], ps[2 + k][:, :], ol[k][:], ALU.add, r=[f"ps{2 + k}", f"ol{k}"], w=[f"ol{k}"])
                    yield
                    P.act(sq[k][:], ol[k][:], AF.Square, r=[f"ol{k}"], w=[f"sq{k}"])
                    yield
                    P.mm(ps[4 + k][:, :], ones, sq[k][:], r=["cst", f"sq{k}"], w=[f"ps{4 + k}"])
                    yield
                    P.ts(rs[k][:], ps[4 + k][:, :], 1.0 / 128, EPS, ALU.mult, ALU.add, r=[f"ps{4 + k}"], w=[f"rs{k}"])
                    yield
                    P.act(rs[k][:], rs[k][:], AF.Sqrt, r=[f"rs{k}"], w=[f"rs{k}"])
                    yield
                    P.recip(rs[k][:], rs[k][:], r=[f"rs{k}"], w=[f"rs{k}"])
                    yield
                    P.tt(ol[k][:], ol[k][:], rs[k][:], ALU.mult, r=[f"ol{k}", f"rs{k}"], w=[f"ol{k}"])
                    yield
                    P.stt(mixT[:, h, tsl], ol[k][:], pp[:, C_HNW + h:C_HNW + h + 1], gs[k][:], ALU.mult, ALU.mult,
                          r=[f"ol{k}", "pp", f"gs{k}"], w=["mixT"])
                    yield
                return gen

            for hg in range(2):
                P.dma("pool", wb[0][:], w_in_v[:, :, 3072 + hg * 512:3072 + (hg + 1) * 512], "wb0", w=["wb0"])
                run_tasks3([hg_item(hg * 4 + hh, hh, tg) for hh in range(4) for tg in range(4)])

            zs = gs
            gb = [[sb(f"gb{a}{i}", [128, 512], F32, ph) for i in range(2)] for a in range(2)]
            ch3 = [sb(f"ch3{i}", [128, 512], BF16, ph) for i in range(4)]
            yz1_ = sb("yz", [128, 4, 512], F32, ph)

            def ssd_item(g, tg, jp, yk):
                j = 4 * g + jp

                def gen(k):
                    tsl = slice(tg * 512, (tg + 1) * 512)
                    bk = k
                    P.dma("sp", ol[k][:], olocd[8 + j][:, tsl], f"ol{k}", r=[f"olocd{8 + j}"], w=[f"ol{k}"])
                    for hh in range(2):
                        h = 2 * j + hh
                        src = bass.AP(gsd.tensor, h * NT + tg * 512, [[0, 128], [1, 512]])
                        P.dma("sp", gb[k][hh][:], src, f"gb{k}{hh}", r=["gsd"], w=[f"gb{k}{hh}"])
                    for kc in range(16):
                        P.mm(ps[bk][:, :], wb[0][:, kc, jp * 128:(jp + 1) * 128], hT[:, kc, tsl],
                             start=(kc == 0), stop=(kc == 15), r=["wb0", "hT"], w=[f"ps{bk}"])
                    yield
                    P.act(zs[k][:], ps[bk][:, :], AF.Silu, r=[f"ps{bk}"], w=[f"gs{k}"])
                    yield
                    for hh in range(2):
                        P.act(gb[k][hh][:], gb[k][hh][:], AF.Exp, r=[f"gb{k}{hh}"], w=[f"gb{k}{hh}"])
                        P.tt(ch3[2 * k + hh][:], CTall[:, g, tsl], gb[k][hh][:], ALU.mult, r=["CT", f"gb{k}{hh}"],
                             w=[f"ch3{2 * k + hh}"])
                        yield
                    P.mm(ps[2 + k][:, :], SinA[:, j, :], ch3[2 * k][:], start=True, stop=False,
                         r=["SinA", f"ch3{2 * k}"], w=[f"ps{2 + k}"])
                    P.mm(ps[2 + k][:, :], SinB[:, j, :], ch3[2 * k + 1][:], start=False, stop=True,
                         r=["SinB", f"ch3{2 * k + 1}"], w=[f"ps{2 + k}"])
                    yield
                    P.tt(ol[k][:], ps[2 + k][:, :], ol[k][:], ALU.add, r=[f"ps{2 + k}", f"ol{k}"], w=[f"ol{k}"])
                    yield
                    P.tt(yz1_[:, jp, :], ol[k][:], zs[k][:], ALU.mult, r=[f"ol{k}", f"gs{k}"], w=[f"yz{jp}"])
                    yield
                    P.act(sq[k][:], yz1_[:, jp, :], AF.Square, r=[f"yz{jp}"], w=[f"sq{k}"])
                    yield
                    P.mm(ps[4 + yk][:, :], ones, sq[k][:], start=(jp == 0), stop=(jp == 3), r=["cst", f"sq{k}"],
                         w=[f"ps{4 + yk}"])
                    yield
                return gen

            for g in range(2):
                P.dma("pool", wb[0][:], w_in_v[:, :, 4096 + g * 512:4096 + (g + 1) * 512], "wb0", w=["wb0"])
                for tg in range(4):
                    tsl = slice(tg * 512, (tg + 1) * 512)
                    yk = tg % 2
                    run_tasks3([ssd_item(g, tg, jp, yk) for jp in range(4)])
                    P.ts(rs[yk][:], ps[4 + yk][:, :], 1.0 / 512, EPS, ALU.mult, ALU.add, r=[f"ps{4 + yk}"], w=[f"rs{yk}"])
                    P.act(rs[yk][:], rs[yk][:], AF.Sqrt, r=[f"rs{yk}"], w=[f"rs{yk}"])
                    P.recip(rs[yk][:], rs[yk][:], r=[f"rs{yk}"], w=[f"rs{yk}"])
                    for jp in range(4):
                        j = 4 * g + jp
                        P.stt(mixT[:, 8 + j, tsl], yz1_[:, jp, :], pp[:, C_SNW + j:C_SNW + j + 1], rs[yk][:], ALU.mult, ALU.mult,
                              r=[f"yz{jp}", "pp", f"rs{yk}"], w=["mixT"])
            P.flush()

        if dbg == "mixT":
            with ExitStack() as ph:
                stg = [sb(f"stg{i}", [128, NT], F32, ph) for i in range(2)]
                for j in range(16):
                    P.cp(stg[j % 2][:], mixT[:, j, :], r=["mixT"], w=[f"stg{j % 2}"])
                    P.dma("sp", dbg_d[:, j, :], stg[j % 2][:], f"stg{j % 2}", r=[f"stg{j % 2}"], w=["dbgd"])
                P.flush()

        with ExitStack() as ph:
            pw0 = sb("pw0", [128, D], F32, ph)
            xt = [sb(f"xr{j}", [128, D], F32, ph) for j in range(2)]
            x1t = [sb(f"x1t{j}", [128, D], F32, ph) for j in range(2)]
            junk = sb("junk", [128, 512], BF16, ph)
            st = sb("st4", [128, 16], F32, ph)
            wo = hT
            w_out_v = w_out.rearrange("(k p) n -> p k n", p=128)
            P.dma("sp", pw0[:], pw_d[:, 0, :], "pw0", w=["pw0"])
            for b4 in range(4):
                P.dma("pool", wo[:, :, b4 * 512:(b4 + 1) * 512], w_out_v[:, :, b4 * 512:(b4 + 1) * 512], f"wo{b4}",
                      r=[], w=["hT"])
            for ti in range(16):
                k2 = ti % 2
                xb = f"xr{k2}"
                P.dma("sp", xt[k2][:], xin[128 + ti * 128:128 + (ti + 1) * 128, :], xb, w=[xb])
                for b4 in range(4):
                    bk = 4 * k2 + b4
                    bname = f"ps{bk}q"
                    bank = ps[bk]
                    for kc in range(16):
                        P.mm(bank[:, :], mixT[:, kc, ti * 128:(ti + 1) * 128], wo[:, kc, b4 * 512:(b4 + 1) * 512],
                             start=(kc == 0), stop=(kc == 15), r=["mixT", "hT"], w=[bname])
                    P.act(junk[:], bank[:, :], AF.Square, r=[bname], w=["junk4", f"st4{k2}"],
                          accum_out=st[:, 8 * k2 + b4:8 * k2 + b4 + 1])
                sv = st[:, 8 * k2:8 * k2 + 8]
                P.tt(sv[:, 4:5], sv[:, 0:1], sv[:, 1:2], ALU.add, r=[f"st4{k2}"], w=[f"st4{k2}"])
                P.tt(sv[:, 5:6], sv[:, 2:3], sv[:, 3:4], ALU.add, r=[f"st4{k2}"], w=[f"st4{k2}"])
                P.tt(sv[:, 4:5], sv[:, 4:5], sv[:, 5:6], ALU.add, r=[f"st4{k2}"], w=[f"st4{k2}"])
                P.ts(sv[:, 5:6], sv[:, 4:5], 1.0 / D, EPS, ALU.mult, ALU.add, r=[f"st4{k2}"], w=[f"st4{k2}"])
                P.act(sv[:, 6:7], sv[:, 5:6], AF.Sqrt, r=[f"st4{k2}"], w=[f"st4{k2}"])
                P.recip(sv[:, 7:8], sv[:, 6:7], r=[f"st4{k2}"], w=[f"st4{k2}"])
                for b4 in range(4):
                    bk = 4 * k2 + b4
                    bname = f"ps{bk}q"
                    bank = ps[bk]
                    csl = slice(b4 * 512, (b4 + 1) * 512)
                    P.stt(x1t[k2][:, csl], bank[:, :], sv[:, 7:8], pw0[:, csl], ALU.mult, ALU.mult,
                          r=[bname, f"st4{k2}", "pw0"], w=[f"x1t{k2}"])
                P.tt(x1t[k2][:], x1t[k2][:], xt[k2][:], ALU.add, r=[f"x1t{k2}", xb], w=[f"x1t{k2}"], eng="pool")
                P.dma("sp", x1d[ti * 128:(ti + 1) * 128, :], x1t[k2][:], f"x1t{k2}", r=[f"x1t{k2}"], w=["x1d"])
            P.flush()
        mid.close()
        sinst.close()

        with ExitStack() as ph:
            TG = 512
            pw1 = sb("pw1", [128, D], F32, ph)
            h2s = [sb(f"h2{i}", [128, 16, TG], BF16, ph) for i in range(2)]
            hid = sb("hid", [128, 44, TG], BF16, ph)
            wr = sb("wr", [128, 4, 16 * 512], BF16, ph)
            ff = [sb(f"ff{i}", [128, D], F32, ph) for i in range(4)]
            xt = [sb(f"xq{j}", [128, D], F32, ph) for j in range(2)]
            xa = xt
            sg = [sb(f"sg{j}", [128, TG], F32, ph) for j in range(2)]
            junk = sb("junk5", [128, D], BF16, ph)
            pjunk = [junk[:, 0:1024], junk[:, 1024:2048]]
            st = sb("st5", [128, 16], F32, ph)
            w_gate_v = w_gate.rearrange("(k p) n -> p k n", p=128)
            w_up_v = w_up.rearrange("(k p) n -> p k n", p=128)
            w_down_v = w_down.rearrange("(f p) n -> p f n", p=128)
            P.dma("sp", pw1[:], pw_d[:, 1, :], "pw1", w=["pw1"])

            def wslot(i):
                return wr[:, i, :].rearrange("p (k n) -> p k n", n=512)

            def dslot(i):
                return wr[:, 2 * i:2 * i + 2, :].rearrange("p a b -> p (a b)")[:, 0:44 * 256].rearrange("p (f n) -> p f n", n=256)

            for tgi in range(NT // TG):
                h2 = h2s[tgi % 2]
                h2n = f"h2{tgi % 2}"
                for rnd in range(2):
                    for jj in range(2):
                        j4 = rnd * 2 + jj
                        ti = tgi * 4 + j4
                        xs = xa[jj]
                        xb = f"xq{jj}"
                        P.dma("sp", xs[:], x1d[ti * 128:(ti + 1) * 128, :], xb, r=["x1d"], w=[xb])
                        ss = st[:, 4 * j4:4 * j4 + 1]
                        for hf in range(2):
                            P.act(pjunk[hf], xs[:, hf * 1024:(hf + 1) * 1024], AF.Square, r=[xb], w=["junk5", f"st5{j4}"],
                                  accum_out=st[:, 4 * j4 + 1 + hf:4 * j4 + 2 + hf])
                        P.tt(ss, st[:, 4 * j4 + 1:4 * j4 + 2], st[:, 4 * j4 + 2:4 * j4 + 3], ALU.add, r=[f"st5{j4}"], w=[f"st5{j4}"])
                        P.ts(st[:, 4 * j4 + 1:4 * j4 + 2], ss, 1.0 / D, EPS, ALU.mult, ALU.add, r=[f"st5{j4}"], w=[f"st5{j4}"])
                        P.act(st[:, 4 * j4 + 2:4 * j4 + 3], st[:, 4 * j4 + 1:4 * j4 + 2], AF.Sqrt, r=[f"st5{j4}"], w=[f"st5{j4}"])
                        P.recip(st[:, 4 * j4 + 3:4 * j4 + 4], st[:, 4 * j4 + 2:4 * j4 + 3], r=[f"st5{j4}"], w=[f"st5{j4}"])
                        P.act(xs[:], xs[:], AF.Copy, r=[xb, f"st5{j4}"], w=[xb], scale=st[:, 4 * j4 + 3:4 * j4 + 4])
                    for k2_ in range(8):
                        bk = k2_ % 4
                        for kk2 in range(2):
                            kc = 2 * k2_ + kk2
                            for jj in range(2):
                                P.tr(ps[bk][:, (kk2 * 2 + jj) * 128:(kk2 * 2 + jj + 1) * 128], xa[jj][:, kc * 128:(kc + 1) * 128], ident,
                                     r=[f"xq{jj}", "cst"], w=[f"ps{bk}"], inc=(kk2 == 1 and jj == 1))
                        for kk2 in range(2):
                            kc = 2 * k2_ + kk2
                            wcol = pp[:, C_PREFFN + kc:C_PREFFN + kc + 1]
                            o_ap = h2[:, kc, rnd * 256:(rnd + 1) * 256]
                            i_ap = ps[bk][:, kk2 * 256:(kk2 + 1) * 256]
                            if kk2 == 0:
                                P.ts(o_ap, i_ap, wcol, None, ALU.mult, r=[f"ps{bk}", "pp"], w=[h2n])
                            else:
                                P.act(o_ap, i_ap, AF.Copy, r=[f"ps{bk}", "pp"], w=[h2n], scale=wcol)
                for blk in range(11):
                    gsl, usl = (blk % 2) * 2, (blk % 2) * 2 + 1
                    P.dma("pool", wslot(gsl), w_gate_v[:, :, blk * 512:(blk + 1) * 512], f"wr{gsl}", w=[f"wr{gsl}"])
                    P.dma("pool", wslot(usl), w_up_v[:, :, blk * 512:(blk + 1) * 512], f"wr{usl}", w=[f"wr{usl}"])
                    for f4 in range(4):
                        fc = blk * 4 + f4
                        k2 = fc % 2
                        ga, ua = ps[k2], ps[2 + k2]
                        for kc in range(16):
                            P.mm(ga[:, :], wslot(gsl)[:, kc, f4 * 128:(f4 + 1) * 128], h2[:, kc, :],
                                 start=(kc == 0), stop=(kc == 15), r=[f"wr{gsl}", h2n], w=[f"ps{k2}"])
                        for kc in range(16):
                            P.mm(ua[:, :], wslot(usl)[:, kc, f4 * 128:(f4 + 1) * 128], h2[:, kc, :],
                                 start=(kc == 0), stop=(kc == 15), r=[f"wr{usl}", h2n], w=[f"ps{2 + k2}"])
                        P.act(sg[k2][:], ga[:, :], AF.Silu, r=[f"ps{k2}"], w=[f"sg{k2}"])
                        P.tt(hid[:, fc, :], sg[k2][:], ua[:, :], ALU.mult, r=[f"sg{k2}", f"ps{2 + k2}"], w=["hid"])
                for db in range(8):
                    ds_ = db % 2
                    P.dma("pool", dslot(ds_), w_down_v[:, :, db * 256:(db + 1) * 256], f"wd{ds_}",
                          w=[f"wr{2 * ds_}", f"wr{2 * ds_ + 1}"])
                    for j4 in range(4):
                        bk = 4 + (db * 4 + j4) % 2
                        for fc in range(44):
                            P.mm(ps[bk][:, 0:256], hid[:, fc, j4 * 128:(j4 + 1) * 128], dslot(ds_)[:, fc, :],
                                 start=(fc == 0), stop=(fc == 43), r=["hid", f"wr{2 * ds_}", f"wr{2 * ds_ + 1}"], w=[f"ps{bk}"])
                        P.cp(ff[j4][:, db * 256:(db + 1) * 256], ps[bk][:, 0:256], r=[f"ps{bk}"], w=[f"ff{j4}"],
                             eng=("act" if j4 % 2 else "dve"))
                for j4 in range(4):
                    ti = tgi * 4 + j4
                    k2 = j4 % 2
                    xb = f"xq{k2}"
                    P.dma("sp", xt[k2][:], x1d[ti * 128:(ti + 1) * 128, :], xb, r=["x1d"], w=[xb])
                    ss = st[:, 4 * j4:4 * j4 + 1]
                    for hf in range(2):
                        P.act(pjunk[hf], ff[j4][:, hf * 1024:(hf + 1) * 1024], AF.Square, r=[f"ff{j4}"], w=["junk5", f"st5{j4}"],
                              accum_out=st[:, 4 * j4 + 1 + hf:4 * j4 + 2 + hf])
                    P.tt(ss, st[:, 4 * j4 + 1:4 * j4 + 2], st[:, 4 * j4 + 2:4 * j4 + 3], ALU.add, r=[f"st5{j4}"], w=[f"st5{j4}"])
                    P.ts(st[:, 4 * j4 + 1:4 * j4 + 2], ss, 1.0 / D, EPS, ALU.mult, ALU.add, r=[f"st5{j4}"], w=[f"st5{j4}"])
                    P.act(st[:, 4 * j4 + 2:4 * j4 + 3], st[:, 4 * j4 + 1:4 * j4 + 2], AF.Sqrt, r=[f"st5{j4}"], w=[f"st5{j4}"])
                    P.recip(st[:, 4 * j4 + 3:4 * j4 + 4], st[:, 4 * j4 + 2:4 * j4 + 3], r=[f"st5{j4}"], w=[f"st5{j4}"])
                    P.stt(ff[j4][:], ff[j4][:], st[:, 4 * j4 + 3:4 * j4 + 4], pw1[:], ALU.mult, ALU.mult,
                          r=[f"ff{j4}", f"st5{j4}", "pw1"], w=[f"ff{j4}"])
                    P.tt(xt[k2][:], xt[k2][:], ff[j4][:], ALU.add, r=[xb, f"ff{j4}"], w=[xb], eng="pool")
                    P.dma("sp", y[ti * 128:(ti + 1) * 128, :], xt[k2][:], xb, r=[xb], w=["y"])
            P.flush()
    return nc


def make_inputs(inputs):
    x = np.asarray(inputs["x"], dtype=np.float32)
    g = lambda k: np.asarray(inputs[k], dtype=np.float32)
    pp = np.zeros((128, PPW), np.float32)
    pm = lambda v, n: np.ascontiguousarray(v.reshape(n, 128).T)
    pp[:, C_PREMIX:C_PREMIX + 16] = pm(g("pre_mix_norm_w")[0], 16)
    pp[:, C_PREFFN:C_PREFFN + 16] = pm(g("pre_ffn_norm_w")[0], 16)
    pp[:, C_L0:C_L0 + 8] = pm(g("lb_logits")[0], 8)
    pp[:, C_L1:C_L1 + 8] = pm(g("lb_logits")[1], 8)
    pp[:, C_HNW:C_HNW + 8] = pm(g("hgrn_norm_w")[0], 8)
    cw = g("conv_w")[0]
    for k in range(4):
        pp[:, C_CONVW + k * 12:C_CONVW + (k + 1) * 12] = pm(cw[k], 12)
    pp[:, C_CONVB:C_CONVB + 12] = pm(g("conv_b")[0], 12)
    pp[:, C_SNW:C_SNW + 8] = pm(g("ssd_norm_w")[0], 8)
    pp[:, C_DSK:C_DSK + 8] = pm(np.repeat(g("d_skip")[0], 64), 8)
    pp[32:48, C_DTB] = g("dt_bias")[0]
    pp[32:48, C_ALOG] = g("a_log")[0]
    pp[0:16, C_DTB] = g("dt_bias")[0]
    pp[0:16, C_ALOG] = g("a_log")[0]
    cst = np.zeros((128, CW), np.float32)
    cst[:, K_ID:K_ID + 128] = np.eye(128, dtype=np.float32)
    s = np.arange(128)[:, None]
    t = np.arange(128)[None, :]
    cst[:, K_NEG:K_NEG + 128] = np.where(s <= t, 0.0, -30000.0)
    tri = (np.arange(64)[:, None] <= np.arange(64)[None, :]).astype(np.float32)
    cst[0:64, K_TRE:K_TRE + 64] = tri
    cst[64:128, K_TRO:K_TRO + 64] = tri
    cst[:, K_ONE:K_ONE + 128] = 1.0
    pw = np.zeros((128, 2, D), np.float32)
    pw[:, 0, :] = g("post_mix_norm_w")[0][None, :]
    pw[:, 1, :] = g("post_ffn_norm_w")[0][None, :]
    shared = {"w_in": g("w_in")[0], "w_out": g("w_out")[0], "w_gate": g("w_gate")[0], "w_up": g("w_up")[0],
              "w_down": g("w_down")[0], "cst": cst, "pw": pw}
    maps = []
    for c in range(NCORES):
        b, q = c // 4, c % 4
        xi = np.zeros((NT + 128, D), np.float32)
        xi[128:] = x[b, q * NT:(q + 1) * NT]
        if q > 0:
            xi[:128] = x[b, q * NT - 128:q * NT]
        ppc = pp.copy()
        for j in range(NCORES):
            m = 1.0 if (j // 4 == b and j % 4 < q) else 0.0
            ppc[:, C_M + j] = m
            ppc[:, C_OM + j] = 1.0 - m
        xp = np.zeros((NW * NT, D), np.float32)
        for w in range(NW):
            qq = q - NW + w
            if qq >= 0:
                xp[w * NT:(w + 1) * NT] = x[b, qq * NT:(qq + 1) * NT]
                ppc[:, C_V + w] = 1.0
        d = dict(shared)
        d["xpre"] = xp
        d["xin"] = xi
        d["pp"] = ppc
        maps.append(d)
    return maps


def kernel(**inputs):
    maps = make_inputs(inputs)
    nc = build()
    res = run_bass_kernel_spmd(nc, maps, core_ids=list(range(NCORES)))
    out = np.zeros((2, 4 * NT, D), np.float32)
    for c in range(NCORES):
        out[c // 4, (c % 4) * NT:(c % 4 + 1) * NT] = res.results[c]["y"]
    return out
```
